# Optimizing a Trainium2 kernel written in Bass

```python
import jax, jax.numpy as jnp
from jax import lax
import numpy as np


D_MODEL = 1024
BATCH = 8
SEQ = 4096
DEPTH = 4

D_PLE = 256
HEAD_DIM = 64
CONV_CH = 256
CONV_K = 31
NSA_HEADS = 8
NSA_GROUPS = 2
NSA_HPG = NSA_HEADS // NSA_GROUPS
NSA_CMP_STRIDE = 16
NSA_CMP_LEN = 2 * NSA_CMP_STRIDE
NSA_CMP_HIDDEN = 128
NSA_SEL_LEN = 64
NSA_TOP = 16
NSA_WINDOW = 512
NSA_Q_BLOCK = 64
HGRN_HEADS = 4
HGRN_W = HGRN_HEADS * HEAD_DIM
HGRN_CHUNK = 64
D_FF = 2816
FFN_CONV_K = 3
MIX_W = CONV_CH + NSA_HEADS * HEAD_DIM + HGRN_W
CONV_COLS = 2 * CONV_CH
NSA_Q_COLS = NSA_HEADS * HEAD_DIM
NSA_KV_COLS = 6 * NSA_GROUPS * HEAD_DIM
NSA_GATE_COLS = 3 * NSA_HEADS
HGRN_COLS = 4 * HGRN_W
IN_COLS = CONV_COLS + NSA_Q_COLS + NSA_KV_COLS + NSA_GATE_COLS + HGRN_COLS
IN_SPLITS = (CONV_COLS, CONV_COLS + NSA_Q_COLS, CONV_COLS + NSA_Q_COLS + NSA_KV_COLS, CONV_COLS + NSA_Q_COLS + NSA_KV_COLS + NSA_GATE_COLS)
DN_ALPHA = (2 * DEPTH) ** 0.25
DN_BETA = (8 * DEPTH) ** -0.25
ATTN_SCALE = HEAD_DIM ** -0.5
LN_EPS = 1e-5
MASK_VALUE = -1e30
FORCE_SCORE = 1e9

kernel_name = 'hybrid_conv_nsa_hgrn2_deepnorm'


def layer_norm(x, g, b):
    xf = x.astype(jnp.float32)
    mu = jnp.mean(xf, axis=-1, keepdims=True)
    var = jnp.mean(jnp.square(xf - mu), axis=-1, keepdims=True)
    return ((xf - mu) * lax.rsqrt(var + LN_EPS) * g + b).astype(x.dtype)


def causal_dwconv(x, w, b):
    k = w.shape[0]
    y = lax.conv_general_dilated(x, w[:, None, :], window_strides=(1,), padding=[(k - 1, 0)],
                                 dimension_numbers=('NWC', 'WIO', 'NWC'), feature_group_count=x.shape[-1])
    return y + b


def masked_softmax(s, mask):
    p = jax.nn.softmax(jnp.where(mask, s, MASK_VALUE), axis=-1)
    return jnp.where(mask, p, 0.0)


def conv_module(u, conv_w, conv_b, ln_g, ln_b):
    a, gate = jnp.split(u, 2, axis=-1)
    glu = a * jax.nn.sigmoid(gate)
    c = causal_dwconv(glu, conv_w, conv_b)
    return jax.nn.silu(layer_norm(c, ln_g, ln_b))


def nsa_compress(kv, pe, w1, w2):
    b, s, g, dh = kv.shape
    ch = kv.reshape(b, s // NSA_CMP_STRIDE, NSA_CMP_STRIDE, g, dh)
    blocks = jnp.concatenate([ch[:, :-1], ch[:, 1:]], axis=2) + pe[None, None, :, None, :]
    n_cmp = blocks.shape[1]
    flat = blocks.transpose(0, 1, 3, 2, 4).reshape(b, n_cmp, g, NSA_CMP_LEN * dh)
    return jax.nn.gelu(flat @ w1) @ w2


def nsa_mixer(q, k_cmp, v_cmp, k_slc, v_slc, k_win, v_win, gate_logits, pe_k, pe_v, w1_k, w2_k, w1_v, w2_v):
    f32 = jnp.float32
    b, s = q.shape[0], q.shape[1]
    n_cmp = s // NSA_CMP_STRIDE - 1
    n_sel = s // NSA_SEL_LEN
    n_top = min(NSA_TOP, n_sel)
    kc = nsa_compress(k_cmp, pe_k, w1_k, w2_k)
    vc = nsa_compress(v_cmp, pe_v, w1_v, w2_v).astype(f32)
    cmp_start = jnp.arange(n_cmp) * NSA_CMP_STRIDE
    cmp_end = cmp_start + NSA_CMP_LEN - 1
    sel_start = jnp.arange(n_sel) * NSA_SEL_LEN
    cmp_to_sel = ((cmp_start[:, None] <= sel_start[None, :] + NSA_SEL_LEN - 1)
                  & (cmp_end[:, None] >= sel_start[None, :])).astype(f32)
    k_blocks = k_slc.reshape(b, n_sel, NSA_SEL_LEN, NSA_GROUPS, HEAD_DIM).transpose(0, 3, 1, 2, 4)
    v_blocks = v_slc.reshape(b, n_sel, NSA_SEL_LEN, NSA_GROUPS, HEAD_DIM).transpose(0, 3, 1, 2, 4)
    pad = ((0, 0), (NSA_WINDOW, 0), (0, 0), (0, 0))
    k_win_p = jnp.pad(k_win, pad)
    v_win_p = jnp.pad(v_win, pad)
    qg = q.reshape(b, s, NSA_GROUPS, NSA_HPG, HEAD_DIM)
    gates = jax.nn.sigmoid(gate_logits.astype(f32)).reshape(b, s, NSA_GROUPS, NSA_HPG, 3)
    gather = jax.vmap(jax.vmap(lambda blocks, idx: blocks[idx]))
    sel_off = jnp.arange(NSA_SEL_LEN)
    win_off = jnp.arange(NSA_WINDOW + NSA_Q_BLOCK)
    j_sel = jnp.arange(n_sel)

    def block_fn(c):
        t0 = c * NSA_Q_BLOCK
        qb = lax.dynamic_slice_in_dim(qg, t0, NSA_Q_BLOCK, axis=1)
        pos = t0 + jnp.arange(NSA_Q_BLOCK)
        sc = jnp.einsum('bqghd,bngd->bghqn', qb, kc).astype(f32) * ATTN_SCALE
        p_cmp = masked_softmax(sc, cmp_end[None, :] <= pos[:, None])
        o_cmp = jnp.einsum('bghqn,bngd->bqghd', p_cmp, vc)
        imp = jnp.einsum('bghqn,nj->bgqj', p_cmp, cmp_to_sel)
        blk = pos // NSA_SEL_LEN
        forced = (j_sel[None, :] == 0) | (j_sel[None, :] == blk[:, None]) | (j_sel[None, :] == blk[:, None] - 1)
        causal = j_sel[None, :] <= blk[:, None]
        score = jnp.where(causal, jnp.where(forced, FORCE_SCORE, imp), -jnp.inf)
        top_s, top_i = lax.top_k(score, n_top)
        kg = gather(k_blocks, top_i)
        vg = gather(v_blocks, top_i).astype(f32)
        key_pos = top_i[..., None] * NSA_SEL_LEN + sel_off
        sel_mask = jnp.isfinite(top_s)[..., None] & (key_pos <= pos[None, None, :, None, None])
        ss = jnp.einsum('bqghd,bgqnld->bghqnl', qb, kg).astype(f32) * ATTN_SCALE
        ss = ss.reshape(b, NSA_GROUPS, NSA_HPG, NSA_Q_BLOCK, n_top * NSA_SEL_LEN)
        p_slc = masked_softmax(ss, sel_mask.reshape(b, NSA_GROUPS, 1, NSA_Q_BLOCK, n_top * NSA_SEL_LEN))
        p_slc = p_slc.reshape(b, NSA_GROUPS, NSA_HPG, NSA_Q_BLOCK, n_top, NSA_SEL_LEN)
        o_slc = jnp.einsum('bghqnl,bgqnld->bqghd', p_slc, vg)
        kw = lax.dynamic_slice_in_dim(k_win_p, t0, NSA_WINDOW + NSA_Q_BLOCK, axis=1)
        vw = lax.dynamic_slice_in_dim(v_win_p, t0, NSA_WINDOW + NSA_Q_BLOCK, axis=1).astype(f32)
        kpos = t0 - NSA_WINDOW + win_off
        wmask = ((kpos[None, :] <= pos[:, None]) & (pos[:, None] - kpos[None, :] < NSA_WINDOW)
                 & (kpos[None, :] >= 0))
        sw = jnp.einsum('bqghd,bkgd->bghqk', qb, kw).astype(f32) * ATTN_SCALE
        p_win = masked_softmax(sw, wmask)
        o_win = jnp.einsum('bghqk,bkgd->bqghd', p_win, vw)
        gb = lax.dynamic_slice_in_dim(gates, t0, NSA_Q_BLOCK, axis=1)
        return gb[..., 0:1] * o_cmp + gb[..., 1:2] * o_slc + gb[..., 2:3] * o_win

    out = lax.map(block_fn, jnp.arange(s // NSA_Q_BLOCK))
    return jnp.moveaxis(out, 0, 1).reshape(b, s, NSA_HEADS * HEAD_DIM).astype(q.dtype)


def hgrn2_mixer(u, lb, norm_g):
    f32 = jnp.float32
    b, s, _ = u.shape
    q, fz, i_in, g = jnp.split(u, 4, axis=-1)
    f = lb + (1.0 - lb) * jax.nn.sigmoid(fz.astype(f32))
    log_f = jnp.log(f)
    k = 1.0 - f
    n_chunks = s // HGRN_CHUNK

    def to_chunks(a):
        return a.astype(f32).reshape(b, n_chunks, HGRN_CHUNK, HGRN_HEADS, HEAD_DIM).transpose(1, 0, 3, 2, 4)

    tri = jnp.tril(jnp.ones((HGRN_CHUNK, HGRN_CHUNK), dtype=bool))

    def step(state, inp):
        qc, lfc, kc, vc = inp
        a = jnp.cumsum(lfc, axis=2)
        rel = a[:, :, :, None, :] - a[:, :, None, :, :]
        decay = jnp.exp(jnp.where(tri[:, :, None], rel, -jnp.inf))
        scores = jnp.einsum('bhtk,bhsk,bhtsk->bhts', qc, kc, decay)
        o = (jnp.einsum('bhts,bhsv->bhtv', scores, vc)
             + jnp.einsum('bhtk,bhkv->bhtv', qc * jnp.exp(a), state))
        a_last = a[:, :, -1:, :]
        new_state = (jnp.exp(a_last[:, :, 0, :])[..., None] * state
                     + jnp.einsum('bhsk,bhsv->bhkv', kc * jnp.exp(a_last - a), vc))
        return new_state, o

    state0 = jnp.zeros((b, HGRN_HEADS, HEAD_DIM, HEAD_DIM), f32)
    _, o = lax.scan(step, state0, (to_chunks(q), to_chunks(log_f), to_chunks(k), to_chunks(i_in)))
    o = o.transpose(1, 0, 3, 2, 4).reshape(b, s, HGRN_HEADS, HEAD_DIM)
    o = o * lax.rsqrt(jnp.mean(jnp.square(o), axis=-1, keepdims=True) + LN_EPS) * norm_g
    o = o * jax.nn.silu(g.reshape(b, s, HGRN_HEADS, HEAD_DIM).astype(f32))
    return o.reshape(b, s, HGRN_W).astype(u.dtype)


def hybrid_mixer(h, w_in, conv_w, conv_b, conv_ln_g, conv_ln_b, cmp_pe_k, cmp_pe_v, cmp_w1_k, cmp_w2_k,
                 cmp_w1_v, cmp_w2_v, lb, hgrn_norm_g, w_out):
    b, s, _ = h.shape
    u = h @ w_in
    u_conv, u_q, u_kv, u_gate, u_hgrn = jnp.split(u, IN_SPLITS, axis=-1)
    y_conv = conv_module(u_conv, conv_w, conv_b, conv_ln_g, conv_ln_b)
    q = u_q.reshape(b, s, NSA_HEADS, HEAD_DIM)
    kv = u_kv.reshape(b, s, 6, NSA_GROUPS, HEAD_DIM)
    gate_logits = u_gate.reshape(b, s, NSA_HEADS, 3)
    y_nsa = nsa_mixer(q, kv[:, :, 0], kv[:, :, 1], kv[:, :, 2], kv[:, :, 3], kv[:, :, 4], kv[:, :, 5],
                      gate_logits, cmp_pe_k, cmp_pe_v, cmp_w1_k, cmp_w2_k, cmp_w1_v, cmp_w2_v)
    y_hgrn = hgrn2_mixer(u_hgrn, lb, hgrn_norm_g)
    return jnp.concatenate([y_conv, y_nsa, y_hgrn], axis=-1) @ w_out


def conv_ffn(h, w_up, conv_w, conv_b, w_down):
    u = causal_dwconv(h @ w_up, conv_w, conv_b)
    val, gate = jnp.split(u, 2, axis=-1)
    return (jax.nn.silu(gate) * val) @ w_down


def setup_inputs(seed: int = 0) -> dict:
    key = jax.random.key(seed)
    ks = jax.random.split(key, 26)

    def nrm(k, shape, scale):
        return jax.random.normal(k, shape, jnp.float32) * scale

    return {
        'x': nrm(ks[0], (BATCH, SEQ, D_MODEL), 1.0),
        'p': nrm(ks[1], (DEPTH, BATCH, SEQ, D_PLE), 1.0),
        'w_in': nrm(ks[2], (DEPTH, D_MODEL, IN_COLS), D_MODEL ** -0.5),
        'conv_w': nrm(ks[3], (DEPTH, CONV_K, CONV_CH), CONV_K ** -0.5),
        'conv_b': nrm(ks[4], (DEPTH, CONV_CH), 0.02),
        'conv_ln_g': 1.0 + nrm(ks[5], (DEPTH, CONV_CH), 0.02),
        'conv_ln_b': nrm(ks[6], (DEPTH, CONV_CH), 0.02),
        'cmp_pe_k': nrm(ks[7], (DEPTH, NSA_CMP_LEN, HEAD_DIM), 0.02),
        'cmp_pe_v': nrm(ks[8], (DEPTH, NSA_CMP_LEN, HEAD_DIM), 0.02),
        'cmp_w1_k': nrm(ks[9], (DEPTH, NSA_CMP_LEN * HEAD_DIM, NSA_CMP_HIDDEN), (NSA_CMP_LEN * HEAD_DIM) ** -0.5),
        'cmp_w2_k': nrm(ks[10], (DEPTH, NSA_CMP_HIDDEN, HEAD_DIM), NSA_CMP_HIDDEN ** -0.5),
        'cmp_w1_v': nrm(ks[11], (DEPTH, NSA_CMP_LEN * HEAD_DIM, NSA_CMP_HIDDEN), (NSA_CMP_LEN * HEAD_DIM) ** -0.5),
        'cmp_w2_v': nrm(ks[12], (DEPTH, NSA_CMP_HIDDEN, HEAD_DIM), NSA_CMP_HIDDEN ** -0.5),
        'lb_logits': nrm(ks[13], (DEPTH, HGRN_W), 0.1),
        'hgrn_norm_g': 1.0 + nrm(ks[14], (DEPTH, HEAD_DIM), 0.02),
        'w_out': nrm(ks[15], (DEPTH, MIX_W, D_MODEL), MIX_W ** -0.5 * DN_BETA),
        'ln1_g': 1.0 + nrm(ks[16], (DEPTH, D_MODEL), 0.02),
        'ln1_b': nrm(ks[17], (DEPTH, D_MODEL), 0.02),
        'w_up': nrm(ks[18], (DEPTH, D_MODEL, 2 * D_FF), D_MODEL ** -0.5),
        'ffn_conv_w': nrm(ks[19], (DEPTH, FFN_CONV_K, 2 * D_FF), FFN_CONV_K ** -0.5),
        'ffn_conv_b': nrm(ks[20], (DEPTH, 2 * D_FF), 0.02),
        'w_down': nrm(ks[21], (DEPTH, D_FF, D_MODEL), D_FF ** -0.5 * DN_BETA),
        'w_ple_gate': nrm(ks[22], (DEPTH, D_MODEL, D_MODEL), D_MODEL ** -0.5),
        'w_ple_proj': nrm(ks[23], (DEPTH, D_PLE, D_MODEL), D_PLE ** -0.5 * DN_BETA),
        'ln2_g': 1.0 + nrm(ks[24], (DEPTH, D_MODEL), 0.02),
        'ln2_b': nrm(ks[25], (DEPTH, D_MODEL), 0.02),
    }


def reference(x, p, w_in, conv_w, conv_b, conv_ln_g, conv_ln_b, cmp_pe_k, cmp_pe_v, cmp_w1_k, cmp_w2_k,
              cmp_w1_v, cmp_w2_v, lb_logits, hgrn_norm_g, w_out, ln1_g, ln1_b, w_up, ffn_conv_w, ffn_conv_b,
              w_down, w_ple_gate, w_ple_proj, ln2_g, ln2_b):
    probs = jax.nn.softmax(lb_logits.astype(jnp.float32), axis=0)
    lbs = jnp.cumsum(probs, axis=0) - probs[0]
    h = x
    for i in range(DEPTH):
        m = hybrid_mixer(h, w_in[i], conv_w[i], conv_b[i], conv_ln_g[i], conv_ln_b[i], cmp_pe_k[i], cmp_pe_v[i],
                         cmp_w1_k[i], cmp_w2_k[i], cmp_w1_v[i], cmp_w2_v[i], lbs[i], hgrn_norm_g[i], w_out[i])
        h = layer_norm(DN_ALPHA * h + m, ln1_g[i], ln1_b[i])
        f = conv_ffn(h, w_up[i], ffn_conv_w[i], ffn_conv_b[i], w_down[i])
        e = jax.nn.sigmoid(h @ w_ple_gate[i]) * (p[i] @ w_ple_proj[i])
        h = layer_norm(DN_ALPHA * h + f + e, ln2_g[i], ln2_b[i])
    return h
```

```python
import numpy as np
from contextlib import ExitStack
import concourse.bass as bass
import concourse.mybir as mybir
from concourse.bass_utils import run_bass_kernel_spmd

F32 = mybir.dt.float32
BF16 = mybir.dt.bfloat16
AF = mybir.ActivationFunctionType
ALU = mybir.AluOpType

PE, ACT, DVE, POOL, SP = "tensor", "scalar", "vector", "gpsimd", "sync"
COMPUTE = (PE, ACT, DVE, POOL)
ALLENG = (PE, ACT, DVE, POOL, SP)
INORDER_SAFE = (PE, ACT, DVE)

S_LEN = 4096
D = 1024
DEPTH = 4
NT = S_LEN // 128
NCH = S_LEN // 512
IN_COLS = 2840
D_FF = 2816
NFF = D_FF // 128
ALPHA = (2 * DEPTH) ** 0.25
LN_EPS = 1e-5
NEG = -30000.0
DBGA = 9
STORES_ON_POOL = True
DBGE = 9
DBGD = 9
ACT_FENCE = False
DBGX = 0


class T:
    __slots__ = ("ap", "w", "r", "psum")

    def __init__(self, ap, psum=False):
        self.ap = ap
        self.w = {}
        self.r = {}
        self.psum = psum

    def __getitem__(self, k):
        return self.ap[k]


class Sched:
    def __init__(self, nc, stack, n_dma=40, marked=None):
        self.nc = nc
        self.marked = marked
        self.waited = set()
        self.mrank = {e: 0 for e in COMPUTE}
        self.rank_of = {}
        self.act_scratch = None
        self.cnt = {e: 0 for e in COMPUTE}
        self.known = {e: {} for e in ALLENG}
        self.n_dma = n_dma
        self.dma_cnt = [0] * n_dma
        self.dma_next = 0
        self.sems = {}
        for e in COMPUTE:
            self.sems[e] = stack.enter_context(nc.semaphore("s_" + e))
        for k in range(n_dma):
            self.sems[("dma", k)] = stack.enter_context(nc.semaphore("s_dma%d" % k))
        self.n_ins = 0

    def _deps(self, eng, reads, writes):
        deps = {}
        for t in reads:
            for k, v in t.w.items():
                if deps.get(k, 0) < v:
                    deps[k] = v
            if t.psum:
                for k, v in t.r.items():
                    if k != eng and deps.get(k, 0) < v:
                        deps[k] = v
        for t in writes:
            for src in (t.w, t.r):
                for k, v in src.items():
                    if k == eng and eng in INORDER_SAFE:
                        continue
                    if deps.get(k, 0) < v:
                        deps[k] = v
        kn = self.known[eng]
        out = []
        for k, v in deps.items():
            if kn.get(k, 0) >= v:
                continue
            kn[k] = v
            out.append((k, v))
        return out

    def _commit(self, tok, reads, writes):
        k, v = tok
        for t in reads:
            if t.r.get(k, 0) < v:
                t.r[k] = v
        for t in writes:
            t.w = {k: v}
            t.r = {}

    def _wait(self, e, k, v):
        if isinstance(k, tuple):
            e.wait_ge(self.sems[k], v)
            return
        self.waited.add((k, v))
        val = v if self.marked is None else self.rank_of[(k, v)]
        e.wait_ge(self.sems[k], val)

    def op(self, eng, fn, reads=(), writes=()):
        e = getattr(self.nc, eng)
        deps = self._deps(eng, reads, writes)
        if ACT_FENCE and eng == ACT and self.act_scratch is not None and any(isinstance(k, tuple) for k, v in deps):
            dma_deps = [(k, v) for k, v in deps if isinstance(k, tuple)]
            deps = [(k, v) for k, v in deps if not isinstance(k, tuple)]
            dv = getattr(self.nc, DVE)
            for k, v in dma_deps:
                if self.known[DVE].get(k, 0) < v:
                    self.known[DVE][k] = v
                    self._wait(dv, k, v)
            sc = self.act_scratch
            fins = dv.tensor_copy(out=sc[:, 0:4], in_=sc[:, 8:12])
            self.cnt[DVE] += 1
            fseq = self.cnt[DVE]
            if self.marked is None or (DVE, fseq) in self.marked:
                self.mrank[DVE] += 1
                self.rank_of[(DVE, fseq)] = self.mrank[DVE]
                fins.then_inc(self.sems[DVE], 1)
            if self.known[ACT].get(DVE, 0) < fseq:
                self.known[ACT][DVE] = fseq
                deps = [(k, v) for k, v in deps if k != DVE] + [(DVE, fseq)]
        for k, v in deps:
            self._wait(e, k, v)
        ins = fn(e)
        self.cnt[eng] += 1
        seq = self.cnt[eng]
        if self.marked is None or (eng, seq) in self.marked:
            self.mrank[eng] += 1
            self.rank_of[(eng, seq)] = self.mrank[eng]
            ins.then_inc(self.sems[eng], 1)
        self._commit((eng, seq), reads, writes)
        self.n_ins += 1

    def dma(self, out_t, out_ap, in_t, in_ap, queue=SP, **kw):
        if not STORES_ON_POOL:
            queue = SP
        e = getattr(self.nc, queue)
        k = self.dma_next
        self.dma_next = (k + 1) % self.n_dma
        key = ("dma", k)
        waits = self._deps(queue, [in_t], [out_t])
        if self.known[queue].get(key, 0) < self.dma_cnt[k]:
            self.known[queue][key] = self.dma_cnt[k]
            waits.append((key, self.dma_cnt[k]))
        for kk, v in waits:
            self._wait(e, kk, v)
        self.dma_cnt[k] += 16
        e.dma_start(out=out_ap, in_=in_ap, **kw).then_inc(self.sems[key], 16)
        self._commit((key, self.dma_cnt[k]), [in_t], [out_t])
        self.n_ins += 1

    def barrier(self, engines=ALLENG):
        cur = {e: self.cnt[e] for e in COMPUTE}
        for k in range(self.n_dma):
            cur[("dma", k)] = self.dma_cnt[k]
        for eng in engines:
            e = getattr(self.nc, eng)
            kn = self.known[eng]
            for k, v in cur.items():
                if v > 0 and kn.get(k, 0) < v:
                    kn[k] = v
                    self._wait(e, k, v)


class Ctx:
    pass


def build(depth=DEPTH, debug=False, phases="ABCDEF"):
    waited = _build(depth, debug, phases, None)[1]
    return _build(depth, debug, phases, waited)[0]


def _build(depth, debug, phases, marked):
    nc = bass.Bass("TRN2", target_bir_lowering=False)
    kind_dbg = "ExternalOutput" if debug else "Internal"

    def dram(name, shape, dt, kind="Internal"):
        return nc.dram_tensor(name, list(shape), dt, kind=kind).ap()

    x_in = dram("x", [S_LEN, D], F32, "ExternalInput")
    p_in = dram("p", [DEPTH, S_LEN, 256], F32, "ExternalInput")
    Wd = {}
    wshapes = {
        "w_in": [DEPTH, D, IN_COLS], "convp": [DEPTH, 32, 256], "conv_ln_g": [DEPTH, 256],
        "conv_ln_b": [DEPTH, 256], "cmp_pe_k": [DEPTH, 32, 64], "cmp_pe_v": [DEPTH, 32, 64],
        "cmp_w1_k": [DEPTH, 2048, 128], "cmp_w2_k": [DEPTH, 128, 64], "cmp_w1_v": [DEPTH, 2048, 128],
        "cmp_w2_v": [DEPTH, 128, 64], "lb_logits": [DEPTH, 256], "hgrn_norm_g": [DEPTH, 64],
        "w_out": [DEPTH, D, D], "ln1_g": [DEPTH, D], "ln1_b": [DEPTH, D], "w_up": [DEPTH, D, 2 * D_FF],
        "ffnp": [DEPTH, 4, 2 * D_FF], "w_down": [DEPTH, D_FF, D], "w_ple_gate": [DEPTH, D, D],
        "w_ple_proj": [DEPTH, 256, D], "ln2_g": [DEPTH, D], "ln2_b": [DEPTH, D],
    }
    for k, shp in wshapes.items():
        Wd[k] = dram(k, shp, F32, "ExternalInput")
    y_out = dram("y", [S_LEN, D], F32, "ExternalOutput")

    hres = dram("hres", [S_LEN, D], F32, kind_dbg)
    convT_d = dram("convT_d", [4, 128, S_LEN], BF16)
    QT_d = dram("QT_d", [8, 64, S_LEN], BF16)
    KT_d = dram("KT_d", [8, 64, S_LEN], BF16)
    HQ_d = dram("HQ_d", [4, 64, S_LEN], F32)
    HF_d = dram("HF_d", [4, 64, S_LEN], F32)
    Vtm_d = dram("Vtm_d", [S_LEN, 4, 65], BF16)
    Gtm_d = dram("Gtm_d", [S_LEN, 24], F32)
    Htm_d = dram("Htm_d", [S_LEN, 896], F32)
    mix_d = dram("mix_d", [S_LEN, D], BF16, kind_dbg)
    h1_d = dram("h1_d", [S_LEN, D], F32, kind_dbg)
    hT1_d = dram("hT1_d", [8, 128, S_LEN], BF16)
    g_d = dram("g_d", [NFF, 128, S_LEN], BF16)

    dtiles = {}

    def DT(name, idx=0):
        key = (name, idx)
        if key not in dtiles:
            dtiles[key] = T(None)
        return dtiles[key]

    with ExitStack() as top:
        top.enter_context(nc.allow_low_precision("bf16 matmul operands, fp32 accumulation"))
        top.enter_context(nc.allow_non_contiguous_dma("small parameter loads"))
        S = Sched(nc, top, marked=marked)
        C = Ctx()

        uid = [0]

        def sb(stack, name, shape, dt):
            uid[0] += 1
            return T(stack.enter_context(nc.sbuf_tensor("%s_%d" % (name, uid[0]), list(shape), dt)))

        psum = [T(top.enter_context(nc.psum_tensor("ps%d" % i, [128, 512], F32)), psum=True) for i in range(8)]
        ps_i = [0]

        def PS():
            t = psum[ps_i[0] % 8]
            ps_i[0] += 1
            return t

        rr = [0]

        def evac_eng():
            rr[0] += 1
            return ACT if rr[0] % 2 == 0 else DVE

        def copy(eng, out_t, out_ap, in_t, in_ap):
            if eng == ACT:
                S.op(ACT, lambda e: e.activation(out=out_ap, in_=in_ap, func=AF.Copy), [in_t], [out_t])
            else:
                S.op(eng, lambda e: e.tensor_copy(out=out_ap, in_=in_ap), [in_t], [out_t])

        act_sc = sb(top, "act_sc", [128, 16], F32)
        S.op(POOL, lambda e: e.memset(act_sc[:, :], 0.0), [], [act_sc])
        S.barrier()
        S.act_scratch = act_sc
        ones_f = sb(top, "ones_f", [128, 512], F32)
        S.op(POOL, lambda e: e.memset(ones_f[:, :], 1.0), [], [ones_f])
        zeros_f = sb(top, "zeros_f", [128, 512], F32)
        S.op(POOL, lambda e: e.memset(zeros_f[:, :], 0.0), [], [zeros_f])
        ident_f = sb(top, "ident_f", [128, 128], F32)
        S.op(POOL, lambda e: e.affine_select(out=ident_f[:, :], in_=ones_f[:, 0:128], pattern=[[-1, 128]],
                                             compare_op=ALU.is_equal, fill=0.0, base=0, channel_multiplier=1),
             [ones_f], [ident_f])
        ident_b = sb(top, "ident_b", [128, 128], BF16)
        copy(DVE, ident_b, ident_b[:, :], ident_f, ident_f[:, :])

        def load_bf16(stack, name, dram_ap, kt, ncols, stage_cols=512):
            dst = sb(stack, name, [128, kt, ncols], BF16)
            src = dram_ap.rearrange("(k p) n -> p k n", p=128)
            with ExitStack() as st:
                stg = [sb(st, name + "_stg%d" % i, [128, kt, stage_cols], F32) for i in range(2)]
                i = 0
                for c0 in range(0, ncols, stage_cols):
                    n = min(stage_cols, ncols - c0)
                    s = stg[i % 2]
                    S.dma(s, s[:, :, 0:n], DT(name + "_src"), src[:, :, c0:c0 + n])
                    eng = (DVE, POOL, ACT)[i % 3]
                    copy(eng, dst, dst[:, :, c0:c0 + n], s, s[:, :, 0:n])
                    i += 1
                S.barrier()
            return dst

        def transposes_bf(dst_t, dst_ap, src_t, src_ap_fn, n):
            ps = PS()
            pb = ps.ap.bitcast(BF16)
            for i in range(n):
                S.op(PE, lambda e, i=i: e.transpose(out=pb[:, i * 128:(i + 1) * 128], in_=src_ap_fn(i),
                                                    identity=ident_b[:, :]), [src_t, ident_b], [ps])
            eng = evac_eng()
            copy(eng, dst_t, dst_ap, ps, pb[:, 0:n * 128].rearrange("p (k t) -> p k t", k=n))

        def layernorm(stack_tiles, z, width, g_bc, b_bc, out_t, out_ap):
            st6, mv, rstd = stack_tiles
            nchunk = width // 256 if width > 512 else 1
            cw = width // nchunk
            for i in range(nchunk):
                S.op(DVE, lambda e, i=i: e.bn_stats(out=st6[:, i * 6:(i + 1) * 6], in_=z[:, i * cw:(i + 1) * cw]),
                     [z], [st6])
            S.op(DVE, lambda e: e.bn_aggr(out=mv[:, :], in_=st6[:, 0:nchunk * 6]), [st6], [mv])
            S.op(DVE, lambda e: e.tensor_scalar(out=rstd[:, :], in0=mv[:, 1:2], scalar1=LN_EPS, scalar2=None,
                                                op0=ALU.add), [mv], [rstd])
            S.op(ACT, lambda e: e.activation(out=rstd[:, :], in_=rstd[:, :], func=AF.Sqrt), [rstd], [rstd])
            S.op(DVE, lambda e: e.reciprocal(out=rstd[:, :], in_=rstd[:, :]), [rstd], [rstd])
            S.op(DVE, lambda e: e.tensor_scalar(out=z[:, 0:width], in0=z[:, 0:width], scalar1=mv[:, 0:1],
                                                scalar2=rstd[:, 0:1], op0=ALU.subtract, op1=ALU.mult),
                 [z, mv, rstd], [z])
            S.op(POOL, lambda e: e.tensor_tensor(out=z[:, 0:width], in0=z[:, 0:width], in1=g_bc[:, 0:width],
                                                 op=ALU.mult), [z, g_bc], [z])
            S.op(POOL, lambda e: e.tensor_tensor(out=out_ap, in0=z[:, 0:width], in1=b_bc[:, 0:width],
                                                 op=ALU.add), [z, b_bc], [out_t])

        C.nc, C.S, C.sb, C.PS, C.copy, C.evac_eng, C.DT = nc, S, sb, PS, copy, evac_eng, DT
        C.load_bf16, C.transposes_bf, C.layernorm = load_bf16, transposes_bf, layernorm
        C.ident_f, C.ident_b, C.ones_f, C.zeros_f = ident_f, ident_b, ones_f, zeros_f
        C.Wd, C.x_in, C.p_in, C.y_out = Wd, x_in, p_in, y_out
        C.psum_banks = psum
        C.scr = dict(hres=hres, convT_d=convT_d, QT_d=QT_d, KT_d=KT_d, HQ_d=HQ_d, HF_d=HF_d, Vtm_d=Vtm_d,
                     Gtm_d=Gtm_d, Htm_d=Htm_d, mix_d=mix_d, h1_d=h1_d, hT1_d=hT1_d, g_d=g_d)
        S.barrier()

        for L in range(depth):
            h_src = x_in if L == 0 else hres
            h_dst = y_out if L == depth - 1 else hres
            for nm, fn, args in (("A", phase_proj, (h_src,)), ("B", phase_conv, ()), ("C", phase_nsa, ()),
                                 ("D", phase_hgrn, ()), ("E", phase_ffn1, (h_src,)), ("F", phase_ffn2, (h_dst,))):
                if nm in phases:
                    fn(C, L, *args)
                    S.barrier()
        S.barrier()
    return nc, S.waited


def phase_proj(C, L, h_src):
    nc, S, sb, PS, copy, DT = C.nc, C.S, C.sb, C.PS, C.copy, C.DT
    scr = C.scr
    with ExitStack() as ph:
        win = C.load_bf16(ph, "win", C.Wd["w_in"][L], 8, IN_COLS, stage_cols=568)
        hraw = [sb(ph, "hraw%d" % i, [128, D], F32) for i in range(2)]
        hb = [sb(ph, "hb%d" % i, [128, D], BF16) for i in range(2)]
        hT = [sb(ph, "hT%d" % i, [128, 8, 512], BF16) for i in range(2)]
        st_conv = [sb(ph, "st_conv%d" % i, [128, 4, 512], BF16) for i in range(2)]
        st_q = [sb(ph, "st_q%d" % i, [64, 8, 512], BF16) for i in range(2)]
        st_k = [sb(ph, "st_k%d" % i, [64, 8, 512], BF16) for i in range(2)]
        st_hq = [sb(ph, "st_hq%d" % i, [64, 4, 512], F32) for i in range(2)]
        st_hf = [sb(ph, "st_hf%d" % i, [64, 4, 512], F32) for i in range(2)]
        st_v = [sb(ph, "st_v%d" % i, [128, 4, 65], BF16) for i in range(2)]
        for i in range(2):
            S.op(POOL, lambda e, i=i: e.memset(st_v[i][:, :, :], 1.0), [], [st_v[i]])
        st_g = [sb(ph, "st_g%d" % i, [128, 24], F32) for i in range(2)]
        st_h = [sb(ph, "st_h%d" % i, [128, 896], F32) for i in range(2)]
        for i in range(2):
            S.op(POOL, lambda e, i=i: e.memset(st_h[i][:, 768:896], 0.0), [], [st_h[i]])

        KV0 = 1024
        fm = []
        for j in range(4):
            fm.append((j * 128, 128, st_conv, j))
        for h in range(8):
            fm.append((512 + h * 64, 64, st_q, h))
        kvsel = [(0, 0), (0, 1), (1, 0), (1, 1), (2, 0), (2, 1), (4, 0), (4, 1)]
        for n, (j, g) in enumerate(kvsel):
            fm.append((KV0 + (j * 2 + g) * 64, 64, st_k, n))
        for h in range(4):
            fm.append((1816 + h * 64, 64, st_hq, h))
        for h in range(4):
            fm.append((1816 + 256 + h * 64, 64, st_hf, h))

        def prep_chunk(c):
            b = c % 2
            for t in range(4):
                tt = c * 4 + t
                hr, hbb = hraw[tt % 2], hb[tt % 2]
                S.dma(hr, hr[:, :], DT("h", tt), h_src[tt * 128:(tt + 1) * 128, :])
                copy(POOL, hbb, hbb[:, :], hr, hr[:, :])
                C.transposes_bf(hT[b], hT[b][:, :, t * 128:(t + 1) * 128], hbb,
                                lambda i, hbb=hbb: hbb[:, i * 128:(i + 1) * 128], 8)

        prep_chunk(0)
        for c in range(NCH):
            b = c % 2
            if DBGA < 2:
                continue
            for (c0, n, stg, slot) in fm:
                ps = PS()
                for kt in range(8):
                    S.op(PE, lambda e, kt=kt, ps=ps, c0=c0, n=n: e.matmul(
                        ps[0:n, :], lhsT=win[:, kt, c0:c0 + n], rhs=hT[b][:, kt, :], start=(kt == 0), stop=(kt == 7)),
                        [win, hT[b]], [ps])
                copy(C.evac_eng(), stg[b], stg[b][0:n, slot, :], ps, ps[0:n, :])
            sl = slice(c * 512, (c + 1) * 512)
            if c + 1 < NCH:
                prep_chunk(c + 1)
            if DBGA < 3:
                continue
            S.dma(DT("convT", c), scr["convT_d"][:, :, sl].rearrange("j p t -> p j t"), st_conv[b], st_conv[b][:, :, :], queue=POOL)
            S.dma(DT("QT", c), scr["QT_d"][:, :, sl].rearrange("j p t -> p j t"), st_q[b], st_q[b][:, :, :], queue=POOL)
            S.dma(DT("KT", c), scr["KT_d"][:, :, sl].rearrange("j p t -> p j t"), st_k[b], st_k[b][:, :, :], queue=POOL)
            S.dma(DT("HQ", c), scr["HQ_d"][:, :, sl].rearrange("j p t -> p j t"), st_hq[b], st_hq[b][:, :, :], queue=POOL)
            S.dma(DT("HF", c), scr["HF_d"][:, :, sl].rearrange("j p t -> p j t"), st_hf[b], st_hf[b][:, :, :], queue=POOL)
            if DBGA < 4:
                continue
            for t in range(4):
                tt = c * 4 + t
                b2 = tt % 2
                rows = slice(tt * 128, (tt + 1) * 128)
                groups = [(1408, 128), (1664, 152), (2072, 512), (2584, 256)]
                pss = []
                for (c0, n) in groups:
                    ps = PS()
                    pss.append(ps)
                    for kt in range(8):
                        S.op(PE, lambda e, kt=kt, ps=ps, c0=c0, n=n: e.matmul(
                            ps[:, 0:n], lhsT=hT[b][:, kt, t * 128:(t + 1) * 128], rhs=win[:, kt, c0:c0 + n],
                            start=(kt == 0), stop=(kt == 7)), [win, hT[b]], [ps])
                sv, sg, sh = st_v[b2], st_g[b2], st_h[b2]
                if DBGA < 5:
                    continue
                copy(DVE, sv, sv[:, 0:2, 0:64], pss[0], pss[0][:, 0:128].rearrange("p (n d) -> p n d", n=2))
                copy(DVE, sv, sv[:, 2:4, 0:64], pss[1], pss[1][:, 0:128].rearrange("p (n d) -> p n d", n=2))
                if DBGX == 0:
                    S.op(ACT, lambda e, sh=sh, ps=pss[1]: e.activation(out=sh[:, 768:792], in_=ps[:, 128:152], func=AF.Sigmoid),
                         [pss[1]], [sh])
                else:
                    S.op(ACT, lambda e, sg=sg, ps=pss[1]: e.activation(out=sg[:, :], in_=ps[:, 128:152], func=AF.Sigmoid),
                         [pss[1]], [sg])
                copy(ACT if DBGX < 2 else DVE, sh, sh[:, 0:512], pss[2], pss[2][:, 0:512])
                if DBGX != 0:
                    copy(DVE, sh, sh[:, 768:792], sg, sg[:, :])
                copy(DVE, sh, sh[:, 512:768], pss[3], pss[3][:, 0:256])
                if DBGA >= 6:
                    S.dma(DT("Vtm", tt), scr["Vtm_d"][rows, :, :], sv, sv[:, :, :], queue=POOL)
                if DBGA >= 8:
                    S.dma(DT("Htm", tt), scr["Htm_d"][rows, :], sh, sh[:, :], queue=POOL)


def small_T(C, stack, name, src_dram_ap, rows, cols):
    S, sb, PS = C.S, C.sb, C.PS
    nblk = cols // 128
    dst = sb(stack, name, [128, nblk, rows], F32)
    with ExitStack() as tmp:
        src = sb(tmp, name + "_src", [rows, cols], F32)
        S.dma(src, src[:, :], C.DT(name + "_d"), src_dram_ap)
        per = 512 // rows
        j = 0
        while j < nblk:
            n = min(per, nblk - j)
            ps = PS()
            for i in range(n):
                S.op(PE, lambda e, i=i, j=j, ps=ps: e.transpose(out=ps[:, i * rows:(i + 1) * rows],
                                                              in_=src[:, (j + i) * 128:(j + i + 1) * 128],
                                                              identity=C.ident_f[0:rows, 0:rows]), [src, C.ident_f], [ps])
            C.copy(DVE, dst, dst[:, j:j + n, :], ps, ps[:, 0:n * rows].rearrange("p (k r) -> p k r", k=n))
            j += n
        S.barrier()
    return dst


def bc_load(C, stack, name, dram_row_ap, width):
    t = C.sb(stack, name, [128, width], F32)
    C.S.dma(t, t[:, :], C.DT(name + "_d"), dram_row_ap.partition_broadcast(128))
    return t


def phase_conv(C, L):
    nc, S, sb, PS, copy, DT = C.nc, C.S, C.sb, C.PS, C.copy, C.DT
    scr = C.scr
    with ExitStack() as ph:
        cw = small_T(C, ph, "cw", C.Wd["convp"][L], 32, 256)
        g_bc = bc_load(C, ph, "cln_g", C.Wd["conv_ln_g"][L], 256)
        b_bc = bc_load(C, ph, "cln_b", C.Wd["conv_ln_b"][L], 256)
        dg = sb(ph, "dg", [128, 2, 31, 128], BF16)
        n = 0
        for j in range(2):
            for k in range(31):
                eng = DVE if n % 2 == 0 else POOL
                n += 1
                S.op(eng, lambda e, j=j, k=k: e.tensor_scalar(out=dg[:, j, k, :], in0=C.ident_f[:, :],
                                                               scalar1=cw[:, j, k:k + 1], scalar2=None, op0=ALU.mult),
                     [C.ident_f, cw], [dg])
        cin = [sb(ph, "cin%d" % i, [128, 4, 542], BF16) for i in range(2)]
        sg = [sb(ph, "csg%d" % i, [128, 2, 542], BF16) for i in range(2)]
        glu = [sb(ph, "glu%d" % i, [128, 2, 542], BF16) for i in range(2)]
        cT = [sb(ph, "cT%d" % i, [128, 512], F32) for i in range(2)]
        z = [sb(ph, "cz%d" % i, [128, 256], F32) for i in range(2)]
        z2 = [sb(ph, "cz2%d" % i, [128, 256], F32) for i in range(2)]
        yb = [sb(ph, "cy%d" % i, [128, 256], BF16) for i in range(2)]
        st6 = sb(ph, "cst6", [128, 24], F32)
        mv = sb(ph, "cmv", [128, 2], F32)
        rstd = sb(ph, "crstd", [128, 1], F32)
        for i in range(2):
            S.op(POOL, lambda e, i=i: e.memset(cin[i][:, :, 0:30], 0.0), [], [cin[i]])
        for c in range(NCH):
            b = c % 2
            ci = cin[b]
            if c == 0:
                S.dma(ci, ci[:, :, 30:542], DT("convT", 0), scr["convT_d"][:, :, 0:512].rearrange("j p t -> p j t"))
            else:
                S.dma(ci, ci[:, :, :], DT("convT", c),
                      scr["convT_d"][:, :, c * 512 - 30:(c + 1) * 512].rearrange("j p t -> p j t"))
            S.op(ACT, lambda e: e.activation(out=sg[b][:, :, :], in_=ci[:, 2:4, :], func=AF.Sigmoid), [ci], [sg[b]])
            S.op(DVE, lambda e: e.tensor_tensor(out=glu[b][:, :, :], in0=ci[:, 0:2, :], in1=sg[b][:, :, :], op=ALU.mult),
                 [ci, sg[b]], [glu[b]])
            cts = []
            for j in range(2):
                ps = PS()
                for k in range(31):
                    S.op(PE, lambda e, j=j, k=k, ps=ps: e.matmul(ps[:, :], lhsT=dg[:, j, k, :], rhs=glu[b][:, j, k:k + 512],
                                                                   start=(k == 0), stop=(k == 30)), [dg, glu[b]], [ps])
                ct = cT[j]
                S.op(ACT, lambda e, ps=ps, ct=ct, j=j: e.activation(out=ct[:, :], in_=ps[:, :], func=AF.Identity,
                                                                    bias=cw[:, j, 31:32]), [ps, cw], [ct])
                cts.append(ct)
            for t in range(4):
                tt = c * 4 + t
                zz, zz2, yy = z[tt % 2], z2[tt % 2], yb[tt % 2]
                ps = PS()
                for j in range(2):
                    S.op(PE, lambda e, j=j, ps=ps: e.transpose(out=ps[:, j * 128:(j + 1) * 128],
                                                                in_=cts[j][:, t * 128:(t + 1) * 128],
                                                                identity=C.ident_f[:, :]), [cts[j], C.ident_f], [ps])
                copy(ACT, zz, zz[:, :], ps, ps[:, 0:256])
                C.layernorm((st6, mv, rstd), zz, 256, g_bc, b_bc, zz2, zz2[:, :])
                S.op(ACT, lambda e: e.activation(out=yy[:, :], in_=zz2[:, :], func=AF.Silu), [zz2], [yy])
                S.dma(DT("mixA", tt), scr["mix_d"][tt * 128:(tt + 1) * 128, 0:256], yy, yy[:, :], queue=POOL)


def phase_nsa(C, L):
    nc, S, sb, PS, copy, DT = C.nc, C.S, C.sb, C.PS, C.copy, C.DT
    scr = C.scr
    ident_b, ident_f, ones_f, zeros_f = C.ident_b, C.ident_f, C.ones_f, C.zeros_f
    with ExitStack() as ph:
        zb = sb(ph, "zb", [128, 2176], BF16)
        S.op(POOL, lambda e: e.memset(zb[:, :], 0.0), [], [zb])
        ob = sb(ph, "ob", [128, 512], BF16)
        S.op(POOL, lambda e: e.memset(ob[:, :], 1.0), [], [ob])
        CM4 = sb(ph, "CM4", [128, 4, 128], BF16)
        S.op(POOL, lambda e: e.affine_select(out=CM4[:, :, :], in_=zb[:, 0:512].rearrange("p (h q) -> p h q", h=4),
                                             pattern=[[0, 4], [1, 128]], compare_op=ALU.is_ge, fill=NEG, base=0,
                                             channel_multiplier=-1), [zb], [CM4])
        WM4 = sb(ph, "WM4", [128, 4, 128], BF16)
        S.op(POOL, lambda e: e.affine_select(out=WM4[:, :, :], in_=zb[:, 0:512].rearrange("p (h q) -> p h q", h=4),
                                             pattern=[[0, 4], [-1, 128]], compare_op=ALU.is_gt, fill=NEG, base=0,
                                             channel_multiplier=1), [zb], [WM4])
        Mtab = sb(ph, "Mtab", [128, 2176], BF16)
        S.op(POOL, lambda e: e.affine_select(out=Mtab[:, :], in_=zb[:, :], pattern=[[1, 2176]], compare_op=ALU.is_ge,
                                             fill=NEG, base=-31, channel_multiplier=-16), [zb], [Mtab])
        Ebig = sb(ph, "Ebig", [64, 4096], BF16)
        Etmp = sb(ph, "Etmp", [64, 4096], BF16)
        S.op(POOL, lambda e: e.memset(Etmp[:, :], 1.0), [], [Etmp])
        S.op(POOL, lambda e: e.affine_select(out=Ebig[:, :], in_=Etmp[:, :], pattern=[[1, 4096]], compare_op=ALU.is_ge,
                                             fill=0.0, base=0, channel_multiplier=-64), [Etmp], [Ebig])
        S.op(POOL, lambda e: e.affine_select(out=Etmp[:, :], in_=Ebig[:, :], pattern=[[-1, 4096]], compare_op=ALU.is_ge,
                                             fill=0.0, base=63, channel_multiplier=64), [Ebig], [Etmp])
        Ebig = Etmp
        Cm = []
        for ct in range(2):
            c1 = sb(ph, "Cm_a%d" % ct, [128, 64], BF16)
            c2 = sb(ph, "Cm_b%d" % ct, [128, 64], BF16)
            S.op(POOL, lambda e, ct=ct, c1=c1: e.affine_select(out=c1[:, :], in_=ob[:, 0:64], pattern=[[-4, 64]],
                                                               compare_op=ALU.is_ge, fill=0.0, base=128 * ct + 1,
                                                               channel_multiplier=1), [ob], [c1])
            S.op(POOL, lambda e, ct=ct, c1=c1, c2=c2: e.affine_select(out=c2[:, :], in_=c1[:, :], pattern=[[4, 64]],
                                                                      compare_op=ALU.is_ge, fill=0.0, base=3 - 128 * ct,
                                                                      channel_multiplier=-1), [c1], [c2])
            Cm.append(c2)
        BON = sb(ph, "BON", [128, 3], F32)
        S.op(POOL, lambda e: e.memset(BON[0:64, 0:1], 2e9), [], [BON])
        S.op(POOL, lambda e: e.memset(BON[0:64, 1:2], 3e9), [], [BON])
        S.op(POOL, lambda e: e.memset(BON[0:64, 2:3], -1e9), [], [BON])
        S.op(POOL, lambda e: e.memset(BON[64:128, 0:1], 0.0), [], [BON])
        S.op(POOL, lambda e: e.memset(BON[64:128, 1:2], 2e9), [], [BON])
        S.op(POOL, lambda e: e.memset(BON[64:128, 2:3], 3e9), [], [BON])

        KcT = sb(ph, "KcT", [64, 2, 256], BF16)
        S.op(POOL, lambda e: e.memset(KcT[:, :, :], 0.0), [], [KcT])
        Vc = sb(ph, "Vc", [128, 2, 2, 65], BF16)
        S.op(POOL, lambda e: e.memset(Vc[:, :, :, :], 1.0), [], [Vc])
        with ExitStack() as cs:
            w1s = sb(cs, "w1s", [64, 32, 128], F32)
            w2s = sb(cs, "w2s", [128, 64], F32)
            pes = sb(cs, "pes", [64, 32], F32)
            peraw = sb(cs, "peraw", [32, 64], F32)
            peb = sb(cs, "peb", [64, 32], BF16)
            w1 = sb(cs, "w1", [64, 32, 128], BF16)
            w2 = sb(cs, "w2", [128, 64], BF16)
            bias = sb(cs, "cbias", [128, 1], F32)
            kc = sb(cs, "kc", [64, S_LEN], BF16)
            xh = sb(cs, "xh", [128, 255], F32)
            x2 = sb(cs, "x2", [128, 255], F32)
            hid = sb(cs, "hid", [128, 256], BF16)
            for kind, (w1n, w2n, pen) in enumerate([("cmp_w1_k", "cmp_w2_k", "cmp_pe_k"),
                                                   ("cmp_w1_v", "cmp_w2_v", "cmp_pe_v")]):
                for l4 in range(4):
                    S.dma(w1s, w1s[:, l4 * 8:(l4 + 1) * 8, :], DT(w1n),
                          C.Wd[w1n][L].rearrange("(l d) m -> d l m", d=64)[:, l4 * 8:(l4 + 1) * 8, :])
                S.dma(w2s, w2s[:, :], DT(w2n), C.Wd[w2n][L])
                S.dma(peraw, peraw[:, :], DT(pen), C.Wd[pen][L])
                pspe = PS()
                S.op(PE, lambda e, pspe=pspe: e.transpose(out=pspe[0:64, 0:32], in_=peraw[:, :], identity=ident_f[0:32, 0:32]),
                     [peraw, ident_f], [pspe])
                copy(DVE, pes, pes[:, :], pspe, pspe[0:64, 0:32])
                copy(DVE, w1, w1[:, :, :], w1s, w1s[:, :, :])
                copy(POOL, w2, w2[:, :], w2s, w2s[:, :])
                copy(POOL, peb, peb[:, :], pes, pes[:, :])
                ps = PS()
                for l in range(32):
                    S.op(PE, lambda e, l=l, ps=ps: e.matmul(ps[:, 0:1], lhsT=w1[:, l, :], rhs=peb[:, l:l + 1],
                                                            start=(l == 0), stop=(l == 31)), [w1, peb], [ps])
                copy(DVE, bias, bias[:, :], ps, ps[:, 0:1])
                for g in range(2):
                    S.dma(kc, kc[:, :], DT("KT_all"), scr["KT_d"][kind * 2 + g, :, :])
                    kv = kc.ap[:, :].rearrange("p (i r) -> p i r", r=16)
                    ps = PS()
                    for l in range(32):
                        rhs = kv[:, 0:255, l] if l < 16 else kv[:, 1:256, l - 16]
                        S.op(PE, lambda e, l=l, ps=ps, rhs=rhs: e.matmul(ps[:, 0:255], lhsT=w1[:, l, :], rhs=rhs,
                                                                         start=(l == 0), stop=(l == 31)), [w1, kc], [ps])
                    S.op(ACT, lambda e, ps=ps: e.activation(out=xh[:, :], in_=ps[:, 0:255], func=AF.Identity,
                                                            bias=bias[:, 0:1]), [ps, bias], [xh])
                    S.op(DVE, lambda e: e.tensor_tensor(out=x2[:, :], in0=xh[:, :], in1=xh[:, :], op=ALU.mult), [xh], [x2])
                    S.op(DVE, lambda e: e.tensor_scalar(out=x2[:, :], in0=x2[:, :], scalar1=0.044715, scalar2=1.0,
                                                        op0=ALU.mult, op1=ALU.add), [x2], [x2])
                    S.op(DVE, lambda e: e.tensor_tensor(out=x2[:, :], in0=x2[:, :], in1=xh[:, :], op=ALU.mult), [x2, xh], [x2])
                    S.op(ACT, lambda e: e.activation(out=x2[:, :], in_=x2[:, :], func=AF.Sigmoid, scale=1.5957691216057308),
                         [x2], [x2])
                    S.op(POOL, lambda e: e.memset(hid[:, 255:256], 0.0), [], [hid])
                    S.op(DVE, lambda e: e.tensor_tensor(out=hid[:, 0:255], in0=x2[:, :], in1=xh[:, :], op=ALU.mult),
                         [x2, xh], [hid])
                    if kind == 0:
                        ps2 = PS()
                        S.op(PE, lambda e, ps2=ps2: e.matmul(ps2[0:64, 0:255], lhsT=w2[:, :], rhs=hid[:, 0:255],
                                                              start=True, stop=True), [w2, hid], [ps2])
                        copy(ACT, KcT, KcT[:, g, 0:255], ps2, ps2[0:64, 0:255])
                    else:
                        for ct in range(2):
                            n = 128 if ct == 0 else 127
                            ps2 = PS()
                            S.op(PE, lambda e, ps2=ps2, ct=ct, n=n: e.matmul(ps2[0:n, 0:64], lhsT=hid[:, ct * 128:ct * 128 + n],
                                                                              rhs=w2[:, :], start=True, stop=True), [w2, hid], [ps2])
                            copy(ACT, Vc, Vc[0:n, ct, g, 0:64], ps2, ps2[0:n, 0:64])
            S.barrier()

        KTs = sb(ph, "KTs", [64, 4, S_LEN], BF16)
        S.dma(KTs, KTs[:, :, :], DT("KT_all"), scr["KT_d"][4:8, :, :].rearrange("j p t -> p j t"))
        Vaug = sb(ph, "Vaug", [128, NT, 260], BF16)
        gq = [sb(ph, "gq%d" % i, [128, 128], F32) for i in range(2)]
        for q4 in range(4):
            S.dma(Vaug, Vaug[:, q4 * 8:(q4 + 1) * 8, :], DT("Vtm_all"),
                  scr["Vtm_d"][q4 * 1024:(q4 + 1) * 1024, :, :].rearrange("(t p) n d -> p t (n d)", p=128))
        QTc = [sb(ph, "QTc%d" % i, [64, 8, 512], BF16) for i in range(2)]
        pT = [sb(ph, "pT%d" % i, [128, 4, 128], BF16) for i in range(3)]
        pT_i = [0]
        from_bank = lambda i: C.psum_banks[i]
        po = [from_bank(0), from_bank(1), from_bank(2)]
        IMP = from_bank(3)
        psT = from_bank(4)
        sc_banks = [from_bank(5), from_bank(6), from_bank(7)]
        sc_i = [0]
        rd = [sb(ph, "rd%d" % i, [128, 4], F32) for i in range(3)]
        rdg = [sb(ph, "rdg%d" % i, [128, 4], F32) for i in range(3)]
        acc = [sb(ph, "acc%d" % i, [128, 8, 64], F32) for i in range(2)]
        tmpo = [sb(ph, "tmpo%d" % i, [128, 4, 64], F32) for i in range(2)]
        ybf = [sb(ph, "ynsa%d" % i, [128, 512], BF16) for i in range(2)]
        imp = sb(ph, "imp", [128, 64], F32)
        imp2 = sb(ph, "imp2", [128, 64], F32)
        m8a = sb(ph, "m8a", [128, 8], F32)
        m8b = sb(ph, "m8b", [128, 8], F32)
        negm = sb(ph, "negm", [128, 64], F32)
        S.op(POOL, lambda e: e.memset(negm[:, :], 0.0), [], [negm])
        negT4 = [sb(ph, "negT4_%d" % i, [64, 4, 128], BF16) for i in range(2)]

        def score_tile(kT_ap, kT_t, rhsQ, qt_t, masks):
            ps = sc_banks[sc_i[0] % 3]
            sc_i[0] += 1
            out3 = ps[:, :].rearrange("p (h q) -> p h q", h=4)
            S.op(PE, lambda e: e.matmul(out3, lhsT=kT_ap, rhs=rhsQ, start=True, stop=(len(masks) == 0),
                                        skip_group_check=True), [kT_t, qt_t], [ps])
            for mi, (l_ap, l_t, r_ap, r_t, osl) in enumerate(masks):
                last = (mi == len(masks) - 1)
                o = out3 if osl is None else ps[:, osl]
                S.op(PE, lambda e, l_ap=l_ap, r_ap=r_ap, o=o, last=last: e.matmul(o, lhsT=l_ap, rhs=r_ap, start=False, stop=last,
                                                                                   skip_group_check=True), [l_t, r_t], [ps])
            p = pT[pT_i[0] % 3]
            pT_i[0] += 1
            S.op(ACT, lambda e: e.activation(out=p[:, :, :], in_=out3, func=AF.Exp, scale=0.125), [ps], [p])
            return p

        def pv(p, bank, v_ap, v_t, first, extra=None):
            S.op(PE, lambda e: e.matmul(bank[0:65, :].rearrange("p (h q) -> p h q", h=4), lhsT=v_ap, rhs=p[:, :, :],
                                        start=first, stop=False, skip_group_check=True), [p, v_t], [bank])

        oT = [sb(ph, "oT%d" % i, [65, 512], F32) for i in range(2)]
        oT_i = [0]
        psT3 = psT[:, 0:260].rearrange("p (h e) -> p h e", h=4)

        def finish_branch(b, g, gts, ac, first):
            o = oT[oT_i[0] % 2]
            oT_i[0] += 1
            copy(ACT, o, o[:, :], po[b], po[b][0:65, :])
            for h in range(4):
                S.op(PE, lambda e, h=h: e.transpose(out=psT[:, h * 65:(h + 1) * 65], in_=o[:, h * 128:(h + 1) * 128],
                                                    identity=ident_f[0:65, 0:65]), [o, ident_f], [psT])
            S.op(DVE, lambda e: e.tensor_scalar(out=rd[b][:, :], in0=psT3[:, :, 64], scalar1=1e-30, scalar2=None,
                                                op0=ALU.add), [psT], [rd[b]])
            S.op(DVE, lambda e: e.reciprocal(out=rd[b][:, :], in_=rd[b][:, :]), [rd[b]], [rd[b]])
            S.op(DVE, lambda e: e.tensor_tensor(out=rdg[b][:, :], in0=rd[b][:, :], in1=gts[:, 4 * g:4 * g + 4, b],
                                                op=ALU.mult), [rd[b], Gall], [rdg[b]])
            rb = rdg[b][:, :].unsqueeze(2).to_broadcast([128, 4, 64])
            if first:
                S.op(DVE, lambda e: e.tensor_tensor(out=ac[:, 4 * g:4 * g + 4, :], in0=psT3[:, :, 0:64], in1=rb,
                                                    op=ALU.mult), [psT, rdg[b]], [ac])
            else:
                tp = tmpo[b % 2]
                S.op(DVE, lambda e: e.tensor_tensor(out=tp[:, :, :], in0=psT3[:, :, 0:64], in1=rb,
                                                    op=ALU.mult), [psT, rdg[b]], [tp])
                S.op(POOL, lambda e: e.tensor_tensor(out=ac[:, 4 * g:4 * g + 4, :], in0=ac[:, 4 * g:4 * g + 4, :],
                                                     in1=tp[:, :, :], op=ALU.add), [ac, tp], [ac])

        pend = []

        def defer(fn):
            if len(pend) >= 2:
                pend.pop(0)()
            pend.append(fn)

        def flush():
            while pend:
                pend.pop(0)()

        for qt in range(NT):
            c, t = qt // 4, qt % 4
            if t == 0:
                qc = QTc[c % 2]
                S.dma(qc, qc[:, :, :], DT("QT", c), scr["QT_d"][:, :, c * 512:(c + 1) * 512].rearrange("j p t -> p j t"))
            qc = QTc[c % 2]
            Gall = gq[qt % 2]
            S.dma(Gall, Gall[:, :], DT("Htm", qt), scr["Htm_d"][qt * 128:(qt + 1) * 128, 768:896])
            ac = acc[qt % 2]
            for g in range(2):
                rhsQ = qc[:, 4 * g:4 * g + 4, t * 128:(t + 1) * 128]
                use_topk = qt >= 8
                cts = [0] if qt < 16 else [0, 1]
                for ci, ct in enumerate(cts):
                    delta = qt - 16 * ct
                    masks = []
                    if delta <= 16:
                        for h in range(4):
                            masks.append((ident_b[:, :], ident_b, Mtab[:, 128 * delta:128 * delta + 128], Mtab,
                                          slice(h * 128, (h + 1) * 128)))
                    p = score_tile(KcT[:, g, ct * 128:(ct + 1) * 128], KcT, rhsQ, qc, masks)

                    def cmp_pv(p=p, ct=ct, ci=ci):
                        pv(p, po[0], Vc[:, ct, g, :], Vc, ci == 0)
                        if use_topk:
                            for h in range(4):
                                S.op(PE, lambda e, h=h: e.matmul(IMP[:, h * 64:(h + 1) * 64], lhsT=p[:, h, :],
                                                                 rhs=Cm[ct][:, :], start=(ci == 0 and h == 0),
                                                                 stop=False, skip_group_check=True), [p, Cm[ct]], [IMP])
                    defer(cmp_pv)
                flush()
                gts = Gall[:, 0:24].rearrange("p (h b) -> p h b", b=3)
                finish_branch(0, g, gts, ac, True)
                nT = None
                if use_topk:
                    W = 2 * qt + 2
                    S.op(DVE, lambda e: e.tensor_scalar(out=imp[:, 0:W], in0=IMP[:, 0:W], scalar1=rd[0][:, 0:1], scalar2=None,
                                                        op0=ALU.mult), [IMP, rd[0]], [imp])
                    for h in range(1, 4):
                        S.op(DVE, lambda e, h=h: e.scalar_tensor_tensor(out=imp[:, 0:W], in0=IMP[:, h * 64:h * 64 + W],
                                                                        scalar=rd[0][:, h:h + 1], in1=imp[:, 0:W],
                                                                        op0=ALU.mult, op1=ALU.add), [IMP, rd[0], imp], [imp])
                    S.op(DVE, lambda e: e.memset(imp[:, 0:1], 4e9), [], [imp])
                    S.op(DVE, lambda e: e.tensor_tensor(out=imp[:, 2 * qt - 1:2 * qt + 2], in0=imp[:, 2 * qt - 1:2 * qt + 2],
                                                        in1=BON[:, :], op=ALU.add), [imp, BON], [imp])
                    S.op(DVE, lambda e: e.max(out=m8a[:, :], in_=imp[:, 0:W]), [imp], [m8a])
                    S.op(DVE, lambda e: e.match_replace(out=imp2[:, 0:W], in_to_replace=m8a[:, :], in_values=imp[:, 0:W],
                                                        imm_value=-3e9), [imp, m8a], [imp2])
                    S.op(DVE, lambda e: e.max(out=m8b[:, :], in_=imp2[:, 0:W]), [imp2], [m8b])
                    S.op(DVE, lambda e: e.tensor_scalar(out=negm[:, 0:W], in0=imp[:, 0:W], scalar1=m8b[:, 7:8], scalar2=NEG,
                                                        op0=ALU.is_lt, op1=ALU.mult), [imp, m8b], [negm])
                    S.op(PE, lambda e: e.transpose(out=psT[0:64, 0:128], in_=negm[:, :], identity=ident_f[:, :]),
                         [negm, ident_f], [psT])
                    nT = negT4[(qt * 2 + g) % 2]
                    copy(DVE, nT, nT[:, :, :], psT, psT[0:64, 0:128].unsqueeze(1).to_broadcast([64, 4, 128]))
                k0 = max(0, qt - 4)
                for kt in range(k0, qt + 1):
                    masks = []
                    if kt == qt:
                        masks.append((ident_b[:, :], ident_b, CM4[:, :, :], CM4, None))
                    if kt == qt - 4:
                        masks.append((ident_b[:, :], ident_b, WM4[:, :, :], WM4, None))
                    p = score_tile(KTs[:, 2 + g, kt * 128:(kt + 1) * 128], KTs, rhsQ, qc, masks)
                    defer(lambda p=p, kt=kt: pv(p, po[2], Vaug[:, kt, (2 + g) * 65:(3 + g) * 65], Vaug, kt == k0))
                for kt in range(qt + 1):
                    masks = []
                    if use_topk:
                        masks.append((Ebig[:, kt * 128:(kt + 1) * 128], Ebig, nT[:, :, :], nT, None))
                    if kt == qt:
                        masks.append((ident_b[:, :], ident_b, CM4[:, :, :], CM4, None))
                    p = score_tile(KTs[:, g, kt * 128:(kt + 1) * 128], KTs, rhsQ, qc, masks)
                    defer(lambda p=p, kt=kt: pv(p, po[1], Vaug[:, kt, g * 65:(g + 1) * 65], Vaug, kt == 0))
                flush()
                finish_branch(2, g, gts, ac, False)
                finish_branch(1, g, gts, ac, False)
            yy = ybf[qt % 2]
            copy(ACT, yy, yy[:, :], ac, ac[:, :, :].rearrange("p h d -> p (h d)"))
            S.dma(DT("mixB", qt), scr["mix_d"][qt * 128:(qt + 1) * 128, 256:768], yy, yy[:, :], queue=POOL)


def phase_hgrn(C, L):
    nc, S, sb, PS, copy, DT = C.nc, C.S, C.sb, C.PS, C.copy, C.DT
    scr = C.scr
    ident_f, ones_f, zeros_f = C.ident_f, C.ones_f, C.zeros_f
    with ExitStack() as ph:
        ob = sb(ph, "h_ob", [128, 512], BF16)
        S.op(POOL, lambda e: e.memset(ob[:, :], 1.0), [], [ob])
        HM4 = sb(ph, "HM4", [128, 4, 128], BF16)
        S.op(POOL, lambda e: e.affine_select(out=HM4[:, :, :], in_=ob[:, :].rearrange("p (h q) -> p h q", h=4),
                                             pattern=[[0, 4], [1, 128]], compare_op=ALU.is_ge, fill=0.0, base=0,
                                             channel_multiplier=-1), [ob], [HM4])
        S.op(POOL, lambda e: e.memset(HM4[0:64, :, 64:128], 0.0), [], [HM4])
        Mrev = sb(ph, "Mrev", [128, 128], F32)
        S.op(POOL, lambda e: e.affine_select(out=Mrev[:, :], in_=ones_f[:, 0:128], pattern=[[-1, 128]],
                                             compare_op=ALU.is_gt, fill=0.0, base=0, channel_multiplier=1), [ones_f], [Mrev])
        S.op(POOL, lambda e: e.memset(Mrev[64:128, 0:64], 0.0), [], [Mrev])
        Mmid = sb(ph, "Mmid", [128, 128], F32)
        Bm = sb(ph, "Bm", [128, 128], F32)
        S.op(POOL, lambda e: e.affine_select(out=Mmid[:, :], in_=ones_f[:, 0:128], pattern=[[1, 128]],
                                             compare_op=ALU.is_ge, fill=0.0, base=0, channel_multiplier=-1), [ones_f], [Mmid])
        S.op(POOL, lambda e: e.memset(Mmid[0:64, 64:128], 0.0), [], [Mmid])
        S.op(POOL, lambda e: e.memset(Bm[:, :], 0.0), [], [Bm])
        S.op(POOL, lambda e: e.memset(Bm[0:32, 0:64], 1.0), [], [Bm])
        S.op(POOL, lambda e: e.memset(Bm[64:96, 64:128], 1.0), [], [Bm])
        S.op(POOL, lambda e: e.tensor_tensor(out=Mmid[:, :], in0=Mmid[:, :], in1=Bm[:, :], op=ALU.subtract), [Mmid, Bm], [Mmid])
        Ecol = sb(ph, "Ecol", [128, 4], F32)
        S.op(POOL, lambda e: e.memset(Ecol[:, :], 0.0), [], [Ecol])
        S.op(POOL, lambda e: e.memset(Ecol[0:32, 0:1], 1.0), [], [Ecol])
        S.op(POOL, lambda e: e.memset(Ecol[0:64, 1:2], 1.0), [], [Ecol])
        S.op(POOL, lambda e: e.memset(Ecol[64:96, 2:3], 1.0), [], [Ecol])
        S.op(POOL, lambda e: e.memset(Ecol[64:128, 3:4], 1.0), [], [Ecol])

        def lbcalc(src, shape3, name):
            P_, X = shape3[0], shape3[2]
            ex = sb(ph, name + "_ex", shape3, F32)
            S.op(ACT, lambda e: e.activation(out=ex[:, :, :], in_=src[:, :, :], func=AF.Exp), [src], [ex])
            ssum = sb(ph, name + "_ss", [P_, X], F32)
            S.op(DVE, lambda e: e.tensor_tensor(out=ssum[:, :], in0=ex[:, 0, :], in1=ex[:, 1, :], op=ALU.add), [ex], [ssum])
            S.op(DVE, lambda e: e.tensor_tensor(out=ssum[:, :], in0=ssum[:, :], in1=ex[:, 2, :], op=ALU.add), [ex, ssum], [ssum])
            S.op(DVE, lambda e: e.tensor_tensor(out=ssum[:, :], in0=ssum[:, :], in1=ex[:, 3, :], op=ALU.add), [ex, ssum], [ssum])
            S.op(DVE, lambda e: e.reciprocal(out=ssum[:, :], in_=ssum[:, :]), [ssum], [ssum])
            lb = sb(ph, name + "_lb", [P_, X], F32)
            S.op(DVE, lambda e: e.memset(lb[:, :], 0.0), [], [lb])
            for d in range(1, L + 1):
                S.op(DVE, lambda e, d=d: e.tensor_tensor(out=lb[:, :], in0=lb[:, :], in1=ex[:, d, :], op=ALU.add), [lb, ex], [lb])
            S.op(DVE, lambda e: e.tensor_tensor(out=lb[:, :], in0=lb[:, :], in1=ssum[:, :], op=ALU.mult), [lb, ssum], [lb])
            oml = sb(ph, name + "_oml", [P_, X], F32)
            S.op(DVE, lambda e: e.tensor_scalar(out=oml[:, :], in0=lb[:, :], scalar1=-1.0, scalar2=1.0, op0=ALU.mult,
                                                op1=ALU.add), [lb], [oml])
            return lb, oml
        lsrc_bc = sb(ph, "lsrc_bc", [128, 4, 256], F32)
        S.dma(lsrc_bc, lsrc_bc[:, :, :], DT("lbl"), C.Wd["lb_logits"].partition_broadcast(128))
        lb_bc, oml_bc = lbcalc(lsrc_bc, [128, 4, 256], "lbb")
        lsrc_fm = sb(ph, "lsrc_fm", [64, 4, 4], F32)
        lraw = sb(ph, "lraw", [4, 256], F32)
        S.dma(lraw, lraw[:, :], DT("lbl"), C.Wd["lb_logits"])
        psl = PS()
        for h in range(4):
            S.op(PE, lambda e, h=h: e.transpose(out=psl[0:64, h * 4:(h + 1) * 4], in_=lraw[:, h * 64:(h + 1) * 64],
                                                identity=ident_f[0:4, 0:4]), [lraw, ident_f], [psl])
        copy(DVE, lsrc_fm, lsrc_fm[:, :, :].rearrange("p d h -> p h d"), psl, psl[0:64, 0:16].rearrange("p (h d) -> p h d", h=4))
        lb_fm, oml_fm = lbcalc(lsrc_fm, [64, 4, 4], "lbf")
        ng_bc = bc_load(C, ph, "ng_bc", C.Wd["hgrn_norm_g"][L], 64)

        htm = [sb(ph, "htm%d" % i, [128, 768], F32) for i in range(2)]
        hq = [sb(ph, "hq%d" % i, [64, 4, 128], F32) for i in range(2)]
        hf = [sb(ph, "hf%d" % i, [64, 4, 128], F32) for i in range(2)]
        sig = sb(ph, "hsig", [128, 256], F32)
        ff = sb(ph, "hff", [128, 256], F32)
        logf = [sb(ph, "hlogf%d" % i, [128, 256], F32) for i in range(2)]
        kk = sb(ph, "hkk", [128, 256], F32)
        erev = sb(ph, "herev", [128, 256], F32)
        khat = [sb(ph, "hkhat%d" % i, [128, 256], BF16) for i in range(2)]
        vb = [sb(ph, "hvb%d" % i, [128, 256], BF16) for i in range(2)]
        sigT = sb(ph, "hsigT", [64, 4, 128], F32)
        kkT = sb(ph, "hkkT", [64, 4, 128], F32)
        eq = sb(ph, "heq", [64, 4, 128], F32)
        ek = sb(ph, "hek", [64, 4, 128], F32)
        qz0 = [sb(ph, "hqz0_%d" % i, [64, 4, 128], BF16) for i in range(2)]
        qz1 = [sb(ph, "hqz1_%d" % i, [64, 4, 128], BF16) for i in range(2)]
        for i in range(2):
            S.op(POOL, lambda e, i=i: e.memset(qz0[i][:, :, :], 0.0), [], [qz0[i]])
            S.op(POOL, lambda e, i=i: e.memset(qz1[i][:, :, :], 0.0), [], [qz1[i]])
        ktl = [sb(ph, "hktl%d" % i, [64, 4, 128], BF16) for i in range(2)]
        em = [sb(ph, "hem%d" % i, [64, 4, 4], F32) for i in range(2)]
        sT = [sb(ph, "hsT%d" % i, [128, 4, 128], BF16) for i in range(2)]
        state = [sb(ph, "hstate%d" % i, [64, 4, 64], F32) for i in range(2)]
        S.op(DVE, lambda e: e.memset(state[0][:, :, :], 0.0), [], [state[0]])
        stmp = sb(ph, "hstmp", [64, 4, 64], F32)
        ss0 = [sb(ph, "hss0_%d" % i, [64, 4, 64], BF16) for i in range(2)]
        ss1 = [sb(ph, "hss1_%d" % i, [64, 4, 64], BF16) for i in range(2)]
        o = sb(ph, "ho", [128, 4, 64], F32)
        osq = sb(ph, "hosq", [128, 4, 64], F32)
        rs = sb(ph, "hrs", [128, 4], F32)
        sgl = sb(ph, "hsgl", [128, 256], F32)
        yb = [sb(ph, "hy%d" % i, [128, 256], BF16) for i in range(2)]

        lbf_b = lb_fm[:, L, :] if False else None
        for tt in range(NT):
            b = tt % 2
            rows = slice(tt * 128, (tt + 1) * 128)
            cols = slice(tt * 128, (tt + 1) * 128)
            S.dma(htm[b], htm[b][:, :], DT("Htm", tt), scr["Htm_d"][rows, 0:768])
            S.dma(hq[b], hq[b][:, :, :], DT("HQ", tt // 4), scr["HQ_d"][:, :, cols].rearrange("j p t -> p j t"))
            S.dma(hf[b], hf[b][:, :, :], DT("HF", tt // 4), scr["HF_d"][:, :, cols].rearrange("j p t -> p j t"))
            H = htm[b]
            if DBGD < 2:
                continue
            S.op(ACT, lambda e: e.activation(out=sig[:, :], in_=H[:, 0:256], func=AF.Sigmoid), [H], [sig])
            S.op(DVE, lambda e: e.tensor_tensor(out=ff[:, :], in0=sig[:, :], in1=oml_bc[:, :], op=ALU.mult), [sig, oml_bc], [ff])
            S.op(POOL, lambda e: e.tensor_tensor(out=ff[:, :], in0=ff[:, :], in1=lb_bc[:, :], op=ALU.add), [ff, lb_bc], [ff])
            lf = logf[b]
            S.op(ACT, lambda e: e.activation(out=lf[:, :], in_=ff[:, :], func=AF.Ln), [ff], [lf])
            S.op(DVE, lambda e: e.tensor_scalar(out=kk[:, :], in0=ff[:, :], scalar1=-1.0, scalar2=1.0, op0=ALU.mult,
                                                op1=ALU.add), [ff], [kk])
            psA = PS()
            S.op(PE, lambda e: e.matmul(psA[:, 0:256], lhsT=Mrev[:, :], rhs=lf[:, :], start=True, stop=True), [Mrev, lf], [psA])
            S.op(ACT, lambda e: e.activation(out=erev[:, :], in_=psA[:, 0:256], func=AF.Exp), [psA], [erev])
            S.op(DVE, lambda e: e.tensor_tensor(out=khat[b][:, :], in0=kk[:, :], in1=erev[:, :], op=ALU.mult), [kk, erev], [khat[b]])
            copy(POOL, vb[b], vb[b][:, :], H, H[:, 256:512])
            if DBGD < 3:
                continue
            S.op(ACT, lambda e: e.activation(out=sigT[:, :, :], in_=hf[b][:, :, :], func=AF.Sigmoid), [hf[b]], [sigT])
            S.op(DVE, lambda e: e.tensor_tensor(out=sigT[:, :, :], in0=sigT[:, :, :],
                                                in1=oml_fm[:, :].unsqueeze(2).to_broadcast([64, 4, 128]), op=ALU.mult),
                 [sigT, oml_fm], [sigT])
            S.op(DVE, lambda e: e.tensor_tensor(out=sigT[:, :, :], in0=sigT[:, :, :],
                                                 in1=lb_fm[:, :].unsqueeze(2).to_broadcast([64, 4, 128]), op=ALU.add),
                 [sigT, lb_fm], [sigT])
            S.op(DVE, lambda e: e.tensor_scalar(out=kkT[:, :, :], in0=sigT[:, :, :], scalar1=-1.0, scalar2=1.0, op0=ALU.mult,
                                                op1=ALU.add), [sigT], [kkT])
            psB = PS()
            for h in range(4):
                S.op(PE, lambda e, h=h: e.matmul(psB[0:64, h * 128:(h + 1) * 128], lhsT=lf[:, h * 64:(h + 1) * 64], rhs=Mmid[:, :],
                                                 start=True, stop=True), [lf, Mmid], [psB])
            psB3 = psB[0:64, :].rearrange("p (h t) -> p h t", h=4)
            S.op(ACT, lambda e: e.activation(out=eq[:, :, :], in_=psB3, func=AF.Exp), [psB], [eq])
            S.op(ACT, lambda e: e.activation(out=ek[:, :, :], in_=psB3, func=AF.Exp, scale=-1.0), [psB], [ek])
            S.op(DVE, lambda e: e.tensor_tensor(out=qz0[b][:, :, 0:64], in0=hq[b][:, :, 0:64], in1=eq[:, :, 0:64], op=ALU.mult),
                 [hq[b], eq], [qz0[b]])
            S.op(POOL, lambda e: e.tensor_tensor(out=qz1[b][:, :, 64:128], in0=hq[b][:, :, 64:128], in1=eq[:, :, 64:128],
                                                 op=ALU.mult), [hq[b], eq], [qz1[b]])
            S.op(DVE, lambda e: e.tensor_tensor(out=ktl[b][:, :, :], in0=kkT[:, :, :], in1=ek[:, :, :], op=ALU.mult),
                 [kkT, ek], [ktl[b]])
            psC = PS()
            for h in range(4):
                S.op(PE, lambda e, h=h: e.matmul(psC[0:64, h * 4:(h + 1) * 4], lhsT=lf[:, h * 64:(h + 1) * 64], rhs=Ecol[:, :],
                                                 start=True, stop=True), [lf, Ecol], [psC])
            S.op(ACT, lambda e: e.activation(out=em[b][:, :, :], in_=psC[0:64, 0:16].rearrange("p (h c) -> p h c", h=4),
                                             func=AF.Exp), [psC], [em[b]])
            if DBGD < 4:
                continue
            psS = PS()
            for h in range(4):
                S.op(PE, lambda e, h=h: e.matmul(psS[:, h * 128:(h + 1) * 128], lhsT=ktl[b][:, h, :], rhs=qz0[b][:, h, :],
                                                 start=True, stop=False), [ktl[b], qz0[b]], [psS])
                S.op(PE, lambda e, h=h: e.matmul(psS[:, h * 128:(h + 1) * 128], lhsT=ktl[b][:, h, :], rhs=qz1[b][:, h, :],
                                                 start=False, stop=True), [ktl[b], qz1[b]], [psS])
            S.op(DVE, lambda e: e.tensor_tensor(out=sT[b][:, :, :], in0=psS[:, :].rearrange("p (h t) -> p h t", h=4),
                                                in1=HM4[:, :, :], op=ALU.mult), [psS, HM4], [sT[b]])
            if DBGD < 5:
                continue
            psKc = [PS(), PS()]
            for c in range(2):
                for h in range(4):
                    S.op(PE, lambda e, c=c, h=h: e.matmul(psKc[c][0:64, h * 64:(h + 1) * 64],
                                                          lhsT=khat[b][c * 64:(c + 1) * 64, h * 64:(h + 1) * 64],
                                                          rhs=vb[b][c * 64:(c + 1) * 64, h * 64:(h + 1) * 64],
                                                          start=True, stop=True), [khat[b], vb[b]], [psKc[c]])
            psK4 = [psKc[c][0:64, 0:256].rearrange("p (h v) -> p h v", h=4) for c in range(2)]
            st_in = state[tt % 2]
            st_out = state[(tt + 1) % 2]
            E = em[b]

            def ebc(i):
                return E[:, :, i:i + 1].to_broadcast([64, 4, 64])
            S.op(DVE, lambda e: e.tensor_tensor(out=ss0[b][:, :, :], in0=st_in[:, :, :], in1=ebc(0), op=ALU.mult), [st_in, E], [ss0[b]])
            S.op(DVE, lambda e: e.tensor_tensor(out=stmp[:, :, :], in0=st_in[:, :, :], in1=ebc(1), op=ALU.mult), [st_in, E], [stmp])
            S.op(DVE, lambda e: e.tensor_tensor(out=stmp[:, :, :], in0=stmp[:, :, :], in1=psK4[0], op=ALU.add),
                 [stmp, psKc[0]], [stmp])
            S.op(DVE, lambda e: e.tensor_tensor(out=ss1[b][:, :, :], in0=stmp[:, :, :], in1=ebc(2), op=ALU.mult), [stmp, E], [ss1[b]])
            S.op(DVE, lambda e: e.tensor_tensor(out=st_out[:, :, :], in0=stmp[:, :, :], in1=ebc(3), op=ALU.mult), [stmp, E], [st_out])
            S.op(DVE, lambda e: e.tensor_tensor(out=st_out[:, :, :], in0=st_out[:, :, :], in1=psK4[1], op=ALU.add),
                 [st_out, psKc[1]], [st_out])
            if DBGD < 6:
                continue
            psO = PS()
            for h in range(4):
                osl = psO[:, h * 64:(h + 1) * 64]
                S.op(PE, lambda e, h=h, osl=osl: e.matmul(osl, lhsT=sT[b][:, h, :], rhs=vb[b][:, h * 64:(h + 1) * 64],
                                                          start=True, stop=False), [sT[b], vb[b]], [psO])
                S.op(PE, lambda e, h=h, osl=osl: e.matmul(osl, lhsT=qz0[b][:, h, :], rhs=ss0[b][:, h, :],
                                                          start=False, stop=False), [qz0[b], ss0[b]], [psO])
                S.op(PE, lambda e, h=h, osl=osl: e.matmul(osl, lhsT=qz1[b][:, h, :], rhs=ss1[b][:, h, :],
                                                          start=False, stop=True), [qz1[b], ss1[b]], [psO])
            copy(ACT, o, o[:, :, :], psO, psO[:, 0:256].rearrange("p (h v) -> p h v", h=4))
            S.op(POOL, lambda e: e.tensor_tensor(out=osq[:, :, :], in0=o[:, :, :], in1=o[:, :, :], op=ALU.mult), [o], [osq])
            S.op(DVE, lambda e: e.tensor_reduce(out=rs[:, :], in_=osq[:, :, :], axis=mybir.AxisListType.X, op=ALU.add), [osq], [rs])
            S.op(DVE, lambda e: e.tensor_scalar(out=rs[:, :], in0=rs[:, :], scalar1=1.0 / 64.0, scalar2=LN_EPS, op0=ALU.mult,
                                                op1=ALU.add), [rs], [rs])
            S.op(ACT, lambda e: e.activation(out=rs[:, :], in_=rs[:, :], func=AF.Sqrt), [rs], [rs])
            S.op(DVE, lambda e: e.reciprocal(out=rs[:, :], in_=rs[:, :]), [rs], [rs])
            S.op(DVE, lambda e: e.tensor_tensor(out=o[:, :, :], in0=o[:, :, :], in1=rs[:, :].unsqueeze(2).to_broadcast([128, 4, 64]),
                                                op=ALU.mult), [o, rs], [o])
            S.op(DVE, lambda e: e.tensor_tensor(out=o[:, :, :], in0=o[:, :, :],
                                                 in1=ng_bc[:, :].unsqueeze(1).to_broadcast([128, 4, 64]), op=ALU.mult), [o, ng_bc], [o])
            S.op(ACT, lambda e: e.activation(out=sgl[:, :], in_=H[:, 512:768], func=AF.Silu), [H], [sgl])
            S.op(DVE, lambda e: e.tensor_tensor(out=yb[b][:, :], in0=o[:, :, :].rearrange("p h v -> p (h v)"), in1=sgl[:, :],
                                                op=ALU.mult), [o, sgl], [yb[b]])
            S.dma(DT("mixC", tt), scr["mix_d"][rows, 768:1024], yb[b], yb[b][:, :], queue=POOL)


def phase_ffn1(C, L, h_src):
    nc, S, sb, PS, copy, DT = C.nc, C.S, C.sb, C.PS, C.copy, C.DT
    scr = C.scr
    with ExitStack() as ph:
        wout = C.load_bf16(ph, "wout", C.Wd["w_out"][L], 8, D, stage_cols=512)
        wup = C.load_bf16(ph, "wup", C.Wd["w_up"][L], 8, 2 * D_FF, stage_cols=512)
        fw = small_T(C, ph, "fw", C.Wd["ffnp"][L], 4, 2 * D_FF)
        g_bc = bc_load(C, ph, "ln1g", C.Wd["ln1_g"][L], D)
        b_bc = bc_load(C, ph, "ln1b", C.Wd["ln1_b"][L], D)
        mixt = [sb(ph, "mixt%d" % i, [128, D], BF16) for i in range(2)]
        mixT = [sb(ph, "mixT%d" % i, [128, 8, 128], BF16) for i in range(2)]
        hin = [sb(ph, "hin%d" % i, [128, D], F32) for i in range(2)]
        z = [sb(ph, "z1_%d" % i, [128, D], F32) for i in range(2)]
        h1 = [sb(ph, "h1_%d" % i, [128, D], F32) for i in range(2)]
        h1b = [sb(ph, "h1b_%d" % i, [128, D], BF16) for i in range(2)]
        hT1 = [sb(ph, "hT1_%d" % i, [128, 8, 512], BF16) for i in range(2)]
        st6 = sb(ph, "st6", [128, 24], F32)
        mv = sb(ph, "mv", [128, 2], F32)
        rstd = sb(ph, "rstd", [128, 1], F32)
        usb = [sb(ph, "usb%d" % i, [128, 514], F32) for i in range(4)]
        yv = [sb(ph, "yv%d" % i, [128, 512], F32) for i in range(3)]
        yg = [sb(ph, "yg%d" % i, [128, 512], F32) for i in range(3)]
        gj = [sb(ph, "gj%d" % i, [128, 512], BF16) for i in range(3)]
        carry = [sb(ph, "carry%d" % i, [128, 44, 2], F32) for i in range(2)]
        S.op(POOL, lambda e: e.memset(carry[0][:, :, :], 0.0), [], [carry[0]])
        def ln_tile(c, t):
            cb = c % 2
            tt = c * 4 + t
            b = tt % 2
            rows = slice(tt * 128, (tt + 1) * 128)
            S.dma(mixt[b], mixt[b][:, :], DT("mix_all"), scr["mix_d"][rows, :])
            S.dma(hin[b], hin[b][:, :], DT("h", tt), h_src[rows, :])
            C.transposes_bf(mixT[b], mixT[b][:, :, :], mixt[b], lambda i, b=b: mixt[b][:, i * 128:(i + 1) * 128], 8)
            pss = []
            for half in range(2):
                ps = PS()
                pss.append(ps)
                for kt in range(8):
                    S.op(PE, lambda e, kt=kt, ps=ps, half=half, b=b: e.matmul(
                        ps[:, :], lhsT=mixT[b][:, kt, :], rhs=wout[:, kt, half * 512:(half + 1) * 512],
                        start=(kt == 0), stop=(kt == 7)), [mixT[b], wout], [ps])
            for half in range(2):
                S.op(DVE, lambda e, half=half, b=b: e.scalar_tensor_tensor(
                    out=z[b][:, half * 512:(half + 1) * 512], in0=hin[b][:, half * 512:(half + 1) * 512], scalar=ALPHA,
                    in1=pss[half][:, :], op0=ALU.mult, op1=ALU.add), [hin[b], pss[half]], [z[b]])
            C.layernorm((st6, mv, rstd), z[b], D, g_bc, b_bc, h1[b], h1[b][:, :])
            S.dma(DT("h1", tt), scr["h1_d"][rows, :], h1[b], h1[b][:, :], queue=POOL)
            copy(ACT, h1b[b], h1b[b][:, :], h1[b], h1[b][:, :])
            C.transposes_bf(hT1[cb], hT1[cb][:, :, t * 128:(t + 1) * 128], h1b[b],
                            lambda i, b=b: h1b[b][:, i * 128:(i + 1) * 128], 8)

        def store_hT1(c):
            cb = c % 2
            sl = slice(c * 512, (c + 1) * 512)
            S.dma(DT("hT1", c), scr["hT1_d"][:, :, sl].rearrange("j p t -> p j t"), hT1[cb], hT1[cb][:, :, :], queue=POOL)

        tails = []

        def up_j(c, j):
            cb = c % 2
            sl = slice(c * 512, (c + 1) * 512)
            cin, cout = carry[c % 2], carry[(c + 1) % 2]
            ys = []
            for vg in range(2):
                col = vg * D_FF + j * 128
                jj = vg * NFF + j
                ps = PS()
                for kt in range(8):
                    S.op(PE, lambda e, kt=kt, ps=ps, col=col: e.matmul(ps[:, :], lhsT=wup[:, kt, col:col + 128],
                                                                        rhs=hT1[cb][:, kt, :], start=(kt == 0), stop=(kt == 7)),
                         [wup, hT1[cb]], [ps])
                u = usb[(j * 2 + vg) % 4]
                copy(ACT, u, u[:, 2:514], ps, ps[:, :])
                copy(ACT, u, u[:, 0:2], cin, cin[:, jj, :])
                copy(ACT, cout, cout[:, jj, :], u, u[:, 512:514])
                y = (yv if vg == 0 else yg)[j % 3]
                S.op(POOL, lambda e, u=u, y=y, jj=jj: e.tensor_scalar(out=y[:, :], in0=u[:, 2:514], scalar1=fw[:, jj, 2:3],
                                                                     scalar2=fw[:, jj, 3:4], op0=ALU.mult, op1=ALU.add),
                     [u, fw], [y])
                S.op(DVE, lambda e, u=u, y=y, jj=jj: e.scalar_tensor_tensor(out=y[:, :], in0=u[:, 1:513], scalar=fw[:, jj, 1:2],
                                                                             in1=y[:, :], op0=ALU.mult, op1=ALU.add),
                     [u, fw, y], [y])
                S.op(DVE, lambda e, u=u, y=y, jj=jj: e.scalar_tensor_tensor(out=y[:, :], in0=u[:, 0:512], scalar=fw[:, jj, 0:1],
                                                                            in1=y[:, :], op0=ALU.mult, op1=ALU.add),
                     [u, fw, y], [y])
                ys.append(y)
            def tail(ys=ys, j=j, c=c, sl=sl):
                S.op(ACT, lambda e, y=ys[1]: e.activation(out=y[:, :], in_=y[:, :], func=AF.Silu), [ys[1]], [ys[1]])
                gjt = gj[j % 3]
                S.op(POOL, lambda e, gjt=gjt, ys=ys: e.tensor_tensor(out=gjt[:, :], in0=ys[0][:, :], in1=ys[1][:, :], op=ALU.mult),
                     [ys[0], ys[1]], [gjt])
                S.dma(DT("g", c), scr["g_d"][j, :, sl], gjt, gjt[:, :], queue=POOL)
            if tails:
                tails.pop(0)()
            tails.append(tail)

        for t in range(4):
            ln_tile(0, t)
        store_hT1(0)
        for c in range(NCH):
            for j in range(NFF if DBGE >= 2 else 0):
                up_j(c, j)
                if c + 1 < NCH and j in (2, 7, 12, 17):
                    ln_tile(c + 1, (j - 2) // 5)
            while tails:
                tails.pop(0)()
            if c + 1 < NCH:
                if DBGE < 2:
                    for t in range(4):
                        ln_tile(c + 1, t)
                store_hT1(c + 1)


def phase_ffn2(C, L, h_dst):
    nc, S, sb, PS, copy, DT = C.nc, C.S, C.sb, C.PS, C.copy, C.DT
    scr = C.scr
    with ExitStack() as ph:
        wdn = C.load_bf16(ph, "wdn", C.Wd["w_down"][L], NFF, D, stage_cols=256)
        wg = C.load_bf16(ph, "wg", C.Wd["w_ple_gate"][L], 8, D, stage_cols=512)
        wp = C.load_bf16(ph, "wp", C.Wd["w_ple_proj"][L], 2, D, stage_cols=512)
        g_bc = bc_load(C, ph, "ln2g", C.Wd["ln2_g"][L], D)
        b_bc = bc_load(C, ph, "ln2b", C.Wd["ln2_b"][L], D)
        gT = [sb(ph, "gTl%d" % i, [128, NFF, 512], BF16) for i in range(2)]
        hT1 = [sb(ph, "hT1l%d" % i, [128, 8, 512], BF16) for i in range(2)]
        h1 = [sb(ph, "h1l%d" % i, [128, D], F32) for i in range(2)]
        pt = [sb(ph, "pt%d" % i, [128, 256], F32) for i in range(2)]
        ptb = [sb(ph, "ptb%d" % i, [128, 256], BF16) for i in range(2)]
        pT = [sb(ph, "ppT%d" % i, [128, 2, 128], BF16) for i in range(2)]
        sgt = [sb(ph, "sgt%d" % i, [128, D], F32) for i in range(2)]
        z = [sb(ph, "z2_%d" % i, [128, D], F32) for i in range(2)]
        h2 = [sb(ph, "h2_%d" % i, [128, D], F32) for i in range(2)]
        st6 = sb(ph, "st6b", [128, 24], F32)
        mv = sb(ph, "mvb", [128, 2], F32)
        rstd = sb(ph, "rstdb", [128, 1], F32)
        def prep(tt):
            b = tt % 2
            rows = slice(tt * 128, (tt + 1) * 128)
            S.dma(h1[b], h1[b][:, :], DT("h1", tt), scr["h1_d"][rows, :])
            S.dma(pt[b], pt[b][:, :], DT("p", tt), C.p_in[L, rows, :])
            copy(DVE, ptb[b], ptb[b][:, :], pt[b], pt[b][:, :])
            C.transposes_bf(pT[b], pT[b][:, :, :], ptb[b], lambda i, b=b: ptb[b][:, i * 128:(i + 1) * 128], 2)

        for c in range(NCH):
            cb = c % 2
            sl = slice(c * 512, (c + 1) * 512)
            S.dma(gT[cb], gT[cb][:, :, :], DT("g", c), scr["g_d"][:, :, sl].rearrange("j p t -> p j t"))
            S.dma(hT1[cb], hT1[cb][:, :, :], DT("hT1", c), scr["hT1_d"][:, :, sl].rearrange("j p t -> p j t"))
            for t in range(4):
                tt = c * 4 + t
                b = tt % 2
                rows = slice(tt * 128, (tt + 1) * 128)
                tok = slice(t * 128, (t + 1) * 128)
                if tt == 0:
                    prep(0)
                if tt + 1 < NT:
                    prep(tt + 1)
                psf, psg, psp = [], [], []
                for half in range(2):
                    hs = slice(half * 512, (half + 1) * 512)
                    ps = PS()
                    psg.append(ps)
                    for kt in range(8):
                        S.op(PE, lambda e, kt=kt, ps=ps, hs=hs: e.matmul(ps[:, :], lhsT=hT1[cb][:, kt, tok], rhs=wg[:, kt, hs],
                                                                          start=(kt == 0), stop=(kt == 7)), [hT1[cb], wg], [ps])
                    S.op(ACT, lambda e, ps=ps, hs=hs, b=b: e.activation(out=sgt[b][:, hs], in_=ps[:, :], func=AF.Sigmoid),
                         [ps], [sgt[b]])
                    ps = PS()
                    psp.append(ps)
                    for kt in range(2):
                        S.op(PE, lambda e, kt=kt, ps=ps, hs=hs, b=b: e.matmul(ps[:, :], lhsT=pT[b][:, kt, :], rhs=wp[:, kt, hs],
                                                                               start=(kt == 0), stop=(kt == 1)), [pT[b], wp], [ps])
                    S.op(DVE, lambda e, ps=ps, hs=hs, b=b: e.tensor_tensor(out=sgt[b][:, hs], in0=sgt[b][:, hs], in1=ps[:, :],
                                                                           op=ALU.mult), [sgt[b], ps], [sgt[b]])
                    ps = PS()
                    psf.append(ps)
                    for j in range(NFF):
                        S.op(PE, lambda e, j=j, ps=ps, hs=hs: e.matmul(ps[:, :], lhsT=gT[cb][:, j, tok], rhs=wdn[:, j, hs],
                                                                        start=(j == 0), stop=(j == NFF - 1)), [gT[cb], wdn], [ps])
                    S.op(DVE, lambda e, ps=ps, hs=hs, b=b: e.scalar_tensor_tensor(out=z[b][:, hs], in0=h1[b][:, hs], scalar=ALPHA,
                                                                                  in1=ps[:, :], op0=ALU.mult, op1=ALU.add),
                         [h1[b], ps], [z[b]])
                    S.op(POOL, lambda e, hs=hs, b=b: e.tensor_tensor(out=z[b][:, hs], in0=z[b][:, hs], in1=sgt[b][:, hs], op=ALU.add),
                         [z[b], sgt[b]], [z[b]])
                C.layernorm((st6, mv, rstd), z[b], D, g_bc, b_bc, h2[b], h2[b][:, :])
                S.dma(DT("h", tt), h_dst[rows, :], h2[b], h2[b][:, :], queue=POOL)


_NC_CACHE = {}


def _prep_inputs(inp, depth=DEPTH):
    f = lambda a: np.ascontiguousarray(np.asarray(a, dtype=np.float32))
    shared = {}
    for k in ("w_in", "conv_ln_g", "conv_ln_b", "cmp_pe_k", "cmp_pe_v", "cmp_w1_k", "cmp_w2_k", "cmp_w1_v", "cmp_w2_v",
              "lb_logits", "hgrn_norm_g", "w_out", "ln1_g", "ln1_b", "w_up", "w_down", "w_ple_gate", "w_ple_proj",
              "ln2_g", "ln2_b"):
        shared[k] = f(inp[k])
    shared["convp"] = f(np.concatenate([np.asarray(inp["conv_w"]), np.asarray(inp["conv_b"])[:, None, :]], axis=1))
    shared["ffnp"] = f(np.concatenate([np.asarray(inp["ffn_conv_w"]), np.asarray(inp["ffn_conv_b"])[:, None, :]], axis=1))
    x = np.asarray(inp["x"], dtype=np.float32)
    p = np.asarray(inp["p"], dtype=np.float32)
    maps = []
    for b in range(8):
        m = dict(shared)
        m["x"] = np.ascontiguousarray(x[b])
        m["p"] = np.ascontiguousarray(p[:, b])
        maps.append(m)
    return maps


def kernel(**inputs):
    if "nc" not in _NC_CACHE:
        _NC_CACHE["nc"] = build()
    nc = _NC_CACHE["nc"]
    maps = _prep_inputs(inputs)
    res = run_bass_kernel_spmd(nc, maps, core_ids=list(range(8)))
    out = np.stack([np.asarray(r["y"], dtype=np.float32) for r in res.results], axis=0)
    return out
```

```python
import numpy as np
from contextlib import ExitStack
import concourse.bass as bass
import concourse.mybir as mybir
from concourse.bass_utils import run_bass_kernel_spmd

F32 = mybir.dt.float32
BF16 = mybir.dt.bfloat16
AF = mybir.ActivationFunctionType
ALU = mybir.AluOpType

PE, ACT, DVE, POOL, SP = "tensor", "scalar", "vector", "gpsimd", "sync"
COMPUTE = (PE, ACT, DVE, POOL)
ALLENG = (PE, ACT, DVE, POOL, SP)
INORDER_SAFE = (PE, ACT, DVE)

S_LEN = 4096
D = 1024
DEPTH = 4
NT = S_LEN // 128
NCH = S_LEN // 512
IN_COLS = 2840
D_FF = 2816
NFF = D_FF // 128
ALPHA = (2 * DEPTH) ** 0.25
LN_EPS = 1e-5
NEG = -30000.0
DBGA = 9
STORES_ON_POOL = True
DBGE = 9
DBGD = 9
ACT_FENCE = False
DBGX = 0


class T:
    __slots__ = ("ap", "w", "r", "psum")

    def __init__(self, ap, psum=False):
        self.ap = ap
        self.w = {}
        self.r = {}
        self.psum = psum

    def __getitem__(self, k):
        return self.ap[k]


class Sched:
    def __init__(self, nc, stack, n_dma=40, marked=None):
        self.nc = nc
        self.marked = marked
        self.waited = set()
        self.mrank = {e: 0 for e in COMPUTE}
        self.rank_of = {}
        self.act_scratch = None
        self.cnt = {e: 0 for e in COMPUTE}
        self.known = {e: {} for e in ALLENG}
        self.n_dma = n_dma
        self.dma_cnt = [0] * n_dma
        self.dma_next = 0
        self.sems = {}
        for e in COMPUTE:
            self.sems[e] = stack.enter_context(nc.semaphore("s_" + e))
        for k in range(n_dma):
            self.sems[("dma", k)] = stack.enter_context(nc.semaphore("s_dma%d" % k))
        self.n_ins = 0

    def _deps(self, eng, reads, writes):
        deps = {}
        for t in reads:
            for k, v in t.w.items():
                if deps.get(k, 0) < v:
                    deps[k] = v
            if t.psum:
                for k, v in t.r.items():
                    if k != eng and deps.get(k, 0) < v:
                        deps[k] = v
        for t in writes:
            for src in (t.w, t.r):
                for k, v in src.items():
                    if k == eng and eng in INORDER_SAFE:
                        continue
                    if deps.get(k, 0) < v:
                        deps[k] = v
        kn = self.known[eng]
        out = []
        for k, v in deps.items():
            if kn.get(k, 0) >= v:
                continue
            kn[k] = v
            out.append((k, v))
        return out

    def _commit(self, tok, reads, writes):
        k, v = tok
        for t in reads:
            if t.r.get(k, 0) < v:
                t.r[k] = v
        for t in writes:
            t.w = {k: v}
            t.r = {}

    def _wait(self, e, k, v):
        if isinstance(k, tuple):
            e.wait_ge(self.sems[k], v)
            return
        self.waited.add((k, v))
        val = v if self.marked is None else self.rank_of[(k, v)]
        e.wait_ge(self.sems[k], val)

    def op(self, eng, fn, reads=(), writes=()):
        e = getattr(self.nc, eng)
        deps = self._deps(eng, reads, writes)
        if ACT_FENCE and eng == ACT and self.act_scratch is not None and any(isinstance(k, tuple) for k, v in deps):
            dma_deps = [(k, v) for k, v in deps if isinstance(k, tuple)]
            deps = [(k, v) for k, v in deps if not isinstance(k, tuple)]
            dv = getattr(self.nc, DVE)
            for k, v in dma_deps:
                if self.known[DVE].get(k, 0) < v:
                    self.known[DVE][k] = v
                    self._wait(dv, k, v)
            sc = self.act_scratch
            fins = dv.tensor_copy(out=sc[:, 0:4], in_=sc[:, 8:12])
            self.cnt[DVE] += 1
            fseq = self.cnt[DVE]
            if self.marked is None or (DVE, fseq) in self.marked:
                self.mrank[DVE] += 1
                self.rank_of[(DVE, fseq)] = self.mrank[DVE]
                fins.then_inc(self.sems[DVE], 1)
            if self.known[ACT].get(DVE, 0) < fseq:
                self.known[ACT][DVE] = fseq
                deps = [(k, v) for k, v in deps if k != DVE] + [(DVE, fseq)]
        for k, v in deps:
            self._wait(e, k, v)
        ins = fn(e)
        self.cnt[eng] += 1
        seq = self.cnt[eng]
        if self.marked is None or (eng, seq) in self.marked:
            self.mrank[eng] += 1
            self.rank_of[(eng, seq)] = self.mrank[eng]
            ins.then_inc(self.sems[eng], 1)
        self._commit((eng, seq), reads, writes)
        self.n_ins += 1

    def dma(self, out_t, out_ap, in_t, in_ap, queue=SP, **kw):
        if not STORES_ON_POOL:
            queue = SP
        e = getattr(self.nc, queue)
        k = self.dma_next
        self.dma_next = (k + 1) % self.n_dma
        key = ("dma", k)
        waits = self._deps(queue, [in_t], [out_t])
        if self.known[queue].get(key, 0) < self.dma_cnt[k]:
            self.known[queue][key] = self.dma_cnt[k]
            waits.append((key, self.dma_cnt[k]))
        for kk, v in waits:
            self._wait(e, kk, v)
        self.dma_cnt[k] += 16
        e.dma_start(out=out_ap, in_=in_ap, **kw).then_inc(self.sems[key], 16)
        self._commit((key, self.dma_cnt[k]), [in_t], [out_t])
        self.n_ins += 1

    def barrier(self, engines=ALLENG):
        cur = {e: self.cnt[e] for e in COMPUTE}
        for k in range(self.n_dma):
            cur[("dma", k)] = self.dma_cnt[k]
        for eng in engines:
            e = getattr(self.nc, eng)
            kn = self.known[eng]
            for k, v in cur.items():
                if v > 0 and kn.get(k, 0) < v:
                    kn[k] = v
                    self._wait(e, k, v)


class Ctx:
    pass


def build(depth=DEPTH, debug=False, phases="ABCDEF"):
    waited = _build(depth, debug, phases, None)[1]
    return _build(depth, debug, phases, waited)[0]


def _build(depth, debug, phases, marked):
    nc = bass.Bass("TRN2", target_bir_lowering=False)
    kind_dbg = "ExternalOutput" if debug else "Internal"

    def dram(name, shape, dt, kind="Internal"):
        return nc.dram_tensor(name, list(shape), dt, kind=kind).ap()

    x_in = dram("x", [S_LEN, D], F32, "ExternalInput")
    p_in = dram("p", [DEPTH, S_LEN, 256], F32, "ExternalInput")
    Wd = {}
    wshapes = {
        "w_in": [DEPTH, D, IN_COLS], "convp": [DEPTH, 32, 256], "conv_ln_g": [DEPTH, 256],
        "conv_ln_b": [DEPTH, 256], "cmp_pe_k": [DEPTH, 32, 64], "cmp_pe_v": [DEPTH, 32, 64],
        "cmp_w1_k": [DEPTH, 2048, 128], "cmp_w2_k": [DEPTH, 128, 64], "cmp_w1_v": [DEPTH, 2048, 128],
        "cmp_w2_v": [DEPTH, 128, 64], "lb_logits": [DEPTH, 256], "hgrn_norm_g": [DEPTH, 64],
        "w_out": [DEPTH, D, D], "ln1_g": [DEPTH, D], "ln1_b": [DEPTH, D], "w_up": [DEPTH, D, 2 * D_FF],
        "ffnp": [DEPTH, 4, 2 * D_FF], "w_down": [DEPTH, D_FF, D], "w_ple_gate": [DEPTH, D, D],
        "w_ple_proj": [DEPTH, 256, D], "ln2_g": [DEPTH, D], "ln2_b": [DEPTH, D],
    }
    for k, shp in wshapes.items():
        Wd[k] = dram(k, shp, F32, "ExternalInput")
    y_out = dram("y", [S_LEN, D], F32, "ExternalOutput")

    hres = dram("hres", [S_LEN, D], F32, kind_dbg)
    convT_d = dram("convT_d", [4, 128, S_LEN], BF16)
    QT_d = dram("QT_d", [8, 64, S_LEN], BF16)
    KT_d = dram("KT_d", [8, 64, S_LEN], BF16)
    HQ_d = dram("HQ_d", [4, 64, S_LEN], F32)
    HF_d = dram("HF_d", [4, 64, S_LEN], F32)
    Vtm_d = dram("Vtm_d", [S_LEN, 4, 65], BF16)
    Gtm_d = dram("Gtm_d", [S_LEN, 24], F32)
    Htm_d = dram("Htm_d", [S_LEN, 896], F32)
    mix_d = dram("mix_d", [S_LEN, D], BF16, kind_dbg)
    h1_d = dram("h1_d", [S_LEN, D], F32, kind_dbg)
    hT1_d = dram("hT1_d", [8, 128, S_LEN], BF16)
    g_d = dram("g_d", [NFF, 128, S_LEN], BF16)

    dtiles = {}

    def DT(name, idx=0):
        key = (name, idx)
        if key not in dtiles:
            dtiles[key] = T(None)
        return dtiles[key]

    with ExitStack() as top:
        top.enter_context(nc.allow_low_precision("bf16 matmul operands, fp32 accumulation"))
        top.enter_context(nc.allow_non_contiguous_dma("small parameter loads"))
        S = Sched(nc, top, marked=marked)
        C = Ctx()

        uid = [0]

        def sb(stack, name, shape, dt):
            uid[0] += 1
            return T(stack.enter_context(nc.sbuf_tensor("%s_%d" % (name, uid[0]), list(shape), dt)))

        psum = [T(top.enter_context(nc.psum_tensor("ps%d" % i, [128, 512], F32)), psum=True) for i in range(8)]
        ps_i = [0]

        def PS():
            t = psum[ps_i[0] % 8]
            ps_i[0] += 1
            return t

        rr = [0]

        def evac_eng():
            rr[0] += 1
            return ACT if rr[0] % 2 == 0 else DVE

        def copy(eng, out_t, out_ap, in_t, in_ap):
            if eng == ACT:
                S.op(ACT, lambda e: e.activation(out=out_ap, in_=in_ap, func=AF.Copy), [in_t], [out_t])
            else:
                S.op(eng, lambda e: e.tensor_copy(out=out_ap, in_=in_ap), [in_t], [out_t])

        act_sc = sb(top, "act_sc", [128, 16], F32)
        S.op(POOL, lambda e: e.memset(act_sc[:, :], 0.0), [], [act_sc])
        S.barrier()
        S.act_scratch = act_sc
        ones_f = sb(top, "ones_f", [128, 512], F32)
        S.op(POOL, lambda e: e.memset(ones_f[:, :], 1.0), [], [ones_f])
        zeros_f = sb(top, "zeros_f", [128, 512], F32)
        S.op(POOL, lambda e: e.memset(zeros_f[:, :], 0.0), [], [zeros_f])
        ident_f = sb(top, "ident_f", [128, 128], F32)
        S.op(POOL, lambda e: e.affine_select(out=ident_f[:, :], in_=ones_f[:, 0:128], pattern=[[-1, 128]],
                                             compare_op=ALU.is_equal, fill=0.0, base=0, channel_multiplier=1),
             [ones_f], [ident_f])
        ident_b = sb(top, "ident_b", [128, 128], BF16)
        copy(DVE, ident_b, ident_b[:, :], ident_f, ident_f[:, :])

        def load_bf16(stack, name, dram_ap, kt, ncols, stage_cols=512):
            dst = sb(stack, name, [128, kt, ncols], BF16)
            src = dram_ap.rearrange("(k p) n -> p k n", p=128)
            with ExitStack() as st:
                stg = [sb(st, name + "_stg%d" % i, [128, kt, stage_cols], F32) for i in range(2)]
                i = 0
                for c0 in range(0, ncols, stage_cols):
                    n = min(stage_cols, ncols - c0)
                    s = stg[i % 2]
                    S.dma(s, s[:, :, 0:n], DT(name + "_src"), src[:, :, c0:c0 + n])
                    eng = (DVE, POOL, ACT)[i % 3]
                    copy(eng, dst, dst[:, :, c0:c0 + n], s, s[:, :, 0:n])
                    i += 1
                S.barrier()
            return dst

        def transposes_bf(dst_t, dst_ap, src_t, src_ap_fn, n):
            ps = PS()
            pb = ps.ap.bitcast(BF16)
            for i in range(n):
                S.op(PE, lambda e, i=i: e.transpose(out=pb[:, i * 128:(i + 1) * 128], in_=src_ap_fn(i),
                                                    identity=ident_b[:, :]), [src_t, ident_b], [ps])
            eng = evac_eng()
            copy(eng, dst_t, dst_ap, ps, pb[:, 0:n * 128].rearrange("p (k t) -> p k t", k=n))

        def layernorm(stack_tiles, z, width, g_bc, b_bc, out_t, out_ap):
            st6, mv, rstd = stack_tiles
            nchunk = width // 256 if width > 512 else 1
            cw = width // nchunk
            for i in range(nchunk):
                S.op(DVE, lambda e, i=i: e.bn_stats(out=st6[:, i * 6:(i + 1) * 6], in_=z[:, i * cw:(i + 1) * cw]),
                     [z], [st6])
            S.op(DVE, lambda e: e.bn_aggr(out=mv[:, :], in_=st6[:, 0:nchunk * 6]), [st6], [mv])
            S.op(DVE, lambda e: e.tensor_scalar(out=rstd[:, :], in0=mv[:, 1:2], scalar1=LN_EPS, scalar2=None,
                                                op0=ALU.add), [mv], [rstd])
            S.op(ACT, lambda e: e.activation(out=rstd[:, :], in_=rstd[:, :], func=AF.Sqrt), [rstd], [rstd])
            S.op(DVE, lambda e: e.reciprocal(out=rstd[:, :], in_=rstd[:, :]), [rstd], [rstd])
            S.op(DVE, lambda e: e.tensor_scalar(out=z[:, 0:width], in0=z[:, 0:width], scalar1=mv[:, 0:1],
                                                scalar2=rstd[:, 0:1], op0=ALU.subtract, op1=ALU.mult),
                 [z, mv, rstd], [z])
            S.op(POOL, lambda e: e.tensor_tensor(out=z[:, 0:width], in0=z[:, 0:width], in1=g_bc[:, 0:width],
                                                 op=ALU.mult), [z, g_bc], [z])
            S.op(POOL, lambda e: e.tensor_tensor(out=out_ap, in0=z[:, 0:width], in1=b_bc[:, 0:width],
                                                 op=ALU.add), [z, b_bc], [out_t])

        C.nc, C.S, C.sb, C.PS, C.copy, C.evac_eng, C.DT = nc, S, sb, PS, copy, evac_eng, DT
        C.load_bf16, C.transposes_bf, C.layernorm = load_bf16, transposes_bf, layernorm
        C.ident_f, C.ident_b, C.ones_f, C.zeros_f = ident_f, ident_b, ones_f, zeros_f
        C.Wd, C.x_in, C.p_in, C.y_out = Wd, x_in, p_in, y_out
        C.psum_banks = psum
        C.scr = dict(hres=hres, convT_d=convT_d, QT_d=QT_d, KT_d=KT_d, HQ_d=HQ_d, HF_d=HF_d, Vtm_d=Vtm_d,
                     Gtm_d=Gtm_d, Htm_d=Htm_d, mix_d=mix_d, h1_d=h1_d, hT1_d=hT1_d, g_d=g_d)
        S.barrier()

        for L in range(depth):
            h_src = x_in if L == 0 else hres
            h_dst = y_out if L == depth - 1 else hres
            def run_bd():
                gb = phase_conv(C, L) if "B" in phases else iter(())
                next(gb, None)
                gd = phase_hgrn(C, L) if "D" in phases else iter(())
                next(gd, None)
                for c in range(NCH):
                    next(gb, None)
                    for t in range(4):
                        next(gd, None)
                for _ in gd:
                    pass
                for _ in gb:
                    pass

            for nm, fn, args in (("A", phase_proj, (h_src,)), ("BD", run_bd, None), ("C", phase_nsa, ()),
                                 ("E", phase_ffn1, (h_src,)), ("F", phase_ffn2, (h_dst,))):
                if nm == "BD":
                    if "B" in phases or "D" in phases:
                        fn()
                        S.barrier()
                elif nm in phases:
                    fn(C, L, *args)
                    S.barrier()
        S.barrier()
    return nc, S.waited


def phase_proj(C, L, h_src):
    nc, S, sb, PS, copy, DT = C.nc, C.S, C.sb, C.PS, C.copy, C.DT
    scr = C.scr
    with ExitStack() as ph:
        win = C.load_bf16(ph, "win", C.Wd["w_in"][L], 8, IN_COLS, stage_cols=568)
        hraw = [sb(ph, "hraw%d" % i, [128, D], F32) for i in range(2)]
        hb = [sb(ph, "hb%d" % i, [128, D], BF16) for i in range(2)]
        hT = [sb(ph, "hT%d" % i, [128, 8, 512], BF16) for i in range(2)]
        st_conv = [sb(ph, "st_conv%d" % i, [128, 4, 512], BF16) for i in range(2)]
        st_q = [sb(ph, "st_q%d" % i, [64, 8, 512], BF16) for i in range(2)]
        st_k = [sb(ph, "st_k%d" % i, [64, 8, 512], BF16) for i in range(2)]
        st_hq = [sb(ph, "st_hq%d" % i, [64, 4, 512], F32) for i in range(2)]
        st_hf = [sb(ph, "st_hf%d" % i, [64, 4, 512], F32) for i in range(2)]
        st_v = [sb(ph, "st_v%d" % i, [128, 4, 65], BF16) for i in range(2)]
        for i in range(2):
            S.op(POOL, lambda e, i=i: e.memset(st_v[i][:, :, :], 1.0), [], [st_v[i]])
        st_g = [sb(ph, "st_g%d" % i, [128, 24], F32) for i in range(2)]
        st_h = [sb(ph, "st_h%d" % i, [128, 896], F32) for i in range(2)]
        for i in range(2):
            S.op(POOL, lambda e, i=i: e.memset(st_h[i][:, 768:896], 0.0), [], [st_h[i]])

        KV0 = 1024
        fm = []
        for j in range(4):
            fm.append((j * 128, 128, st_conv, j))
        for h in range(8):
            fm.append((512 + h * 64, 64, st_q, h))
        kvsel = [(0, 0), (0, 1), (1, 0), (1, 1), (2, 0), (2, 1), (4, 0), (4, 1)]
        for n, (j, g) in enumerate(kvsel):
            fm.append((KV0 + (j * 2 + g) * 64, 64, st_k, n))
        for h in range(4):
            fm.append((1816 + h * 64, 64, st_hq, h))
        for h in range(4):
            fm.append((1816 + 256 + h * 64, 64, st_hf, h))

        def prep_chunk(c):
            b = c % 2
            for t in range(4):
                tt = c * 4 + t
                hr, hbb = hraw[tt % 2], hb[tt % 2]
                S.dma(hr, hr[:, :], DT("h", tt), h_src[tt * 128:(tt + 1) * 128, :])
                copy(POOL, hbb, hbb[:, :], hr, hr[:, :])
                C.transposes_bf(hT[b], hT[b][:, :, t * 128:(t + 1) * 128], hbb,
                                lambda i, hbb=hbb: hbb[:, i * 128:(i + 1) * 128], 8)

        prep_chunk(0)
        for c in range(NCH):
            b = c % 2
            if DBGA < 2:
                continue
            for (c0, n, stg, slot) in fm:
                ps = PS()
                for kt in range(8):
                    S.op(PE, lambda e, kt=kt, ps=ps, c0=c0, n=n: e.matmul(
                        ps[0:n, :], lhsT=win[:, kt, c0:c0 + n], rhs=hT[b][:, kt, :], start=(kt == 0), stop=(kt == 7)),
                        [win, hT[b]], [ps])
                copy(C.evac_eng(), stg[b], stg[b][0:n, slot, :], ps, ps[0:n, :])
            sl = slice(c * 512, (c + 1) * 512)
            if c + 1 < NCH:
                prep_chunk(c + 1)
            if DBGA < 3:
                continue
            S.dma(DT("convT", c), scr["convT_d"][:, :, sl].rearrange("j p t -> p j t"), st_conv[b], st_conv[b][:, :, :], queue=POOL)
            S.dma(DT("QT", c), scr["QT_d"][:, :, sl].rearrange("j p t -> p j t"), st_q[b], st_q[b][:, :, :], queue=POOL)
            S.dma(DT("KT", c), scr["KT_d"][:, :, sl].rearrange("j p t -> p j t"), st_k[b], st_k[b][:, :, :], queue=POOL)
            S.dma(DT("HQ", c), scr["HQ_d"][:, :, sl].rearrange("j p t -> p j t"), st_hq[b], st_hq[b][:, :, :], queue=POOL)
            S.dma(DT("HF", c), scr["HF_d"][:, :, sl].rearrange("j p t -> p j t"), st_hf[b], st_hf[b][:, :, :], queue=POOL)
            if DBGA < 4:
                continue
            for t in range(4):
                tt = c * 4 + t
                b2 = tt % 2
                rows = slice(tt * 128, (tt + 1) * 128)
                groups = [(1408, 128), (1664, 152), (2072, 512), (2584, 256)]
                pss = []
                for (c0, n) in groups:
                    ps = PS()
                    pss.append(ps)
                    for kt in range(8):
                        S.op(PE, lambda e, kt=kt, ps=ps, c0=c0, n=n: e.matmul(
                            ps[:, 0:n], lhsT=hT[b][:, kt, t * 128:(t + 1) * 128], rhs=win[:, kt, c0:c0 + n],
                            start=(kt == 0), stop=(kt == 7)), [win, hT[b]], [ps])
                sv, sg, sh = st_v[b2], st_g[b2], st_h[b2]
                if DBGA < 5:
                    continue
                copy(DVE, sv, sv[:, 0:2, 0:64], pss[0], pss[0][:, 0:128].rearrange("p (n d) -> p n d", n=2))
                copy(DVE, sv, sv[:, 2:4, 0:64], pss[1], pss[1][:, 0:128].rearrange("p (n d) -> p n d", n=2))
                if DBGX == 0:
                    S.op(ACT, lambda e, sh=sh, ps=pss[1]: e.activation(out=sh[:, 768:792], in_=ps[:, 128:152], func=AF.Sigmoid),
                         [pss[1]], [sh])
                else:
                    S.op(ACT, lambda e, sg=sg, ps=pss[1]: e.activation(out=sg[:, :], in_=ps[:, 128:152], func=AF.Sigmoid),
                         [pss[1]], [sg])
                copy(ACT if DBGX < 2 else DVE, sh, sh[:, 0:512], pss[2], pss[2][:, 0:512])
                if DBGX != 0:
                    copy(DVE, sh, sh[:, 768:792], sg, sg[:, :])
                copy(DVE, sh, sh[:, 512:768], pss[3], pss[3][:, 0:256])
                if DBGA >= 6:
                    S.dma(DT("Vtm", tt), scr["Vtm_d"][rows, :, :], sv, sv[:, :, :], queue=POOL)
                if DBGA >= 8:
                    S.dma(DT("Htm", tt), scr["Htm_d"][rows, :], sh, sh[:, :], queue=POOL)


def small_T(C, stack, name, src_dram_ap, rows, cols):
    S, sb, PS = C.S, C.sb, C.PS
    nblk = cols // 128
    dst = sb(stack, name, [128, nblk, rows], F32)
    with ExitStack() as tmp:
        src = sb(tmp, name + "_src", [rows, cols], F32)
        S.dma(src, src[:, :], C.DT(name + "_d"), src_dram_ap)
        per = 512 // rows
        j = 0
        while j < nblk:
            n = min(per, nblk - j)
            ps = PS()
            for i in range(n):
                S.op(PE, lambda e, i=i, j=j, ps=ps: e.transpose(out=ps[:, i * rows:(i + 1) * rows],
                                                              in_=src[:, (j + i) * 128:(j + i + 1) * 128],
                                                              identity=C.ident_f[0:rows, 0:rows]), [src, C.ident_f], [ps])
            C.copy(DVE, dst, dst[:, j:j + n, :], ps, ps[:, 0:n * rows].rearrange("p (k r) -> p k r", k=n))
            j += n
        S.barrier()
    return dst


def bc_load(C, stack, name, dram_row_ap, width):
    t = C.sb(stack, name, [128, width], F32)
    C.S.dma(t, t[:, :], C.DT(name + "_d"), dram_row_ap.partition_broadcast(128))
    return t


def phase_conv(C, L):
    nc, S, sb, PS, copy, DT = C.nc, C.S, C.sb, C.PS, C.copy, C.DT
    scr = C.scr
    with ExitStack() as ph:
        cw = small_T(C, ph, "cw", C.Wd["convp"][L], 32, 256)
        g_bc = bc_load(C, ph, "cln_g", C.Wd["conv_ln_g"][L], 256)
        b_bc = bc_load(C, ph, "cln_b", C.Wd["conv_ln_b"][L], 256)
        dg = sb(ph, "dg", [128, 2, 31, 128], BF16)
        n = 0
        for j in range(2):
            for k in range(31):
                eng = DVE if n % 2 == 0 else POOL
                n += 1
                S.op(eng, lambda e, j=j, k=k: e.tensor_scalar(out=dg[:, j, k, :], in0=C.ident_f[:, :],
                                                               scalar1=cw[:, j, k:k + 1], scalar2=None, op0=ALU.mult),
                     [C.ident_f, cw], [dg])
        cin = [sb(ph, "cin%d" % i, [128, 4, 542], BF16) for i in range(2)]
        sg = [sb(ph, "csg%d" % i, [128, 2, 542], BF16) for i in range(2)]
        glu = [sb(ph, "glu%d" % i, [128, 2, 542], BF16) for i in range(2)]
        cT = [sb(ph, "cT%d" % i, [128, 512], F32) for i in range(2)]
        z = [sb(ph, "cz%d" % i, [128, 256], F32) for i in range(2)]
        z2 = [sb(ph, "cz2%d" % i, [128, 256], F32) for i in range(2)]
        yb = [sb(ph, "cy%d" % i, [128, 256], BF16) for i in range(2)]
        st6 = sb(ph, "cst6", [128, 24], F32)
        mv = sb(ph, "cmv", [128, 2], F32)
        rstd = sb(ph, "crstd", [128, 1], F32)
        for i in range(2):
            S.op(POOL, lambda e, i=i: e.memset(cin[i][:, :, 0:30], 0.0), [], [cin[i]])
        yield
        for c in range(NCH):
            b = c % 2
            ci = cin[b]
            if c == 0:
                S.dma(ci, ci[:, :, 30:542], DT("convT", 0), scr["convT_d"][:, :, 0:512].rearrange("j p t -> p j t"))
            else:
                S.dma(ci, ci[:, :, :], DT("convT", c),
                      scr["convT_d"][:, :, c * 512 - 30:(c + 1) * 512].rearrange("j p t -> p j t"))
            S.op(ACT, lambda e: e.activation(out=sg[b][:, :, :], in_=ci[:, 2:4, :], func=AF.Sigmoid), [ci], [sg[b]])
            S.op(DVE, lambda e: e.tensor_tensor(out=glu[b][:, :, :], in0=ci[:, 0:2, :], in1=sg[b][:, :, :], op=ALU.mult),
                 [ci, sg[b]], [glu[b]])
            cts = []
            for j in range(2):
                ps = PS()
                for k in range(31):
                    S.op(PE, lambda e, j=j, k=k, ps=ps: e.matmul(ps[:, :], lhsT=dg[:, j, k, :], rhs=glu[b][:, j, k:k + 512],
                                                                   start=(k == 0), stop=(k == 30)), [dg, glu[b]], [ps])
                ct = cT[j]
                S.op(ACT, lambda e, ps=ps, ct=ct, j=j: e.activation(out=ct[:, :], in_=ps[:, :], func=AF.Identity,
                                                                    bias=cw[:, j, 31:32]), [ps, cw], [ct])
                cts.append(ct)
            for t in range(4):
                tt = c * 4 + t
                zz, zz2, yy = z[tt % 2], z2[tt % 2], yb[tt % 2]
                ps = PS()
                for j in range(2):
                    S.op(PE, lambda e, j=j, ps=ps: e.transpose(out=ps[:, j * 128:(j + 1) * 128],
                                                                in_=cts[j][:, t * 128:(t + 1) * 128],
                                                                identity=C.ident_f[:, :]), [cts[j], C.ident_f], [ps])
                copy(ACT, zz, zz[:, :], ps, ps[:, 0:256])
                C.layernorm((st6, mv, rstd), zz, 256, g_bc, b_bc, zz2, zz2[:, :])
                S.op(ACT, lambda e: e.activation(out=yy[:, :], in_=zz2[:, :], func=AF.Silu), [zz2], [yy])
                S.dma(DT("mixA", tt), scr["mix_d"][tt * 128:(tt + 1) * 128, 0:256], yy, yy[:, :], queue=POOL)
            yield


def phase_nsa(C, L):
    nc, S, sb, PS, copy, DT = C.nc, C.S, C.sb, C.PS, C.copy, C.DT
    scr = C.scr
    ident_b, ident_f, ones_f, zeros_f = C.ident_b, C.ident_f, C.ones_f, C.zeros_f
    with ExitStack() as ph:
        zb = sb(ph, "zb", [128, 2176], BF16)
        S.op(POOL, lambda e: e.memset(zb[:, :], 0.0), [], [zb])
        ob = sb(ph, "ob", [128, 512], BF16)
        S.op(POOL, lambda e: e.memset(ob[:, :], 1.0), [], [ob])
        CM4 = sb(ph, "CM4", [128, 4, 128], BF16)
        S.op(POOL, lambda e: e.affine_select(out=CM4[:, :, :], in_=zb[:, 0:512].rearrange("p (h q) -> p h q", h=4),
                                             pattern=[[0, 4], [1, 128]], compare_op=ALU.is_ge, fill=NEG, base=0,
                                             channel_multiplier=-1), [zb], [CM4])
        WM4 = sb(ph, "WM4", [128, 4, 128], BF16)
        S.op(POOL, lambda e: e.affine_select(out=WM4[:, :, :], in_=zb[:, 0:512].rearrange("p (h q) -> p h q", h=4),
                                             pattern=[[0, 4], [-1, 128]], compare_op=ALU.is_gt, fill=NEG, base=0,
                                             channel_multiplier=1), [zb], [WM4])
        Mtab = sb(ph, "Mtab", [128, 2176], BF16)
        S.op(POOL, lambda e: e.affine_select(out=Mtab[:, :], in_=zb[:, :], pattern=[[1, 2176]], compare_op=ALU.is_ge,
                                             fill=NEG, base=-31, channel_multiplier=-16), [zb], [Mtab])
        Ebig = sb(ph, "Ebig", [64, 4096], BF16)
        Etmp = sb(ph, "Etmp", [64, 4096], BF16)
        S.op(POOL, lambda e: e.memset(Etmp[:, :], 1.0), [], [Etmp])
        S.op(POOL, lambda e: e.affine_select(out=Ebig[:, :], in_=Etmp[:, :], pattern=[[1, 4096]], compare_op=ALU.is_ge,
                                             fill=0.0, base=0, channel_multiplier=-64), [Etmp], [Ebig])
        S.op(POOL, lambda e: e.affine_select(out=Etmp[:, :], in_=Ebig[:, :], pattern=[[-1, 4096]], compare_op=ALU.is_ge,
                                             fill=0.0, base=63, channel_multiplier=64), [Ebig], [Etmp])
        Ebig = Etmp
        Cm = []
        for ct in range(2):
            c1 = sb(ph, "Cm_a%d" % ct, [128, 64], BF16)
            c2 = sb(ph, "Cm_b%d" % ct, [128, 64], BF16)
            S.op(POOL, lambda e, ct=ct, c1=c1: e.affine_select(out=c1[:, :], in_=ob[:, 0:64], pattern=[[-4, 64]],
                                                               compare_op=ALU.is_ge, fill=0.0, base=128 * ct + 1,
                                                               channel_multiplier=1), [ob], [c1])
            S.op(POOL, lambda e, ct=ct, c1=c1, c2=c2: e.affine_select(out=c2[:, :], in_=c1[:, :], pattern=[[4, 64]],
                                                                      compare_op=ALU.is_ge, fill=0.0, base=3 - 128 * ct,
                                                                      channel_multiplier=-1), [c1], [c2])
            Cm.append(c2)
        BON = sb(ph, "BON", [128, 3], F32)
        S.op(POOL, lambda e: e.memset(BON[0:64, 0:1], 2e9), [], [BON])
        S.op(POOL, lambda e: e.memset(BON[0:64, 1:2], 3e9), [], [BON])
        S.op(POOL, lambda e: e.memset(BON[0:64, 2:3], -1e9), [], [BON])
        S.op(POOL, lambda e: e.memset(BON[64:128, 0:1], 0.0), [], [BON])
        S.op(POOL, lambda e: e.memset(BON[64:128, 1:2], 2e9), [], [BON])
        S.op(POOL, lambda e: e.memset(BON[64:128, 2:3], 3e9), [], [BON])

        KcT = sb(ph, "KcT", [64, 2, 256], BF16)
        S.op(POOL, lambda e: e.memset(KcT[:, :, :], 0.0), [], [KcT])
        Vc = sb(ph, "Vc", [128, 2, 2, 65], BF16)
        S.op(POOL, lambda e: e.memset(Vc[:, :, :, :], 1.0), [], [Vc])
        with ExitStack() as cs:
            w1s = sb(cs, "w1s", [64, 32, 128], F32)
            w2s = sb(cs, "w2s", [128, 64], F32)
            pes = sb(cs, "pes", [64, 32], F32)
            peraw = sb(cs, "peraw", [32, 64], F32)
            peb = sb(cs, "peb", [64, 32], BF16)
            w1 = sb(cs, "w1", [64, 32, 128], BF16)
            w2 = sb(cs, "w2", [128, 64], BF16)
            bias = sb(cs, "cbias", [128, 1], F32)
            kc = sb(cs, "kc", [64, S_LEN], BF16)
            xh = sb(cs, "xh", [128, 255], F32)
            x2 = sb(cs, "x2", [128, 255], F32)
            hid = sb(cs, "hid", [128, 256], BF16)
            for kind, (w1n, w2n, pen) in enumerate([("cmp_w1_k", "cmp_w2_k", "cmp_pe_k"),
                                                   ("cmp_w1_v", "cmp_w2_v", "cmp_pe_v")]):
                for l4 in range(4):
                    S.dma(w1s, w1s[:, l4 * 8:(l4 + 1) * 8, :], DT(w1n),
                          C.Wd[w1n][L].rearrange("(l d) m -> d l m", d=64)[:, l4 * 8:(l4 + 1) * 8, :])
                S.dma(w2s, w2s[:, :], DT(w2n), C.Wd[w2n][L])
                S.dma(peraw, peraw[:, :], DT(pen), C.Wd[pen][L])
                pspe = PS()
                S.op(PE, lambda e, pspe=pspe: e.transpose(out=pspe[0:64, 0:32], in_=peraw[:, :], identity=ident_f[0:32, 0:32]),
                     [peraw, ident_f], [pspe])
                copy(DVE, pes, pes[:, :], pspe, pspe[0:64, 0:32])
                copy(DVE, w1, w1[:, :, :], w1s, w1s[:, :, :])
                copy(POOL, w2, w2[:, :], w2s, w2s[:, :])
                copy(POOL, peb, peb[:, :], pes, pes[:, :])
                ps = PS()
                for l in range(32):
                    S.op(PE, lambda e, l=l, ps=ps: e.matmul(ps[:, 0:1], lhsT=w1[:, l, :], rhs=peb[:, l:l + 1],
                                                            start=(l == 0), stop=(l == 31)), [w1, peb], [ps])
                copy(DVE, bias, bias[:, :], ps, ps[:, 0:1])
                for g in range(2):
                    S.dma(kc, kc[:, :], DT("KT_all"), scr["KT_d"][kind * 2 + g, :, :])
                    kv = kc.ap[:, :].rearrange("p (i r) -> p i r", r=16)
                    ps = PS()
                    for l in range(32):
                        rhs = kv[:, 0:255, l] if l < 16 else kv[:, 1:256, l - 16]
                        S.op(PE, lambda e, l=l, ps=ps, rhs=rhs: e.matmul(ps[:, 0:255], lhsT=w1[:, l, :], rhs=rhs,
                                                                         start=(l == 0), stop=(l == 31)), [w1, kc], [ps])
                    S.op(ACT, lambda e, ps=ps: e.activation(out=xh[:, :], in_=ps[:, 0:255], func=AF.Identity,
                                                            bias=bias[:, 0:1]), [ps, bias], [xh])
                    S.op(DVE, lambda e: e.tensor_tensor(out=x2[:, :], in0=xh[:, :], in1=xh[:, :], op=ALU.mult), [xh], [x2])
                    S.op(DVE, lambda e: e.tensor_scalar(out=x2[:, :], in0=x2[:, :], scalar1=0.044715, scalar2=1.0,
                                                        op0=ALU.mult, op1=ALU.add), [x2], [x2])
                    S.op(DVE, lambda e: e.tensor_tensor(out=x2[:, :], in0=x2[:, :], in1=xh[:, :], op=ALU.mult), [x2, xh], [x2])
                    S.op(ACT, lambda e: e.activation(out=x2[:, :], in_=x2[:, :], func=AF.Sigmoid, scale=1.5957691216057308),
                         [x2], [x2])
                    S.op(POOL, lambda e: e.memset(hid[:, 255:256], 0.0), [], [hid])
                    S.op(DVE, lambda e: e.tensor_tensor(out=hid[:, 0:255], in0=x2[:, :], in1=xh[:, :], op=ALU.mult),
                         [x2, xh], [hid])
                    if kind == 0:
                        ps2 = PS()
                        S.op(PE, lambda e, ps2=ps2: e.matmul(ps2[0:64, 0:255], lhsT=w2[:, :], rhs=hid[:, 0:255],
                                                              start=True, stop=True), [w2, hid], [ps2])
                        copy(ACT, KcT, KcT[:, g, 0:255], ps2, ps2[0:64, 0:255])
                    else:
                        for ct in range(2):
                            n = 128 if ct == 0 else 127
                            ps2 = PS()
                            S.op(PE, lambda e, ps2=ps2, ct=ct, n=n: e.matmul(ps2[0:n, 0:64], lhsT=hid[:, ct * 128:ct * 128 + n],
                                                                              rhs=w2[:, :], start=True, stop=True), [w2, hid], [ps2])
                            copy(ACT, Vc, Vc[0:n, ct, g, 0:64], ps2, ps2[0:n, 0:64])
            S.barrier()

        KTs = sb(ph, "KTs", [64, 4, S_LEN], BF16)
        S.dma(KTs, KTs[:, :, :], DT("KT_all"), scr["KT_d"][4:8, :, :].rearrange("j p t -> p j t"))
        Vaug = sb(ph, "Vaug", [128, NT, 260], BF16)
        gq = [sb(ph, "gq%d" % i, [128, 128], F32) for i in range(2)]
        for q4 in range(4):
            S.dma(Vaug, Vaug[:, q4 * 8:(q4 + 1) * 8, :], DT("Vtm_all"),
                  scr["Vtm_d"][q4 * 1024:(q4 + 1) * 1024, :, :].rearrange("(t p) n d -> p t (n d)", p=128))
        QTc = [sb(ph, "QTc%d" % i, [64, 8, 512], BF16) for i in range(2)]
        pT = [sb(ph, "pT%d" % i, [128, 4, 128], BF16) for i in range(3)]
        pT_i = [0]
        from_bank = lambda i: C.psum_banks[i]
        po = [from_bank(0), from_bank(1), from_bank(2)]
        IMP = from_bank(3)
        psT = from_bank(4)
        sc_banks = [from_bank(5), from_bank(6), from_bank(7)]
        sc_i = [0]
        rd = [sb(ph, "rd%d" % i, [128, 4], F32) for i in range(3)]
        rdg = [sb(ph, "rdg%d" % i, [128, 4], F32) for i in range(3)]
        acc = [sb(ph, "acc%d" % i, [128, 8, 64], F32) for i in range(2)]
        tmpo = [sb(ph, "tmpo%d" % i, [128, 4, 64], F32) for i in range(2)]
        ybf = [sb(ph, "ynsa%d" % i, [128, 512], BF16) for i in range(2)]
        imp = sb(ph, "imp", [128, 64], F32)
        imp2 = sb(ph, "imp2", [128, 64], F32)
        m8a = sb(ph, "m8a", [128, 8], F32)
        m8b = sb(ph, "m8b", [128, 8], F32)
        negm = sb(ph, "negm", [128, 64], F32)
        S.op(POOL, lambda e: e.memset(negm[:, :], 0.0), [], [negm])
        negT4 = [sb(ph, "negT4_%d" % i, [64, 4, 128], BF16) for i in range(2)]

        def score_tile(kT_ap, kT_t, rhsQ, qt_t, masks):
            ps = sc_banks[sc_i[0] % 3]
            sc_i[0] += 1
            out3 = ps[:, :].rearrange("p (h q) -> p h q", h=4)
            S.op(PE, lambda e: e.matmul(out3, lhsT=kT_ap, rhs=rhsQ, start=True, stop=(len(masks) == 0),
                                        skip_group_check=True), [kT_t, qt_t], [ps])
            for mi, (l_ap, l_t, r_ap, r_t, osl) in enumerate(masks):
                last = (mi == len(masks) - 1)
                o = out3 if osl is None else ps[:, osl]
                S.op(PE, lambda e, l_ap=l_ap, r_ap=r_ap, o=o, last=last: e.matmul(o, lhsT=l_ap, rhs=r_ap, start=False, stop=last,
                                                                                   skip_group_check=True), [l_t, r_t], [ps])
            p = pT[pT_i[0] % 3]
            pT_i[0] += 1
            S.op(ACT, lambda e: e.activation(out=p[:, :, :], in_=out3, func=AF.Exp, scale=0.125), [ps], [p])
            return p

        def pv(p, bank, v_ap, v_t, first, extra=None):
            for h in range(4):
                S.op(PE, lambda e, h=h: e.matmul(bank[:, h * 65:(h + 1) * 65], lhsT=p[:, h, :], rhs=v_ap,
                                                 start=(first and h == 0), stop=False, skip_group_check=True), [p, v_t], [bank])

        pend = []

        def defer(fn):
            if len(pend) >= 2:
                pend.pop(0)()
            pend.append(fn)

        def flush():
            while pend:
                pend.pop(0)()

        for qt in range(NT):
            c, t = qt // 4, qt % 4
            if t == 0:
                qc = QTc[c % 2]
                S.dma(qc, qc[:, :, :], DT("QT", c), scr["QT_d"][:, :, c * 512:(c + 1) * 512].rearrange("j p t -> p j t"))
            qc = QTc[c % 2]
            Gall = gq[qt % 2]
            S.dma(Gall, Gall[:, :], DT("Htm", qt), scr["Htm_d"][qt * 128:(qt + 1) * 128, 768:896])
            ac = acc[qt % 2]
            for g in range(2):
                rhsQ = qc[:, 4 * g:4 * g + 4, t * 128:(t + 1) * 128]
                use_topk = qt >= 8
                cts = [0] if qt < 16 else [0, 1]
                for ci, ct in enumerate(cts):
                    delta = qt - 16 * ct
                    masks = []
                    if delta <= 16:
                        for h in range(4):
                            masks.append((ident_b[:, :], ident_b, Mtab[:, 128 * delta:128 * delta + 128], Mtab,
                                          slice(h * 128, (h + 1) * 128)))
                    p = score_tile(KcT[:, g, ct * 128:(ct + 1) * 128], KcT, rhsQ, qc, masks)

                    def cmp_pv(p=p, ct=ct, ci=ci):
                        pv(p, po[0], Vc[:, ct, g, :], Vc, ci == 0)
                        if use_topk:
                            for h in range(4):
                                S.op(PE, lambda e, h=h: e.matmul(IMP[:, h * 64:(h + 1) * 64], lhsT=p[:, h, :],
                                                                 rhs=Cm[ct][:, :], start=(ci == 0 and h == 0),
                                                                 stop=False, skip_group_check=True), [p, Cm[ct]], [IMP])
                    defer(cmp_pv)
                flush()
                po3 = [po[b][:, 0:260].rearrange("p (h e) -> p h e", h=4) for b in range(3)]
                S.op(DVE, lambda e: e.tensor_scalar(out=rd[0][:, :], in0=po3[0][:, :, 64], scalar1=1e-30, scalar2=None,
                                                    op0=ALU.add), [po[0]], [rd[0]])
                S.op(DVE, lambda e: e.reciprocal(out=rd[0][:, :], in_=rd[0][:, :]), [rd[0]], [rd[0]])
                nT = None
                if use_topk:
                    W = 2 * qt + 2
                    S.op(DVE, lambda e: e.tensor_scalar(out=imp[:, 0:W], in0=IMP[:, 0:W], scalar1=rd[0][:, 0:1], scalar2=None,
                                                        op0=ALU.mult), [IMP, rd[0]], [imp])
                    for h in range(1, 4):
                        S.op(DVE, lambda e, h=h: e.scalar_tensor_tensor(out=imp[:, 0:W], in0=IMP[:, h * 64:h * 64 + W],
                                                                        scalar=rd[0][:, h:h + 1], in1=imp[:, 0:W],
                                                                        op0=ALU.mult, op1=ALU.add), [IMP, rd[0], imp], [imp])
                    S.op(DVE, lambda e: e.memset(imp[:, 0:1], 4e9), [], [imp])
                    S.op(DVE, lambda e: e.tensor_tensor(out=imp[:, 2 * qt - 1:2 * qt + 2], in0=imp[:, 2 * qt - 1:2 * qt + 2],
                                                        in1=BON[:, :], op=ALU.add), [imp, BON], [imp])
                    S.op(DVE, lambda e: e.max(out=m8a[:, :], in_=imp[:, 0:W]), [imp], [m8a])
                    S.op(DVE, lambda e: e.match_replace(out=imp2[:, 0:W], in_to_replace=m8a[:, :], in_values=imp[:, 0:W],
                                                        imm_value=-3e9), [imp, m8a], [imp2])
                    S.op(DVE, lambda e: e.max(out=m8b[:, :], in_=imp2[:, 0:W]), [imp2], [m8b])
                    S.op(DVE, lambda e: e.tensor_scalar(out=negm[:, 0:W], in0=imp[:, 0:W], scalar1=m8b[:, 7:8], scalar2=NEG,
                                                        op0=ALU.is_lt, op1=ALU.mult), [imp, m8b], [negm])
                    S.op(PE, lambda e: e.transpose(out=psT[0:64, 0:128], in_=negm[:, :], identity=ident_f[:, :]),
                         [negm, ident_f], [psT])
                    nT = negT4[(qt * 2 + g) % 2]
                    copy(DVE, nT, nT[:, :, :], psT, psT[0:64, 0:128].unsqueeze(1).to_broadcast([64, 4, 128]))
                k0 = max(0, qt - 4)
                for kt in range(k0, qt + 1):
                    masks = []
                    if kt == qt:
                        masks.append((ident_b[:, :], ident_b, CM4[:, :, :], CM4, None))
                    if kt == qt - 4:
                        masks.append((ident_b[:, :], ident_b, WM4[:, :, :], WM4, None))
                    p = score_tile(KTs[:, 2 + g, kt * 128:(kt + 1) * 128], KTs, rhsQ, qc, masks)
                    defer(lambda p=p, kt=kt: pv(p, po[2], Vaug[:, kt, (2 + g) * 65:(3 + g) * 65], Vaug, kt == k0))
                for kt in range(qt + 1):
                    masks = []
                    if use_topk:
                        masks.append((Ebig[:, kt * 128:(kt + 1) * 128], Ebig, nT[:, :, :], nT, None))
                    if kt == qt:
                        masks.append((ident_b[:, :], ident_b, CM4[:, :, :], CM4, None))
                    p = score_tile(KTs[:, g, kt * 128:(kt + 1) * 128], KTs, rhsQ, qc, masks)
                    defer(lambda p=p, kt=kt: pv(p, po[1], Vaug[:, kt, g * 65:(g + 1) * 65], Vaug, kt == 0))
                flush()
                gts = Gall[:, 0:24].rearrange("p (h b) -> p h b", b=3)
                for b in range(3):
                    if b > 0:
                        S.op(DVE, lambda e, b=b: e.tensor_scalar(out=rd[b][:, :], in0=po3[b][:, :, 64], scalar1=1e-30,
                                                                 scalar2=None, op0=ALU.add), [po[b]], [rd[b]])
                        S.op(DVE, lambda e, b=b: e.reciprocal(out=rd[b][:, :], in_=rd[b][:, :]), [rd[b]], [rd[b]])
                    S.op(DVE, lambda e, b=b: e.tensor_tensor(out=rdg[b][:, :], in0=rd[b][:, :], in1=gts[:, 4 * g:4 * g + 4, b],
                                                             op=ALU.mult), [rd[b], Gall], [rdg[b]])
                    rb = rdg[b][:, :].unsqueeze(2).to_broadcast([128, 4, 64])
                    if b == 0:
                        S.op(DVE, lambda e, rb=rb: e.tensor_tensor(out=ac[:, 4 * g:4 * g + 4, :], in0=po3[0][:, :, 0:64], in1=rb,
                                                                   op=ALU.mult), [po[0], rdg[0]], [ac])
                    else:
                        tp = tmpo[b % 2]
                        S.op(DVE, lambda e, rb=rb, b=b, tp=tp: e.tensor_tensor(out=tp[:, :, :], in0=po3[b][:, :, 0:64], in1=rb,
                                                                               op=ALU.mult), [po[b], rdg[b]], [tp])
                        S.op(POOL, lambda e, tp=tp: e.tensor_tensor(out=ac[:, 4 * g:4 * g + 4, :], in0=ac[:, 4 * g:4 * g + 4, :],
                                                                     in1=tp[:, :, :], op=ALU.add), [ac, tp], [ac])
            yy = ybf[qt % 2]
            copy(ACT, yy, yy[:, :], ac, ac[:, :, :].rearrange("p h d -> p (h d)"))
            S.dma(DT("mixB", qt), scr["mix_d"][qt * 128:(qt + 1) * 128, 256:768], yy, yy[:, :], queue=POOL)


def phase_hgrn(C, L):
    nc, S, sb, PS, copy, DT = C.nc, C.S, C.sb, C.PS, C.copy, C.DT
    scr = C.scr
    ident_f, ones_f, zeros_f = C.ident_f, C.ones_f, C.zeros_f
    with ExitStack() as ph:
        ob = sb(ph, "h_ob", [128, 512], BF16)
        S.op(POOL, lambda e: e.memset(ob[:, :], 1.0), [], [ob])
        HM4 = sb(ph, "HM4", [128, 4, 128], BF16)
        S.op(POOL, lambda e: e.affine_select(out=HM4[:, :, :], in_=ob[:, :].rearrange("p (h q) -> p h q", h=4),
                                             pattern=[[0, 4], [1, 128]], compare_op=ALU.is_ge, fill=0.0, base=0,
                                             channel_multiplier=-1), [ob], [HM4])
        S.op(POOL, lambda e: e.memset(HM4[0:64, :, 64:128], 0.0), [], [HM4])
        Mrev = sb(ph, "Mrev", [128, 128], F32)
        S.op(POOL, lambda e: e.affine_select(out=Mrev[:, :], in_=ones_f[:, 0:128], pattern=[[-1, 128]],
                                             compare_op=ALU.is_gt, fill=0.0, base=0, channel_multiplier=1), [ones_f], [Mrev])
        S.op(POOL, lambda e: e.memset(Mrev[64:128, 0:64], 0.0), [], [Mrev])
        Mmid = sb(ph, "Mmid", [128, 128], F32)
        Bm = sb(ph, "Bm", [128, 128], F32)
        S.op(POOL, lambda e: e.affine_select(out=Mmid[:, :], in_=ones_f[:, 0:128], pattern=[[1, 128]],
                                             compare_op=ALU.is_ge, fill=0.0, base=0, channel_multiplier=-1), [ones_f], [Mmid])
        S.op(POOL, lambda e: e.memset(Mmid[0:64, 64:128], 0.0), [], [Mmid])
        S.op(POOL, lambda e: e.memset(Bm[:, :], 0.0), [], [Bm])
        S.op(POOL, lambda e: e.memset(Bm[0:32, 0:64], 1.0), [], [Bm])
        S.op(POOL, lambda e: e.memset(Bm[64:96, 64:128], 1.0), [], [Bm])
        S.op(POOL, lambda e: e.tensor_tensor(out=Mmid[:, :], in0=Mmid[:, :], in1=Bm[:, :], op=ALU.subtract), [Mmid, Bm], [Mmid])
        Ecol = sb(ph, "Ecol", [128, 4], F32)
        S.op(POOL, lambda e: e.memset(Ecol[:, :], 0.0), [], [Ecol])
        S.op(POOL, lambda e: e.memset(Ecol[0:32, 0:1], 1.0), [], [Ecol])
        S.op(POOL, lambda e: e.memset(Ecol[0:64, 1:2], 1.0), [], [Ecol])
        S.op(POOL, lambda e: e.memset(Ecol[64:96, 2:3], 1.0), [], [Ecol])
        S.op(POOL, lambda e: e.memset(Ecol[64:128, 3:4], 1.0), [], [Ecol])

        def lbcalc(src, shape3, name):
            P_, X = shape3[0], shape3[2]
            ex = sb(ph, name + "_ex", shape3, F32)
            S.op(ACT, lambda e: e.activation(out=ex[:, :, :], in_=src[:, :, :], func=AF.Exp), [src], [ex])
            ssum = sb(ph, name + "_ss", [P_, X], F32)
            S.op(DVE, lambda e: e.tensor_tensor(out=ssum[:, :], in0=ex[:, 0, :], in1=ex[:, 1, :], op=ALU.add), [ex], [ssum])
            S.op(DVE, lambda e: e.tensor_tensor(out=ssum[:, :], in0=ssum[:, :], in1=ex[:, 2, :], op=ALU.add), [ex, ssum], [ssum])
            S.op(DVE, lambda e: e.tensor_tensor(out=ssum[:, :], in0=ssum[:, :], in1=ex[:, 3, :], op=ALU.add), [ex, ssum], [ssum])
            S.op(DVE, lambda e: e.reciprocal(out=ssum[:, :], in_=ssum[:, :]), [ssum], [ssum])
            lb = sb(ph, name + "_lb", [P_, X], F32)
            S.op(DVE, lambda e: e.memset(lb[:, :], 0.0), [], [lb])
            for d in range(1, L + 1):
                S.op(DVE, lambda e, d=d: e.tensor_tensor(out=lb[:, :], in0=lb[:, :], in1=ex[:, d, :], op=ALU.add), [lb, ex], [lb])
            S.op(DVE, lambda e: e.tensor_tensor(out=lb[:, :], in0=lb[:, :], in1=ssum[:, :], op=ALU.mult), [lb, ssum], [lb])
            oml = sb(ph, name + "_oml", [P_, X], F32)
            S.op(DVE, lambda e: e.tensor_scalar(out=oml[:, :], in0=lb[:, :], scalar1=-1.0, scalar2=1.0, op0=ALU.mult,
                                                op1=ALU.add), [lb], [oml])
            return lb, oml
        lsrc_bc = sb(ph, "lsrc_bc", [128, 4, 256], F32)
        S.dma(lsrc_bc, lsrc_bc[:, :, :], DT("lbl"), C.Wd["lb_logits"].partition_broadcast(128))
        lb_bc, oml_bc = lbcalc(lsrc_bc, [128, 4, 256], "lbb")
        lsrc_fm = sb(ph, "lsrc_fm", [64, 4, 4], F32)
        lraw = sb(ph, "lraw", [4, 256], F32)
        S.dma(lraw, lraw[:, :], DT("lbl"), C.Wd["lb_logits"])
        psl = PS()
        for h in range(4):
            S.op(PE, lambda e, h=h: e.transpose(out=psl[0:64, h * 4:(h + 1) * 4], in_=lraw[:, h * 64:(h + 1) * 64],
                                                identity=ident_f[0:4, 0:4]), [lraw, ident_f], [psl])
        copy(DVE, lsrc_fm, lsrc_fm[:, :, :].rearrange("p d h -> p h d"), psl, psl[0:64, 0:16].rearrange("p (h d) -> p h d", h=4))
        lb_fm, oml_fm = lbcalc(lsrc_fm, [64, 4, 4], "lbf")
        ng_bc = bc_load(C, ph, "ng_bc", C.Wd["hgrn_norm_g"][L], 64)

        htm = [sb(ph, "htm%d" % i, [128, 768], F32) for i in range(2)]
        hq = [sb(ph, "hq%d" % i, [64, 4, 128], F32) for i in range(2)]
        hf = [sb(ph, "hf%d" % i, [64, 4, 128], F32) for i in range(2)]
        sig = sb(ph, "hsig", [128, 256], F32)
        ff = sb(ph, "hff", [128, 256], F32)
        logf = [sb(ph, "hlogf%d" % i, [128, 256], F32) for i in range(2)]
        kk = sb(ph, "hkk", [128, 256], F32)
        erev = sb(ph, "herev", [128, 256], F32)
        khat = [sb(ph, "hkhat%d" % i, [128, 256], BF16) for i in range(2)]
        vb = [sb(ph, "hvb%d" % i, [128, 256], BF16) for i in range(2)]
        sigT = sb(ph, "hsigT", [64, 4, 128], F32)
        kkT = sb(ph, "hkkT", [64, 4, 128], F32)
        eq = sb(ph, "heq", [64, 4, 128], F32)
        ek = sb(ph, "hek", [64, 4, 128], F32)
        qz0 = [sb(ph, "hqz0_%d" % i, [64, 4, 128], BF16) for i in range(2)]
        qz1 = [sb(ph, "hqz1_%d" % i, [64, 4, 128], BF16) for i in range(2)]
        for i in range(2):
            S.op(POOL, lambda e, i=i: e.memset(qz0[i][:, :, :], 0.0), [], [qz0[i]])
            S.op(POOL, lambda e, i=i: e.memset(qz1[i][:, :, :], 0.0), [], [qz1[i]])
        ktl = [sb(ph, "hktl%d" % i, [64, 4, 128], BF16) for i in range(2)]
        em = [sb(ph, "hem%d" % i, [64, 4, 4], F32) for i in range(2)]
        sT = [sb(ph, "hsT%d" % i, [128, 4, 128], BF16) for i in range(2)]
        state = [sb(ph, "hstate%d" % i, [64, 4, 64], F32) for i in range(2)]
        S.op(DVE, lambda e: e.memset(state[0][:, :, :], 0.0), [], [state[0]])
        stmp = sb(ph, "hstmp", [64, 4, 64], F32)
        ss0 = [sb(ph, "hss0_%d" % i, [64, 4, 64], BF16) for i in range(2)]
        ss1 = [sb(ph, "hss1_%d" % i, [64, 4, 64], BF16) for i in range(2)]
        o = sb(ph, "ho", [128, 4, 64], F32)
        osq = sb(ph, "hosq", [128, 4, 64], F32)
        rs = sb(ph, "hrs", [128, 4], F32)
        sgl = sb(ph, "hsgl", [128, 256], F32)
        yb = [sb(ph, "hy%d" % i, [128, 256], BF16) for i in range(2)]

        yield
        for tt in range(NT):
            b = tt % 2
            rows = slice(tt * 128, (tt + 1) * 128)
            cols = slice(tt * 128, (tt + 1) * 128)
            S.dma(htm[b], htm[b][:, :], DT("Htm", tt), scr["Htm_d"][rows, 0:768])
            S.dma(hq[b], hq[b][:, :, :], DT("HQ", tt // 4), scr["HQ_d"][:, :, cols].rearrange("j p t -> p j t"))
            S.dma(hf[b], hf[b][:, :, :], DT("HF", tt // 4), scr["HF_d"][:, :, cols].rearrange("j p t -> p j t"))
            H = htm[b]
            if DBGD < 2:
                yield
                continue
            S.op(ACT, lambda e: e.activation(out=sig[:, :], in_=H[:, 0:256], func=AF.Sigmoid), [H], [sig])
            S.op(DVE, lambda e: e.tensor_tensor(out=ff[:, :], in0=sig[:, :], in1=oml_bc[:, :], op=ALU.mult), [sig, oml_bc], [ff])
            S.op(POOL, lambda e: e.tensor_tensor(out=ff[:, :], in0=ff[:, :], in1=lb_bc[:, :], op=ALU.add), [ff, lb_bc], [ff])
            lf = logf[b]
            S.op(ACT, lambda e: e.activation(out=lf[:, :], in_=ff[:, :], func=AF.Ln), [ff], [lf])
            S.op(DVE, lambda e: e.tensor_scalar(out=kk[:, :], in0=ff[:, :], scalar1=-1.0, scalar2=1.0, op0=ALU.mult,
                                                op1=ALU.add), [ff], [kk])
            psA = PS()
            S.op(PE, lambda e: e.matmul(psA[:, 0:256], lhsT=Mrev[:, :], rhs=lf[:, :], start=True, stop=True), [Mrev, lf], [psA])
            S.op(ACT, lambda e: e.activation(out=erev[:, :], in_=psA[:, 0:256], func=AF.Exp), [psA], [erev])
            S.op(DVE, lambda e: e.tensor_tensor(out=khat[b][:, :], in0=kk[:, :], in1=erev[:, :], op=ALU.mult), [kk, erev], [khat[b]])
            copy(POOL, vb[b], vb[b][:, :], H, H[:, 256:512])
            if DBGD < 3:
                yield
                continue
            S.op(ACT, lambda e: e.activation(out=sigT[:, :, :], in_=hf[b][:, :, :], func=AF.Sigmoid), [hf[b]], [sigT])
            S.op(DVE, lambda e: e.tensor_tensor(out=sigT[:, :, :], in0=sigT[:, :, :],
                                                in1=oml_fm[:, :].unsqueeze(2).to_broadcast([64, 4, 128]), op=ALU.mult),
                 [sigT, oml_fm], [sigT])
            S.op(DVE, lambda e: e.tensor_tensor(out=sigT[:, :, :], in0=sigT[:, :, :],
                                                 in1=lb_fm[:, :].unsqueeze(2).to_broadcast([64, 4, 128]), op=ALU.add),
                 [sigT, lb_fm], [sigT])
            S.op(DVE, lambda e: e.tensor_scalar(out=kkT[:, :, :], in0=sigT[:, :, :], scalar1=-1.0, scalar2=1.0, op0=ALU.mult,
                                                op1=ALU.add), [sigT], [kkT])
            psB = PS()
            for h in range(4):
                S.op(PE, lambda e, h=h: e.matmul(psB[0:64, h * 128:(h + 1) * 128], lhsT=lf[:, h * 64:(h + 1) * 64], rhs=Mmid[:, :],
                                                 start=True, stop=True), [lf, Mmid], [psB])
            psB3 = psB[0:64, :].rearrange("p (h t) -> p h t", h=4)
            S.op(ACT, lambda e: e.activation(out=eq[:, :, :], in_=psB3, func=AF.Exp), [psB], [eq])
            S.op(ACT, lambda e: e.activation(out=ek[:, :, :], in_=psB3, func=AF.Exp, scale=-1.0), [psB], [ek])
            S.op(DVE, lambda e: e.tensor_tensor(out=qz0[b][:, :, 0:64], in0=hq[b][:, :, 0:64], in1=eq[:, :, 0:64], op=ALU.mult),
                 [hq[b], eq], [qz0[b]])
            S.op(POOL, lambda e: e.tensor_tensor(out=qz1[b][:, :, 64:128], in0=hq[b][:, :, 64:128], in1=eq[:, :, 64:128],
                                                 op=ALU.mult), [hq[b], eq], [qz1[b]])
            S.op(DVE, lambda e: e.tensor_tensor(out=ktl[b][:, :, :], in0=kkT[:, :, :], in1=ek[:, :, :], op=ALU.mult),
                 [kkT, ek], [ktl[b]])
            psC = PS()
            for h in range(4):
                S.op(PE, lambda e, h=h: e.matmul(psC[0:64, h * 4:(h + 1) * 4], lhsT=lf[:, h * 64:(h + 1) * 64], rhs=Ecol[:, :],
                                                 start=True, stop=True), [lf, Ecol], [psC])
            S.op(ACT, lambda e: e.activation(out=em[b][:, :, :], in_=psC[0:64, 0:16].rearrange("p (h c) -> p h c", h=4),
                                             func=AF.Exp), [psC], [em[b]])
            if DBGD < 4:
                yield
                continue
            psS = PS()
            for h in range(4):
                S.op(PE, lambda e, h=h: e.matmul(psS[:, h * 128:(h + 1) * 128], lhsT=ktl[b][:, h, :], rhs=qz0[b][:, h, :],
                                                 start=True, stop=False), [ktl[b], qz0[b]], [psS])
                S.op(PE, lambda e, h=h: e.matmul(psS[:, h * 128:(h + 1) * 128], lhsT=ktl[b][:, h, :], rhs=qz1[b][:, h, :],
                                                 start=False, stop=True), [ktl[b], qz1[b]], [psS])
            S.op(DVE, lambda e: e.tensor_tensor(out=sT[b][:, :, :], in0=psS[:, :].rearrange("p (h t) -> p h t", h=4),
                                                in1=HM4[:, :, :], op=ALU.mult), [psS, HM4], [sT[b]])
            if DBGD < 5:
                yield
                continue
            psKc = [PS(), PS()]
            for c in range(2):
                for h in range(4):
                    S.op(PE, lambda e, c=c, h=h: e.matmul(psKc[c][0:64, h * 64:(h + 1) * 64],
                                                          lhsT=khat[b][c * 64:(c + 1) * 64, h * 64:(h + 1) * 64],
                                                          rhs=vb[b][c * 64:(c + 1) * 64, h * 64:(h + 1) * 64],
                                                          start=True, stop=True), [khat[b], vb[b]], [psKc[c]])
            psK4 = [psKc[c][0:64, 0:256].rearrange("p (h v) -> p h v", h=4) for c in range(2)]
            st_in = state[tt % 2]
            st_out = state[(tt + 1) % 2]
            E = em[b]

            def ebc(i):
                return E[:, :, i:i + 1].to_broadcast([64, 4, 64])
            S.op(DVE, lambda e: e.tensor_tensor(out=ss0[b][:, :, :], in0=st_in[:, :, :], in1=ebc(0), op=ALU.mult), [st_in, E], [ss0[b]])
            S.op(DVE, lambda e: e.tensor_tensor(out=stmp[:, :, :], in0=st_in[:, :, :], in1=ebc(1), op=ALU.mult), [st_in, E], [stmp])
            S.op(DVE, lambda e: e.tensor_tensor(out=stmp[:, :, :], in0=stmp[:, :, :], in1=psK4[0], op=ALU.add),
                 [stmp, psKc[0]], [stmp])
            S.op(DVE, lambda e: e.tensor_tensor(out=ss1[b][:, :, :], in0=stmp[:, :, :], in1=ebc(2), op=ALU.mult), [stmp, E], [ss1[b]])
            S.op(DVE, lambda e: e.tensor_tensor(out=st_out[:, :, :], in0=stmp[:, :, :], in1=ebc(3), op=ALU.mult), [stmp, E], [st_out])
            S.op(DVE, lambda e: e.tensor_tensor(out=st_out[:, :, :], in0=st_out[:, :, :], in1=psK4[1], op=ALU.add),
                 [st_out, psKc[1]], [st_out])
            if DBGD < 6:
                yield
                continue
            psO = PS()
            for h in range(4):
                osl = psO[:, h * 64:(h + 1) * 64]
                S.op(PE, lambda e, h=h, osl=osl: e.matmul(osl, lhsT=sT[b][:, h, :], rhs=vb[b][:, h * 64:(h + 1) * 64],
                                                          start=True, stop=False), [sT[b], vb[b]], [psO])
                S.op(PE, lambda e, h=h, osl=osl: e.matmul(osl, lhsT=qz0[b][:, h, :], rhs=ss0[b][:, h, :],
                                                          start=False, stop=False), [qz0[b], ss0[b]], [psO])
                S.op(PE, lambda e, h=h, osl=osl: e.matmul(osl, lhsT=qz1[b][:, h, :], rhs=ss1[b][:, h, :],
                                                          start=False, stop=True), [qz1[b], ss1[b]], [psO])
            copy(ACT, o, o[:, :, :], psO, psO[:, 0:256].rearrange("p (h v) -> p h v", h=4))
            S.op(POOL, lambda e: e.tensor_tensor(out=osq[:, :, :], in0=o[:, :, :], in1=o[:, :, :], op=ALU.mult), [o], [osq])
            S.op(DVE, lambda e: e.tensor_reduce(out=rs[:, :], in_=osq[:, :, :], axis=mybir.AxisListType.X, op=ALU.add), [osq], [rs])
            S.op(DVE, lambda e: e.tensor_scalar(out=rs[:, :], in0=rs[:, :], scalar1=1.0 / 64.0, scalar2=LN_EPS, op0=ALU.mult,
                                                op1=ALU.add), [rs], [rs])
            S.op(ACT, lambda e: e.activation(out=rs[:, :], in_=rs[:, :], func=AF.Sqrt), [rs], [rs])
            S.op(DVE, lambda e: e.reciprocal(out=rs[:, :], in_=rs[:, :]), [rs], [rs])
            S.op(DVE, lambda e: e.tensor_tensor(out=o[:, :, :], in0=o[:, :, :], in1=rs[:, :].unsqueeze(2).to_broadcast([128, 4, 64]),
                                                op=ALU.mult), [o, rs], [o])
            S.op(DVE, lambda e: e.tensor_tensor(out=o[:, :, :], in0=o[:, :, :],
                                                 in1=ng_bc[:, :].unsqueeze(1).to_broadcast([128, 4, 64]), op=ALU.mult), [o, ng_bc], [o])
            S.op(ACT, lambda e: e.activation(out=sgl[:, :], in_=H[:, 512:768], func=AF.Silu), [H], [sgl])
            S.op(DVE, lambda e: e.tensor_tensor(out=yb[b][:, :], in0=o[:, :, :].rearrange("p h v -> p (h v)"), in1=sgl[:, :],
                                                op=ALU.mult), [o, sgl], [yb[b]])
            S.dma(DT("mixC", tt), scr["mix_d"][rows, 768:1024], yb[b], yb[b][:, :], queue=POOL)
            yield


def phase_ffn1(C, L, h_src):
    nc, S, sb, PS, copy, DT = C.nc, C.S, C.sb, C.PS, C.copy, C.DT
    scr = C.scr
    with ExitStack() as ph:
        wout = C.load_bf16(ph, "wout", C.Wd["w_out"][L], 8, D, stage_cols=512)
        wup = C.load_bf16(ph, "wup", C.Wd["w_up"][L], 8, 2 * D_FF, stage_cols=512)
        fw = small_T(C, ph, "fw", C.Wd["ffnp"][L], 4, 2 * D_FF)
        g_bc = bc_load(C, ph, "ln1g", C.Wd["ln1_g"][L], D)
        b_bc = bc_load(C, ph, "ln1b", C.Wd["ln1_b"][L], D)
        mixt = [sb(ph, "mixt%d" % i, [128, D], BF16) for i in range(2)]
        mixT = [sb(ph, "mixT%d" % i, [128, 8, 128], BF16) for i in range(2)]
        hin = [sb(ph, "hin%d" % i, [128, D], F32) for i in range(2)]
        z = [sb(ph, "z1_%d" % i, [128, D], F32) for i in range(2)]
        h1 = [sb(ph, "h1_%d" % i, [128, D], F32) for i in range(2)]
        h1b = [sb(ph, "h1b_%d" % i, [128, D], BF16) for i in range(2)]
        hT1 = [sb(ph, "hT1_%d" % i, [128, 8, 512], BF16) for i in range(2)]
        st6 = sb(ph, "st6", [128, 24], F32)
        mv = sb(ph, "mv", [128, 2], F32)
        rstd = sb(ph, "rstd", [128, 1], F32)
        usb = [sb(ph, "usb%d" % i, [128, 514], F32) for i in range(4)]
        yv = [sb(ph, "yv%d" % i, [128, 512], F32) for i in range(3)]
        yg = [sb(ph, "yg%d" % i, [128, 512], F32) for i in range(3)]
        gj = [sb(ph, "gj%d" % i, [128, 512], BF16) for i in range(3)]
        carry = [sb(ph, "carry%d" % i, [128, 44, 2], F32) for i in range(2)]
        S.op(POOL, lambda e: e.memset(carry[0][:, :, :], 0.0), [], [carry[0]])
        def ln_tile(c, t):
            cb = c % 2
            tt = c * 4 + t
            b = tt % 2
            rows = slice(tt * 128, (tt + 1) * 128)
            S.dma(mixt[b], mixt[b][:, :], DT("mix_all"), scr["mix_d"][rows, :])
            S.dma(hin[b], hin[b][:, :], DT("h", tt), h_src[rows, :])
            C.transposes_bf(mixT[b], mixT[b][:, :, :], mixt[b], lambda i, b=b: mixt[b][:, i * 128:(i + 1) * 128], 8)
            pss = []
            for half in range(2):
                ps = PS()
                pss.append(ps)
                for kt in range(8):
                    S.op(PE, lambda e, kt=kt, ps=ps, half=half, b=b: e.matmul(
                        ps[:, :], lhsT=mixT[b][:, kt, :], rhs=wout[:, kt, half * 512:(half + 1) * 512],
                        start=(kt == 0), stop=(kt == 7)), [mixT[b], wout], [ps])
            for half in range(2):
                S.op(DVE, lambda e, half=half, b=b: e.scalar_tensor_tensor(
                    out=z[b][:, half * 512:(half + 1) * 512], in0=hin[b][:, half * 512:(half + 1) * 512], scalar=ALPHA,
                    in1=pss[half][:, :], op0=ALU.mult, op1=ALU.add), [hin[b], pss[half]], [z[b]])
            C.layernorm((st6, mv, rstd), z[b], D, g_bc, b_bc, h1[b], h1[b][:, :])
            S.dma(DT("h1", tt), scr["h1_d"][rows, :], h1[b], h1[b][:, :], queue=POOL)
            copy(ACT, h1b[b], h1b[b][:, :], h1[b], h1[b][:, :])
            C.transposes_bf(hT1[cb], hT1[cb][:, :, t * 128:(t + 1) * 128], h1b[b],
                            lambda i, b=b: h1b[b][:, i * 128:(i + 1) * 128], 8)

        def store_hT1(c):
            cb = c % 2
            sl = slice(c * 512, (c + 1) * 512)
            S.dma(DT("hT1", c), scr["hT1_d"][:, :, sl].rearrange("j p t -> p j t"), hT1[cb], hT1[cb][:, :, :], queue=POOL)

        tails = []

        def up_j(c, j):
            cb = c % 2
            sl = slice(c * 512, (c + 1) * 512)
            cin, cout = carry[c % 2], carry[(c + 1) % 2]
            ys = []
            for vg in range(2):
                col = vg * D_FF + j * 128
                jj = vg * NFF + j
                ps = PS()
                for kt in range(8):
                    S.op(PE, lambda e, kt=kt, ps=ps, col=col: e.matmul(ps[:, :], lhsT=wup[:, kt, col:col + 128],
                                                                        rhs=hT1[cb][:, kt, :], start=(kt == 0), stop=(kt == 7)),
                         [wup, hT1[cb]], [ps])
                u = usb[(j * 2 + vg) % 4]
                copy(ACT, u, u[:, 2:514], ps, ps[:, :])
                copy(ACT, u, u[:, 0:2], cin, cin[:, jj, :])
                copy(ACT, cout, cout[:, jj, :], u, u[:, 512:514])
                y = (yv if vg == 0 else yg)[j % 3]
                S.op(POOL, lambda e, u=u, y=y, jj=jj: e.tensor_scalar(out=y[:, :], in0=u[:, 2:514], scalar1=fw[:, jj, 2:3],
                                                                     scalar2=fw[:, jj, 3:4], op0=ALU.mult, op1=ALU.add),
                     [u, fw], [y])
                S.op(DVE, lambda e, u=u, y=y, jj=jj: e.scalar_tensor_tensor(out=y[:, :], in0=u[:, 1:513], scalar=fw[:, jj, 1:2],
                                                                             in1=y[:, :], op0=ALU.mult, op1=ALU.add),
                     [u, fw, y], [y])
                S.op(DVE, lambda e, u=u, y=y, jj=jj: e.scalar_tensor_tensor(out=y[:, :], in0=u[:, 0:512], scalar=fw[:, jj, 0:1],
                                                                            in1=y[:, :], op0=ALU.mult, op1=ALU.add),
                     [u, fw, y], [y])
                ys.append(y)
            def tail(ys=ys, j=j, c=c, sl=sl):
                S.op(ACT, lambda e, y=ys[1]: e.activation(out=y[:, :], in_=y[:, :], func=AF.Silu), [ys[1]], [ys[1]])
                gjt = gj[j % 3]
                S.op(POOL, lambda e, gjt=gjt, ys=ys: e.tensor_tensor(out=gjt[:, :], in0=ys[0][:, :], in1=ys[1][:, :], op=ALU.mult),
                     [ys[0], ys[1]], [gjt])
                S.dma(DT("g", c), scr["g_d"][j, :, sl], gjt, gjt[:, :], queue=POOL)
            if tails:
                tails.pop(0)()
            tails.append(tail)

        for t in range(4):
            ln_tile(0, t)
        store_hT1(0)
        for c in range(NCH):
            for j in range(NFF if DBGE >= 2 else 0):
                up_j(c, j)
                if c + 1 < NCH and j in (2, 7, 12, 17):
                    ln_tile(c + 1, (j - 2) // 5)
            while tails:
                tails.pop(0)()
            if c + 1 < NCH:
                if DBGE < 2:
                    for t in range(4):
                        ln_tile(c + 1, t)
                store_hT1(c + 1)


def phase_ffn2(C, L, h_dst):
    nc, S, sb, PS, copy, DT = C.nc, C.S, C.sb, C.PS, C.copy, C.DT
    scr = C.scr
    with ExitStack() as ph:
        wdn = C.load_bf16(ph, "wdn", C.Wd["w_down"][L], NFF, D, stage_cols=256)
        wg = C.load_bf16(ph, "wg", C.Wd["w_ple_gate"][L], 8, D, stage_cols=512)
        wp = C.load_bf16(ph, "wp", C.Wd["w_ple_proj"][L], 2, D, stage_cols=512)
        g_bc = bc_load(C, ph, "ln2g", C.Wd["ln2_g"][L], D)
        b_bc = bc_load(C, ph, "ln2b", C.Wd["ln2_b"][L], D)
        gT = [sb(ph, "gTl%d" % i, [128, NFF, 512], BF16) for i in range(2)]
        hT1 = [sb(ph, "hT1l%d" % i, [128, 8, 512], BF16) for i in range(2)]
        h1 = [sb(ph, "h1l%d" % i, [128, D], F32) for i in range(2)]
        pt = [sb(ph, "pt%d" % i, [128, 256], F32) for i in range(2)]
        ptb = [sb(ph, "ptb%d" % i, [128, 256], BF16) for i in range(2)]
        pT = [sb(ph, "ppT%d" % i, [128, 2, 128], BF16) for i in range(2)]
        sgt = [sb(ph, "sgt%d" % i, [128, D], F32) for i in range(2)]
        z = [sb(ph, "z2_%d" % i, [128, D], F32) for i in range(2)]
        h2 = [sb(ph, "h2_%d" % i, [128, D], F32) for i in range(2)]
        st6 = sb(ph, "st6b", [128, 24], F32)
        mv = sb(ph, "mvb", [128, 2], F32)
        rstd = sb(ph, "rstdb", [128, 1], F32)
        def prep(tt):
            b = tt % 2
            rows = slice(tt * 128, (tt + 1) * 128)
            S.dma(h1[b], h1[b][:, :], DT("h1", tt), scr["h1_d"][rows, :])
            S.dma(pt[b], pt[b][:, :], DT("p", tt), C.p_in[L, rows, :])
            copy(DVE, ptb[b], ptb[b][:, :], pt[b], pt[b][:, :])
            C.transposes_bf(pT[b], pT[b][:, :, :], ptb[b], lambda i, b=b: ptb[b][:, i * 128:(i + 1) * 128], 2)

        for c in range(NCH):
            cb = c % 2
            sl = slice(c * 512, (c + 1) * 512)
            S.dma(gT[cb], gT[cb][:, :, :], DT("g", c), scr["g_d"][:, :, sl].rearrange("j p t -> p j t"))
            S.dma(hT1[cb], hT1[cb][:, :, :], DT("hT1", c), scr["hT1_d"][:, :, sl].rearrange("j p t -> p j t"))
            for t in range(4):
                tt = c * 4 + t
                b = tt % 2
                rows = slice(tt * 128, (tt + 1) * 128)
                tok = slice(t * 128, (t + 1) * 128)
                if tt == 0:
                    prep(0)
                if tt + 1 < NT:
                    prep(tt + 1)
                psf, psg, psp = [], [], []
                for half in range(2):
                    hs = slice(half * 512, (half + 1) * 512)
                    ps = PS()
                    psg.append(ps)
                    for kt in range(8):
                        S.op(PE, lambda e, kt=kt, ps=ps, hs=hs: e.matmul(ps[:, :], lhsT=hT1[cb][:, kt, tok], rhs=wg[:, kt, hs],
                                                                          start=(kt == 0), stop=(kt == 7)), [hT1[cb], wg], [ps])
                    S.op(ACT, lambda e, ps=ps, hs=hs, b=b: e.activation(out=sgt[b][:, hs], in_=ps[:, :], func=AF.Sigmoid),
                         [ps], [sgt[b]])
                    ps = PS()
                    psp.append(ps)
                    for kt in range(2):
                        S.op(PE, lambda e, kt=kt, ps=ps, hs=hs, b=b: e.matmul(ps[:, :], lhsT=pT[b][:, kt, :], rhs=wp[:, kt, hs],
                                                                               start=(kt == 0), stop=(kt == 1)), [pT[b], wp], [ps])
                    S.op(DVE, lambda e, ps=ps, hs=hs, b=b: e.tensor_tensor(out=sgt[b][:, hs], in0=sgt[b][:, hs], in1=ps[:, :],
                                                                           op=ALU.mult), [sgt[b], ps], [sgt[b]])
                    ps = PS()
                    psf.append(ps)
                    for j in range(NFF):
                        S.op(PE, lambda e, j=j, ps=ps, hs=hs: e.matmul(ps[:, :], lhsT=gT[cb][:, j, tok], rhs=wdn[:, j, hs],
                                                                        start=(j == 0), stop=(j == NFF - 1)), [gT[cb], wdn], [ps])
                    S.op(DVE, lambda e, ps=ps, hs=hs, b=b: e.scalar_tensor_tensor(out=z[b][:, hs], in0=h1[b][:, hs], scalar=ALPHA,
                                                                                  in1=ps[:, :], op0=ALU.mult, op1=ALU.add),
                         [h1[b], ps], [z[b]])
                    S.op(POOL, lambda e, hs=hs, b=b: e.tensor_tensor(out=z[b][:, hs], in0=z[b][:, hs], in1=sgt[b][:, hs], op=ALU.add),
                         [z[b], sgt[b]], [z[b]])
                C.layernorm((st6, mv, rstd), z[b], D, g_bc, b_bc, h2[b], h2[b][:, :])
                S.dma(DT("h", tt), h_dst[rows, :], h2[b], h2[b][:, :], queue=POOL)


_NC_CACHE = {}


def _prep_inputs(inp, depth=DEPTH):
    f = lambda a: np.ascontiguousarray(np.asarray(a, dtype=np.float32))
    shared = {}
    for k in ("w_in", "conv_ln_g", "conv_ln_b", "cmp_pe_k", "cmp_pe_v", "cmp_w1_k", "cmp_w2_k", "cmp_w1_v", "cmp_w2_v",
              "lb_logits", "hgrn_norm_g", "w_out", "ln1_g", "ln1_b", "w_up", "w_down", "w_ple_gate", "w_ple_proj",
              "ln2_g", "ln2_b"):
        shared[k] = f(inp[k])
    shared["convp"] = f(np.concatenate([np.asarray(inp["conv_w"]), np.asarray(inp["conv_b"])[:, None, :]], axis=1))
    shared["ffnp"] = f(np.concatenate([np.asarray(inp["ffn_conv_w"]), np.asarray(inp["ffn_conv_b"])[:, None, :]], axis=1))
    x = np.asarray(inp["x"], dtype=np.float32)
    p = np.asarray(inp["p"], dtype=np.float32)
    maps = []
    for b in range(8):
        m = dict(shared)
        m["x"] = np.ascontiguousarray(x[b])
        m["p"] = np.ascontiguousarray(p[:, b])
        maps.append(m)
    return maps


def kernel(**inputs):
    if "nc" not in _NC_CACHE:
        _NC_CACHE["nc"] = build()
    nc = _NC_CACHE["nc"]
    maps = _prep_inputs(inputs)
    res = run_bass_kernel_spmd(nc, maps, core_ids=list(range(8)))
    out = np.stack([np.asarray(r["y"], dtype=np.float32) for r in res.results], axis=0)
    return out
```

```python
import numpy as np
from contextlib import ExitStack
import concourse.bass as bass
import concourse.mybir as mybir
from concourse.bass_utils import run_bass_kernel_spmd

F32 = mybir.dt.float32
BF16 = mybir.dt.bfloat16
AF = mybir.ActivationFunctionType
ALU = mybir.AluOpType

PE, ACT, DVE, POOL, SP = "tensor", "scalar", "vector", "gpsimd", "sync"
COMPUTE = (PE, ACT, DVE, POOL)
ALLENG = (PE, ACT, DVE, POOL, SP)
INORDER_SAFE = (PE, ACT, DVE)

S_LEN = 4096
D = 1024
DEPTH = 4
NT = S_LEN // 128
NCH = S_LEN // 512
IN_COLS = 2840
D_FF = 2816
NFF = D_FF // 128
ALPHA = (2 * DEPTH) ** 0.25
LN_EPS = 1e-5
NEG = -30000.0
DBGA = 9
STORES_ON_POOL = True
DBGE = 9
DBGD = 9
ACT_FENCE = False
DBGX = 0


class T:
    __slots__ = ("ap", "w", "r", "psum")

    def __init__(self, ap, psum=False):
        self.ap = ap
        self.w = {}
        self.r = {}
        self.psum = psum

    def __getitem__(self, k):
        return self.ap[k]


class Sched:
    def __init__(self, nc, stack, n_dma=40, marked=None):
        self.nc = nc
        self.marked = marked
        self.waited = set()
        self.mrank = {e: 0 for e in COMPUTE}
        self.rank_of = {}
        self.act_scratch = None
        self.cnt = {e: 0 for e in COMPUTE}
        self.known = {e: {} for e in ALLENG}
        self.n_dma = n_dma
        self.dma_cnt = [0] * n_dma
        self.dma_next = 0
        self.sems = {}
        for e in COMPUTE:
            self.sems[e] = stack.enter_context(nc.semaphore("s_" + e))
        for k in range(n_dma):
            self.sems[("dma", k)] = stack.enter_context(nc.semaphore("s_dma%d" % k))
        self.n_ins = 0

    def _deps(self, eng, reads, writes):
        deps = {}
        for t in reads:
            for k, v in t.w.items():
                if deps.get(k, 0) < v:
                    deps[k] = v
            if t.psum:
                for k, v in t.r.items():
                    if k != eng and deps.get(k, 0) < v:
                        deps[k] = v
        for t in writes:
            for src in (t.w, t.r):
                for k, v in src.items():
                    if k == eng and eng in INORDER_SAFE:
                        continue
                    if deps.get(k, 0) < v:
                        deps[k] = v
        kn = self.known[eng]
        out = []
        for k, v in deps.items():
            if kn.get(k, 0) >= v:
                continue
            kn[k] = v
            out.append((k, v))
        return out

    def _commit(self, tok, reads, writes):
        k, v = tok
        for t in reads:
            if t.r.get(k, 0) < v:
                t.r[k] = v
        for t in writes:
            t.w = {k: v}
            t.r = {}

    def _wait(self, e, k, v):
        if isinstance(k, tuple):
            e.wait_ge(self.sems[k], v)
            return
        self.waited.add((k, v))
        val = v if self.marked is None else self.rank_of[(k, v)]
        e.wait_ge(self.sems[k], val)

    def op(self, eng, fn, reads=(), writes=()):
        e = getattr(self.nc, eng)
        deps = self._deps(eng, reads, writes)
        if ACT_FENCE and eng == ACT and self.act_scratch is not None and any(isinstance(k, tuple) for k, v in deps):
            dma_deps = [(k, v) for k, v in deps if isinstance(k, tuple)]
            deps = [(k, v) for k, v in deps if not isinstance(k, tuple)]
            dv = getattr(self.nc, DVE)
            for k, v in dma_deps:
                if self.known[DVE].get(k, 0) < v:
                    self.known[DVE][k] = v
                    self._wait(dv, k, v)
            sc = self.act_scratch
            fins = dv.tensor_copy(out=sc[:, 0:4], in_=sc[:, 8:12])
            self.cnt[DVE] += 1
            fseq = self.cnt[DVE]
            if self.marked is None or (DVE, fseq) in self.marked:
                self.mrank[DVE] += 1
                self.rank_of[(DVE, fseq)] = self.mrank[DVE]
                fins.then_inc(self.sems[DVE], 1)
            if self.known[ACT].get(DVE, 0) < fseq:
                self.known[ACT][DVE] = fseq
                deps = [(k, v) for k, v in deps if k != DVE] + [(DVE, fseq)]
        for k, v in deps:
            self._wait(e, k, v)
        ins = fn(e)
        self.cnt[eng] += 1
        seq = self.cnt[eng]
        if self.marked is None or (eng, seq) in self.marked:
            self.mrank[eng] += 1
            self.rank_of[(eng, seq)] = self.mrank[eng]
            ins.then_inc(self.sems[eng], 1)
        self._commit((eng, seq), reads, writes)
        self.n_ins += 1

    def dma(self, out_t, out_ap, in_t, in_ap, queue=SP, **kw):
        if not STORES_ON_POOL:
            queue = SP
        e = getattr(self.nc, queue)
        k = self.dma_next
        self.dma_next = (k + 1) % self.n_dma
        key = ("dma", k)
        waits = self._deps(queue, [in_t], [out_t])
        if self.known[queue].get(key, 0) < self.dma_cnt[k]:
            self.known[queue][key] = self.dma_cnt[k]
            waits.append((key, self.dma_cnt[k]))
        for kk, v in waits:
            self._wait(e, kk, v)
        self.dma_cnt[k] += 16
        e.dma_start(out=out_ap, in_=in_ap, **kw).then_inc(self.sems[key], 16)
        self._commit((key, self.dma_cnt[k]), [in_t], [out_t])
        self.n_ins += 1

    def barrier(self, engines=ALLENG):
        cur = {e: self.cnt[e] for e in COMPUTE}
        for k in range(self.n_dma):
            cur[("dma", k)] = self.dma_cnt[k]
        for eng in engines:
            e = getattr(self.nc, eng)
            kn = self.known[eng]
            for k, v in cur.items():
                if v > 0 and kn.get(k, 0) < v:
                    kn[k] = v
                    self._wait(e, k, v)


class Ctx:
    pass


def build(depth=DEPTH, debug=False, phases="ABCDEF"):
    waited = _build(depth, debug, phases, None)[1]
    return _build(depth, debug, phases, waited)[0]


def _build(depth, debug, phases, marked):
    nc = bass.Bass("TRN2", target_bir_lowering=False)
    kind_dbg = "ExternalOutput" if debug else "Internal"

    def dram(name, shape, dt, kind="Internal"):
        return nc.dram_tensor(name, list(shape), dt, kind=kind).ap()

    x_in = dram("x", [S_LEN, D], F32, "ExternalInput")
    p_in = dram("p", [DEPTH, S_LEN, 256], F32, "ExternalInput")
    Wd = {}
    wshapes = {
        "w_in": [DEPTH, D, IN_COLS], "convp": [DEPTH, 32, 256], "conv_ln_g": [DEPTH, 256],
        "conv_ln_b": [DEPTH, 256], "cmp_pe_k": [DEPTH, 32, 64], "cmp_pe_v": [DEPTH, 32, 64],
        "cmp_w1_k": [DEPTH, 2048, 128], "cmp_w2_k": [DEPTH, 128, 64], "cmp_w1_v": [DEPTH, 2048, 128],
        "cmp_w2_v": [DEPTH, 128, 64], "lb_logits": [DEPTH, 256], "hgrn_norm_g": [DEPTH, 64],
        "w_out": [DEPTH, D, D], "ln1_g": [DEPTH, D], "ln1_b": [DEPTH, D], "w_up": [DEPTH, D, 2 * D_FF],
        "ffnp": [DEPTH, 4, 2 * D_FF], "w_down": [DEPTH, D_FF, D], "w_ple_gate": [DEPTH, D, D],
        "w_ple_proj": [DEPTH, 256, D], "ln2_g": [DEPTH, D], "ln2_b": [DEPTH, D],
    }
    for k, shp in wshapes.items():
        Wd[k] = dram(k, shp, F32, "ExternalInput")
    y_out = dram("y", [S_LEN, D], F32, "ExternalOutput")

    hres = dram("hres", [S_LEN, D], F32, kind_dbg)
    convT_d = dram("convT_d", [4, 128, S_LEN], BF16)
    QT_d = dram("QT_d", [8, 64, S_LEN], BF16)
    KT_d = dram("KT_d", [8, 64, S_LEN], BF16)
    HQ_d = dram("HQ_d", [4, 64, S_LEN], F32)
    HF_d = dram("HF_d", [4, 64, S_LEN], F32)
    Vtm_d = dram("Vtm_d", [S_LEN, 4, 65], BF16)
    Gtm_d = dram("Gtm_d", [S_LEN, 24], F32)
    Htm_d = dram("Htm_d", [S_LEN, 896], F32)
    mix_d = dram("mix_d", [S_LEN, D], BF16, kind_dbg)
    h1_d = dram("h1_d", [S_LEN, D], F32, kind_dbg)
    hT1_d = dram("hT1_d", [8, 128, S_LEN], BF16)
    g_d = dram("g_d", [NFF, 128, S_LEN], BF16)

    dtiles = {}

    def DT(name, idx=0):
        key = (name, idx)
        if key not in dtiles:
            dtiles[key] = T(None)
        return dtiles[key]

    with ExitStack() as top:
        top.enter_context(nc.allow_low_precision("bf16 matmul operands, fp32 accumulation"))
        top.enter_context(nc.allow_non_contiguous_dma("small parameter loads"))
        S = Sched(nc, top, marked=marked)
        C = Ctx()

        uid = [0]

        def sb(stack, name, shape, dt):
            uid[0] += 1
            return T(stack.enter_context(nc.sbuf_tensor("%s_%d" % (name, uid[0]), list(shape), dt)))

        psum = [T(top.enter_context(nc.psum_tensor("ps%d" % i, [128, 512], F32)), psum=True) for i in range(8)]
        ps_i = [0]

        def PS():
            t = psum[ps_i[0] % 8]
            ps_i[0] += 1
            return t

        rr = [0]

        def evac_eng():
            rr[0] += 1
            return ACT if rr[0] % 2 == 0 else DVE

        def copy(eng, out_t, out_ap, in_t, in_ap):
            if eng == ACT:
                S.op(ACT, lambda e: e.activation(out=out_ap, in_=in_ap, func=AF.Copy), [in_t], [out_t])
            else:
                S.op(eng, lambda e: e.tensor_copy(out=out_ap, in_=in_ap), [in_t], [out_t])

        act_sc = sb(top, "act_sc", [128, 16], F32)
        S.op(POOL, lambda e: e.memset(act_sc[:, :], 0.0), [], [act_sc])
        S.barrier()
        S.act_scratch = act_sc
        ones_f = sb(top, "ones_f", [128, 512], F32)
        S.op(POOL, lambda e: e.memset(ones_f[:, :], 1.0), [], [ones_f])
        zeros_f = sb(top, "zeros_f", [128, 512], F32)
        S.op(POOL, lambda e: e.memset(zeros_f[:, :], 0.0), [], [zeros_f])
        ident_f = sb(top, "ident_f", [128, 128], F32)
        S.op(POOL, lambda e: e.affine_select(out=ident_f[:, :], in_=ones_f[:, 0:128], pattern=[[-1, 128]],
                                             compare_op=ALU.is_equal, fill=0.0, base=0, channel_multiplier=1),
             [ones_f], [ident_f])
        ident_b = sb(top, "ident_b", [128, 128], BF16)
        copy(DVE, ident_b, ident_b[:, :], ident_f, ident_f[:, :])

        def load_bf16(stack, name, dram_ap, kt, ncols, stage_cols=512):
            dst = sb(stack, name, [128, kt, ncols], BF16)
            src = dram_ap.rearrange("(k p) n -> p k n", p=128)
            with ExitStack() as st:
                stg = [sb(st, name + "_stg%d" % i, [128, kt, stage_cols], F32) for i in range(2)]
                i = 0
                for c0 in range(0, ncols, stage_cols):
                    n = min(stage_cols, ncols - c0)
                    s = stg[i % 2]
                    S.dma(s, s[:, :, 0:n], DT(name + "_src"), src[:, :, c0:c0 + n])
                    eng = (DVE, ACT)[i % 2]
                    copy(eng, dst, dst[:, :, c0:c0 + n], s, s[:, :, 0:n])
                    i += 1
                S.barrier()
            return dst

        def transposes_bf(dst_t, dst_ap, src_t, src_ap_fn, n):
            ps = PS()
            pb = ps.ap.bitcast(BF16)
            for i in range(n):
                S.op(PE, lambda e, i=i: e.transpose(out=pb[:, i * 128:(i + 1) * 128], in_=src_ap_fn(i),
                                                    identity=ident_b[:, :]), [src_t, ident_b], [ps])
            eng = evac_eng()
            copy(eng, dst_t, dst_ap, ps, pb[:, 0:n * 128].rearrange("p (k t) -> p k t", k=n))

        def layernorm(stack_tiles, z, width, g_bc, b_bc, out_t, out_ap):
            st6, mv, rstd = stack_tiles
            nchunk = width // 256 if width > 512 else 1
            cw = width // nchunk
            for i in range(nchunk):
                S.op(DVE, lambda e, i=i: e.bn_stats(out=st6[:, i * 6:(i + 1) * 6], in_=z[:, i * cw:(i + 1) * cw]),
                     [z], [st6])
            S.op(DVE, lambda e: e.bn_aggr(out=mv[:, :], in_=st6[:, 0:nchunk * 6]), [st6], [mv])
            S.op(DVE, lambda e: e.tensor_scalar(out=rstd[:, :], in0=mv[:, 1:2], scalar1=LN_EPS, scalar2=None,
                                                op0=ALU.add), [mv], [rstd])
            S.op(ACT, lambda e: e.activation(out=rstd[:, :], in_=rstd[:, :], func=AF.Sqrt), [rstd], [rstd])
            S.op(DVE, lambda e: e.reciprocal(out=rstd[:, :], in_=rstd[:, :]), [rstd], [rstd])
            S.op(DVE, lambda e: e.tensor_scalar(out=z[:, 0:width], in0=z[:, 0:width], scalar1=mv[:, 0:1],
                                                scalar2=rstd[:, 0:1], op0=ALU.subtract, op1=ALU.mult),
                 [z, mv, rstd], [z])
            S.op(POOL, lambda e: e.tensor_tensor(out=z[:, 0:width], in0=z[:, 0:width], in1=g_bc[:, 0:width],
                                                 op=ALU.mult), [z, g_bc], [z])
            S.op(POOL, lambda e: e.tensor_tensor(out=out_ap, in0=z[:, 0:width], in1=b_bc[:, 0:width],
                                                 op=ALU.add), [z, b_bc], [out_t])

        C.nc, C.S, C.sb, C.PS, C.copy, C.evac_eng, C.DT = nc, S, sb, PS, copy, evac_eng, DT
        C.load_bf16, C.transposes_bf, C.layernorm = load_bf16, transposes_bf, layernorm
        C.ident_f, C.ident_b, C.ones_f, C.zeros_f = ident_f, ident_b, ones_f, zeros_f
        C.Wd, C.x_in, C.p_in, C.y_out = Wd, x_in, p_in, y_out
        C.psum_banks = psum
        C.scr = dict(hres=hres, convT_d=convT_d, QT_d=QT_d, KT_d=KT_d, HQ_d=HQ_d, HF_d=HF_d, Vtm_d=Vtm_d,
                     Gtm_d=Gtm_d, Htm_d=Htm_d, mix_d=mix_d, h1_d=h1_d, hT1_d=hT1_d, g_d=g_d)
        S.barrier()

        for L in range(depth):
            h_src = x_in if L == 0 else hres
            h_dst = y_out if L == depth - 1 else hres
            def run_bd():
                gb = phase_conv(C, L) if "B" in phases else iter(())
                next(gb, None)
                gd = phase_hgrn(C, L) if "D" in phases else iter(())
                next(gd, None)
                for c in range(NCH):
                    next(gb, None)
                    for t in range(4):
                        next(gd, None)
                for _ in gd:
                    pass
                for _ in gb:
                    pass

            for nm, fn, args in (("A", phase_proj, (h_src,)), ("BD", run_bd, None), ("C", phase_nsa, ()),
                                 ("E", phase_ffn1, (h_src,)), ("F", phase_ffn2, (h_dst,))):
                if nm == "BD":
                    if "B" in phases or "D" in phases:
                        fn()
                        S.barrier()
                elif nm in phases:
                    fn(C, L, *args)
                    S.barrier()
        S.barrier()
    return nc, S.waited


def phase_proj(C, L, h_src):
    nc, S, sb, PS, copy, DT = C.nc, C.S, C.sb, C.PS, C.copy, C.DT
    scr = C.scr
    with ExitStack() as ph:
        win = C.load_bf16(ph, "win", C.Wd["w_in"][L], 8, IN_COLS, stage_cols=568)
        hraw = [sb(ph, "hraw%d" % i, [128, D], F32) for i in range(2)]
        hb = [sb(ph, "hb%d" % i, [128, D], BF16) for i in range(2)]
        hT = [sb(ph, "hT%d" % i, [128, 8, 512], BF16) for i in range(2)]
        st_conv = [sb(ph, "st_conv%d" % i, [128, 4, 512], BF16) for i in range(2)]
        st_q = [sb(ph, "st_q%d" % i, [64, 8, 512], BF16) for i in range(2)]
        st_k = [sb(ph, "st_k%d" % i, [64, 8, 512], BF16) for i in range(2)]
        st_hq = [sb(ph, "st_hq%d" % i, [64, 4, 512], F32) for i in range(2)]
        st_hf = [sb(ph, "st_hf%d" % i, [64, 4, 512], F32) for i in range(2)]
        st_v = [sb(ph, "st_v%d" % i, [128, 4, 65], BF16) for i in range(2)]
        for i in range(2):
            S.op(POOL, lambda e, i=i: e.memset(st_v[i][:, :, :], 1.0), [], [st_v[i]])
        st_g = [sb(ph, "st_g%d" % i, [128, 24], F32) for i in range(2)]
        st_h = [sb(ph, "st_h%d" % i, [128, 896], F32) for i in range(2)]
        for i in range(2):
            S.op(POOL, lambda e, i=i: e.memset(st_h[i][:, 768:896], 0.0), [], [st_h[i]])

        KV0 = 1024
        fm = []
        for j in range(4):
            fm.append((j * 128, 128, st_conv, j))
        for h in range(8):
            fm.append((512 + h * 64, 64, st_q, h))
        kvsel = [(0, 0), (0, 1), (1, 0), (1, 1), (2, 0), (2, 1), (4, 0), (4, 1)]
        for n, (j, g) in enumerate(kvsel):
            fm.append((KV0 + (j * 2 + g) * 64, 64, st_k, n))
        for h in range(4):
            fm.append((1816 + h * 64, 64, st_hq, h))
        for h in range(4):
            fm.append((1816 + 256 + h * 64, 64, st_hf, h))

        def prep_chunk(c):
            b = c % 2
            for t in range(4):
                tt = c * 4 + t
                hr, hbb = hraw[tt % 2], hb[tt % 2]
                S.dma(hr, hr[:, :], DT("h", tt), h_src[tt * 128:(tt + 1) * 128, :])
                copy(DVE, hbb, hbb[:, :], hr, hr[:, :])
                C.transposes_bf(hT[b], hT[b][:, :, t * 128:(t + 1) * 128], hbb,
                                lambda i, hbb=hbb: hbb[:, i * 128:(i + 1) * 128], 8)

        prep_chunk(0)
        for c in range(NCH):
            b = c % 2
            if DBGA < 2:
                continue
            for (c0, n, stg, slot) in fm:
                ps = PS()
                for kt in range(8):
                    S.op(PE, lambda e, kt=kt, ps=ps, c0=c0, n=n: e.matmul(
                        ps[0:n, :], lhsT=win[:, kt, c0:c0 + n], rhs=hT[b][:, kt, :], start=(kt == 0), stop=(kt == 7)),
                        [win, hT[b]], [ps])
                copy(C.evac_eng(), stg[b], stg[b][0:n, slot, :], ps, ps[0:n, :])
            sl = slice(c * 512, (c + 1) * 512)
            if c + 1 < NCH:
                prep_chunk(c + 1)
            if DBGA < 3:
                continue
            S.dma(DT("convT", c), scr["convT_d"][:, :, sl].rearrange("j p t -> p j t"), st_conv[b], st_conv[b][:, :, :], queue=POOL)
            S.dma(DT("QT", c), scr["QT_d"][:, :, sl].rearrange("j p t -> p j t"), st_q[b], st_q[b][:, :, :], queue=POOL)
            S.dma(DT("KT", c), scr["KT_d"][:, :, sl].rearrange("j p t -> p j t"), st_k[b], st_k[b][:, :, :], queue=POOL)
            S.dma(DT("HQ", c), scr["HQ_d"][:, :, sl].rearrange("j p t -> p j t"), st_hq[b], st_hq[b][:, :, :], queue=POOL)
            S.dma(DT("HF", c), scr["HF_d"][:, :, sl].rearrange("j p t -> p j t"), st_hf[b], st_hf[b][:, :, :], queue=POOL)
            if DBGA < 4:
                continue
            for t in range(4):
                tt = c * 4 + t
                b2 = tt % 2
                rows = slice(tt * 128, (tt + 1) * 128)
                groups = [(1408, 128), (1664, 152), (2072, 512), (2584, 256)]
                pss = []
                for (c0, n) in groups:
                    ps = PS()
                    pss.append(ps)
                    for kt in range(8):
                        S.op(PE, lambda e, kt=kt, ps=ps, c0=c0, n=n: e.matmul(
                            ps[:, 0:n], lhsT=hT[b][:, kt, t * 128:(t + 1) * 128], rhs=win[:, kt, c0:c0 + n],
                            start=(kt == 0), stop=(kt == 7)), [win, hT[b]], [ps])
                sv, sg, sh = st_v[b2], st_g[b2], st_h[b2]
                if DBGA < 5:
                    continue
                copy(DVE, sv, sv[:, 0:2, 0:64], pss[0], pss[0][:, 0:128].rearrange("p (n d) -> p n d", n=2))
                copy(DVE, sv, sv[:, 2:4, 0:64], pss[1], pss[1][:, 0:128].rearrange("p (n d) -> p n d", n=2))
                if DBGX == 0:
                    S.op(ACT, lambda e, sh=sh, ps=pss[1]: e.activation(out=sh[:, 768:792], in_=ps[:, 128:152], func=AF.Sigmoid),
                         [pss[1]], [sh])
                else:
                    S.op(ACT, lambda e, sg=sg, ps=pss[1]: e.activation(out=sg[:, :], in_=ps[:, 128:152], func=AF.Sigmoid),
                         [pss[1]], [sg])
                copy(ACT if DBGX < 2 else DVE, sh, sh[:, 0:512], pss[2], pss[2][:, 0:512])
                if DBGX != 0:
                    copy(DVE, sh, sh[:, 768:792], sg, sg[:, :])
                copy(DVE, sh, sh[:, 512:768], pss[3], pss[3][:, 0:256])
                if DBGA >= 6:
                    S.dma(DT("Vtm", tt), scr["Vtm_d"][rows, :, :], sv, sv[:, :, :], queue=POOL)
                if DBGA >= 8:
                    S.dma(DT("Htm", tt), scr["Htm_d"][rows, :], sh, sh[:, :], queue=POOL)


def small_T(C, stack, name, src_dram_ap, rows, cols):
    S, sb, PS = C.S, C.sb, C.PS
    nblk = cols // 128
    dst = sb(stack, name, [128, nblk, rows], F32)
    with ExitStack() as tmp:
        src = sb(tmp, name + "_src", [rows, cols], F32)
        S.dma(src, src[:, :], C.DT(name + "_d"), src_dram_ap)
        per = 512 // rows
        j = 0
        while j < nblk:
            n = min(per, nblk - j)
            ps = PS()
            for i in range(n):
                S.op(PE, lambda e, i=i, j=j, ps=ps: e.transpose(out=ps[:, i * rows:(i + 1) * rows],
                                                              in_=src[:, (j + i) * 128:(j + i + 1) * 128],
                                                              identity=C.ident_f[0:rows, 0:rows]), [src, C.ident_f], [ps])
            C.copy(DVE, dst, dst[:, j:j + n, :], ps, ps[:, 0:n * rows].rearrange("p (k r) -> p k r", k=n))
            j += n
        S.barrier()
    return dst


def bc_load(C, stack, name, dram_row_ap, width):
    t = C.sb(stack, name, [128, width], F32)
    C.S.dma(t, t[:, :], C.DT(name + "_d"), dram_row_ap.partition_broadcast(128))
    return t


def phase_conv(C, L):
    nc, S, sb, PS, copy, DT = C.nc, C.S, C.sb, C.PS, C.copy, C.DT
    scr = C.scr
    with ExitStack() as ph:
        cw = small_T(C, ph, "cw", C.Wd["convp"][L], 32, 256)
        g_bc = bc_load(C, ph, "cln_g", C.Wd["conv_ln_g"][L], 256)
        b_bc = bc_load(C, ph, "cln_b", C.Wd["conv_ln_b"][L], 256)
        dg = sb(ph, "dg", [128, 2, 31, 128], BF16)
        n = 0
        for j in range(2):
            for k in range(31):
                eng = DVE if n % 2 == 0 else POOL
                n += 1
                S.op(eng, lambda e, j=j, k=k: e.tensor_scalar(out=dg[:, j, k, :], in0=C.ident_f[:, :],
                                                               scalar1=cw[:, j, k:k + 1], scalar2=None, op0=ALU.mult),
                     [C.ident_f, cw], [dg])
        cin = [sb(ph, "cin%d" % i, [128, 4, 542], BF16) for i in range(2)]
        sg = [sb(ph, "csg%d" % i, [128, 2, 542], BF16) for i in range(2)]
        glu = [sb(ph, "glu%d" % i, [128, 2, 542], BF16) for i in range(2)]
        cT = [sb(ph, "cT%d" % i, [128, 512], F32) for i in range(2)]
        z = [sb(ph, "cz%d" % i, [128, 256], F32) for i in range(2)]
        z2 = [sb(ph, "cz2%d" % i, [128, 256], F32) for i in range(2)]
        yb = [sb(ph, "cy%d" % i, [128, 256], BF16) for i in range(2)]
        st6 = sb(ph, "cst6", [128, 24], F32)
        mv = sb(ph, "cmv", [128, 2], F32)
        rstd = sb(ph, "crstd", [128, 1], F32)
        for i in range(2):
            S.op(POOL, lambda e, i=i: e.memset(cin[i][:, :, 0:30], 0.0), [], [cin[i]])
        yield
        for c in range(NCH):
            b = c % 2
            ci = cin[b]
            if c == 0:
                S.dma(ci, ci[:, :, 30:542], DT("convT", 0), scr["convT_d"][:, :, 0:512].rearrange("j p t -> p j t"))
            else:
                S.dma(ci, ci[:, :, :], DT("convT", c),
                      scr["convT_d"][:, :, c * 512 - 30:(c + 1) * 512].rearrange("j p t -> p j t"))
            S.op(ACT, lambda e: e.activation(out=sg[b][:, :, :], in_=ci[:, 2:4, :], func=AF.Sigmoid), [ci], [sg[b]])
            S.op(DVE, lambda e: e.tensor_tensor(out=glu[b][:, :, :], in0=ci[:, 0:2, :], in1=sg[b][:, :, :], op=ALU.mult),
                 [ci, sg[b]], [glu[b]])
            cts = []
            for j in range(2):
                ps = PS()
                for k in range(31):
                    S.op(PE, lambda e, j=j, k=k, ps=ps: e.matmul(ps[:, :], lhsT=dg[:, j, k, :], rhs=glu[b][:, j, k:k + 512],
                                                                   start=(k == 0), stop=(k == 30)), [dg, glu[b]], [ps])
                ct = cT[j]
                S.op(ACT, lambda e, ps=ps, ct=ct, j=j: e.activation(out=ct[:, :], in_=ps[:, :], func=AF.Identity,
                                                                    bias=cw[:, j, 31:32]), [ps, cw], [ct])
                cts.append(ct)
            for t in range(4):
                tt = c * 4 + t
                zz, zz2, yy = z[tt % 2], z2[tt % 2], yb[tt % 2]
                ps = PS()
                for j in range(2):
                    S.op(PE, lambda e, j=j, ps=ps: e.transpose(out=ps[:, j * 128:(j + 1) * 128],
                                                                in_=cts[j][:, t * 128:(t + 1) * 128],
                                                                identity=C.ident_f[:, :]), [cts[j], C.ident_f], [ps])
                copy(ACT, zz, zz[:, :], ps, ps[:, 0:256])
                C.layernorm((st6, mv, rstd), zz, 256, g_bc, b_bc, zz2, zz2[:, :])
                S.op(ACT, lambda e: e.activation(out=yy[:, :], in_=zz2[:, :], func=AF.Silu), [zz2], [yy])
                S.dma(DT("mixA", tt), scr["mix_d"][tt * 128:(tt + 1) * 128, 0:256], yy, yy[:, :], queue=POOL)
            yield


def phase_nsa(C, L):
    nc, S, sb, PS, copy, DT = C.nc, C.S, C.sb, C.PS, C.copy, C.DT
    scr = C.scr
    ident_b, ident_f, ones_f, zeros_f = C.ident_b, C.ident_f, C.ones_f, C.zeros_f
    with ExitStack() as ph:
        zb = sb(ph, "zb", [128, 2176], BF16)
        S.op(POOL, lambda e: e.memset(zb[:, :], 0.0), [], [zb])
        ob = sb(ph, "ob", [128, 512], BF16)
        S.op(POOL, lambda e: e.memset(ob[:, :], 1.0), [], [ob])
        CM4 = sb(ph, "CM4", [128, 4, 128], BF16)
        S.op(POOL, lambda e: e.affine_select(out=CM4[:, :, :], in_=zb[:, 0:512].rearrange("p (h q) -> p h q", h=4),
                                             pattern=[[0, 4], [1, 128]], compare_op=ALU.is_ge, fill=NEG, base=0,
                                             channel_multiplier=-1), [zb], [CM4])
        WM4 = sb(ph, "WM4", [128, 4, 128], BF16)
        S.op(POOL, lambda e: e.affine_select(out=WM4[:, :, :], in_=zb[:, 0:512].rearrange("p (h q) -> p h q", h=4),
                                             pattern=[[0, 4], [-1, 128]], compare_op=ALU.is_gt, fill=NEG, base=0,
                                             channel_multiplier=1), [zb], [WM4])
        Mtab = sb(ph, "Mtab", [128, 2176], BF16)
        S.op(POOL, lambda e: e.affine_select(out=Mtab[:, :], in_=zb[:, :], pattern=[[1, 2176]], compare_op=ALU.is_ge,
                                             fill=NEG, base=-31, channel_multiplier=-16), [zb], [Mtab])
        Ebig = sb(ph, "Ebig", [64, 4096], BF16)
        Etmp = sb(ph, "Etmp", [64, 4096], BF16)
        S.op(POOL, lambda e: e.memset(Etmp[:, :], 1.0), [], [Etmp])
        S.op(POOL, lambda e: e.affine_select(out=Ebig[:, :], in_=Etmp[:, :], pattern=[[1, 4096]], compare_op=ALU.is_ge,
                                             fill=0.0, base=0, channel_multiplier=-64), [Etmp], [Ebig])
        S.op(POOL, lambda e: e.affine_select(out=Etmp[:, :], in_=Ebig[:, :], pattern=[[-1, 4096]], compare_op=ALU.is_ge,
                                             fill=0.0, base=63, channel_multiplier=64), [Ebig], [Etmp])
        Ebig = Etmp
        Cm = []
        for ct in range(2):
            c1 = sb(ph, "Cm_a%d" % ct, [128, 64], BF16)
            c2 = sb(ph, "Cm_b%d" % ct, [128, 64], BF16)
            S.op(POOL, lambda e, ct=ct, c1=c1: e.affine_select(out=c1[:, :], in_=ob[:, 0:64], pattern=[[-4, 64]],
                                                               compare_op=ALU.is_ge, fill=0.0, base=128 * ct + 1,
                                                               channel_multiplier=1), [ob], [c1])
            S.op(POOL, lambda e, ct=ct, c1=c1, c2=c2: e.affine_select(out=c2[:, :], in_=c1[:, :], pattern=[[4, 64]],
                                                                      compare_op=ALU.is_ge, fill=0.0, base=3 - 128 * ct,
                                                                      channel_multiplier=-1), [c1], [c2])
            Cm.append(c2)
        BON = sb(ph, "BON", [128, 3], F32)
        S.op(POOL, lambda e: e.memset(BON[0:64, 0:1], 2e9), [], [BON])
        S.op(POOL, lambda e: e.memset(BON[0:64, 1:2], 3e9), [], [BON])
        S.op(POOL, lambda e: e.memset(BON[0:64, 2:3], -1e9), [], [BON])
        S.op(POOL, lambda e: e.memset(BON[64:128, 0:1], 0.0), [], [BON])
        S.op(POOL, lambda e: e.memset(BON[64:128, 1:2], 2e9), [], [BON])
        S.op(POOL, lambda e: e.memset(BON[64:128, 2:3], 3e9), [], [BON])

        KcT = sb(ph, "KcT", [64, 2, 256], BF16)
        S.op(POOL, lambda e: e.memset(KcT[:, :, :], 0.0), [], [KcT])
        Vc = sb(ph, "Vc", [128, 2, 2, 65], BF16)
        S.op(POOL, lambda e: e.memset(Vc[:, :, :, :], 1.0), [], [Vc])
        with ExitStack() as cs:
            w1s = sb(cs, "w1s", [64, 32, 128], F32)
            w2s = sb(cs, "w2s", [128, 64], F32)
            pes = sb(cs, "pes", [64, 32], F32)
            peraw = sb(cs, "peraw", [32, 64], F32)
            peb = sb(cs, "peb", [64, 32], BF16)
            w1 = sb(cs, "w1", [64, 32, 128], BF16)
            w2 = sb(cs, "w2", [128, 64], BF16)
            bias = sb(cs, "cbias", [128, 1], F32)
            kc = sb(cs, "kc", [64, S_LEN], BF16)
            xh = sb(cs, "xh", [128, 255], F32)
            x2 = sb(cs, "x2", [128, 255], F32)
            hid = sb(cs, "hid", [128, 256], BF16)
            for kind, (w1n, w2n, pen) in enumerate([("cmp_w1_k", "cmp_w2_k", "cmp_pe_k"),
                                                   ("cmp_w1_v", "cmp_w2_v", "cmp_pe_v")]):
                for l4 in range(4):
                    S.dma(w1s, w1s[:, l4 * 8:(l4 + 1) * 8, :], DT(w1n),
                          C.Wd[w1n][L].rearrange("(l d) m -> d l m", d=64)[:, l4 * 8:(l4 + 1) * 8, :])
                S.dma(w2s, w2s[:, :], DT(w2n), C.Wd[w2n][L])
                S.dma(peraw, peraw[:, :], DT(pen), C.Wd[pen][L])
                pspe = PS()
                S.op(PE, lambda e, pspe=pspe: e.transpose(out=pspe[0:64, 0:32], in_=peraw[:, :], identity=ident_f[0:32, 0:32]),
                     [peraw, ident_f], [pspe])
                copy(DVE, pes, pes[:, :], pspe, pspe[0:64, 0:32])
                copy(DVE, w1, w1[:, :, :], w1s, w1s[:, :, :])
                copy(POOL, w2, w2[:, :], w2s, w2s[:, :])
                copy(POOL, peb, peb[:, :], pes, pes[:, :])
                ps = PS()
                for l in range(32):
                    S.op(PE, lambda e, l=l, ps=ps: e.matmul(ps[:, 0:1], lhsT=w1[:, l, :], rhs=peb[:, l:l + 1],
                                                            start=(l == 0), stop=(l == 31)), [w1, peb], [ps])
                copy(DVE, bias, bias[:, :], ps, ps[:, 0:1])
                for g in range(2):
                    S.dma(kc, kc[:, :], DT("KT_all"), scr["KT_d"][kind * 2 + g, :, :])
                    kv = kc.ap[:, :].rearrange("p (i r) -> p i r", r=16)
                    ps = PS()
                    for l in range(32):
                        rhs = kv[:, 0:255, l] if l < 16 else kv[:, 1:256, l - 16]
                        S.op(PE, lambda e, l=l, ps=ps, rhs=rhs: e.matmul(ps[:, 0:255], lhsT=w1[:, l, :], rhs=rhs,
                                                                         start=(l == 0), stop=(l == 31)), [w1, kc], [ps])
                    S.op(ACT, lambda e, ps=ps: e.activation(out=xh[:, :], in_=ps[:, 0:255], func=AF.Identity,
                                                            bias=bias[:, 0:1]), [ps, bias], [xh])
                    S.op(DVE, lambda e: e.tensor_tensor(out=x2[:, :], in0=xh[:, :], in1=xh[:, :], op=ALU.mult), [xh], [x2])
                    S.op(DVE, lambda e: e.tensor_scalar(out=x2[:, :], in0=x2[:, :], scalar1=0.044715, scalar2=1.0,
                                                        op0=ALU.mult, op1=ALU.add), [x2], [x2])
                    S.op(DVE, lambda e: e.tensor_tensor(out=x2[:, :], in0=x2[:, :], in1=xh[:, :], op=ALU.mult), [x2, xh], [x2])
                    S.op(ACT, lambda e: e.activation(out=x2[:, :], in_=x2[:, :], func=AF.Sigmoid, scale=1.5957691216057308),
                         [x2], [x2])
                    S.op(POOL, lambda e: e.memset(hid[:, 255:256], 0.0), [], [hid])
                    S.op(DVE, lambda e: e.tensor_tensor(out=hid[:, 0:255], in0=x2[:, :], in1=xh[:, :], op=ALU.mult),
                         [x2, xh], [hid])
                    if kind == 0:
                        ps2 = PS()
                        S.op(PE, lambda e, ps2=ps2: e.matmul(ps2[0:64, 0:255], lhsT=w2[:, :], rhs=hid[:, 0:255],
                                                              start=True, stop=True), [w2, hid], [ps2])
                        copy(ACT, KcT, KcT[:, g, 0:255], ps2, ps2[0:64, 0:255])
                    else:
                        for ct in range(2):
                            n = 128 if ct == 0 else 127
                            ps2 = PS()
                            S.op(PE, lambda e, ps2=ps2, ct=ct, n=n: e.matmul(ps2[0:n, 0:64], lhsT=hid[:, ct * 128:ct * 128 + n],
                                                                              rhs=w2[:, :], start=True, stop=True), [w2, hid], [ps2])
                            copy(ACT, Vc, Vc[0:n, ct, g, 0:64], ps2, ps2[0:n, 0:64])
            S.barrier()

        KTs = sb(ph, "KTs", [64, 4, S_LEN], BF16)
        S.dma(KTs, KTs[:, :, :], DT("KT_all"), scr["KT_d"][4:8, :, :].rearrange("j p t -> p j t"))
        Vaug = sb(ph, "Vaug", [128, NT, 260], BF16)
        gq = [sb(ph, "gq%d" % i, [128, 128], F32) for i in range(2)]
        for q4 in range(4):
            S.dma(Vaug, Vaug[:, q4 * 8:(q4 + 1) * 8, :], DT("Vtm_all"),
                  scr["Vtm_d"][q4 * 1024:(q4 + 1) * 1024, :, :].rearrange("(t p) n d -> p t (n d)", p=128))
        QTc = [sb(ph, "QTc%d" % i, [64, 8, 512], BF16) for i in range(2)]
        pT = [sb(ph, "pT%d" % i, [128, 4, 128], BF16) for i in range(3)]
        pT_i = [0]
        from_bank = lambda i: C.psum_banks[i]
        po = [from_bank(0), from_bank(1), from_bank(2)]
        IMP = from_bank(3)
        psT = from_bank(4)
        sc_banks = [from_bank(5), from_bank(6), from_bank(7)]
        sc_i = [0]
        rd = [sb(ph, "rd%d" % i, [128, 4], F32) for i in range(3)]
        rdg = [sb(ph, "rdg%d" % i, [128, 4], F32) for i in range(3)]
        acc = [sb(ph, "acc%d" % i, [128, 8, 64], F32) for i in range(2)]
        tmpo = [sb(ph, "tmpo%d" % i, [128, 4, 64], F32) for i in range(2)]
        ybf = [sb(ph, "ynsa%d" % i, [128, 512], BF16) for i in range(2)]
        imp = sb(ph, "imp", [128, 64], F32)
        imp2 = sb(ph, "imp2", [128, 64], F32)
        m8a = sb(ph, "m8a", [128, 8], F32)
        m8b = sb(ph, "m8b", [128, 8], F32)
        negm = sb(ph, "negm", [128, 64], F32)
        S.op(POOL, lambda e: e.memset(negm[:, :], 0.0), [], [negm])
        negT4 = [sb(ph, "negT4_%d" % i, [64, 4, 128], BF16) for i in range(2)]

        def score_tile(kT_ap, kT_t, rhsQ, qt_t, masks):
            ps = sc_banks[sc_i[0] % 3]
            sc_i[0] += 1
            out3 = ps[:, :].rearrange("p (h q) -> p h q", h=4)
            S.op(PE, lambda e: e.matmul(out3, lhsT=kT_ap, rhs=rhsQ, start=True, stop=(len(masks) == 0),
                                        skip_group_check=True), [kT_t, qt_t], [ps])
            for mi, (l_ap, l_t, r_ap, r_t, osl) in enumerate(masks):
                last = (mi == len(masks) - 1)
                o = out3 if osl is None else ps[:, osl]
                S.op(PE, lambda e, l_ap=l_ap, r_ap=r_ap, o=o, last=last: e.matmul(o, lhsT=l_ap, rhs=r_ap, start=False, stop=last,
                                                                                   skip_group_check=True), [l_t, r_t], [ps])
            p = pT[pT_i[0] % 3]
            pT_i[0] += 1
            S.op(ACT, lambda e: e.activation(out=p[:, :, :], in_=out3, func=AF.Exp, scale=0.125), [ps], [p])
            return p

        def pv(p, bank, v_ap, v_t, first, extra=None):
            for h in range(4):
                S.op(PE, lambda e, h=h: e.matmul(bank[:, h * 65:(h + 1) * 65], lhsT=p[:, h, :], rhs=v_ap,
                                                 start=(first and h == 0), stop=False, skip_group_check=True), [p, v_t], [bank])

        pend = []

        def defer(fn):
            if len(pend) >= 2:
                pend.pop(0)()
            pend.append(fn)

        def flush():
            while pend:
                pend.pop(0)()

        for qt in range(NT):
            c, t = qt // 4, qt % 4
            if t == 0:
                qc = QTc[c % 2]
                S.dma(qc, qc[:, :, :], DT("QT", c), scr["QT_d"][:, :, c * 512:(c + 1) * 512].rearrange("j p t -> p j t"))
            qc = QTc[c % 2]
            Gall = gq[qt % 2]
            S.dma(Gall, Gall[:, :], DT("Htm", qt), scr["Htm_d"][qt * 128:(qt + 1) * 128, 768:896])
            ac = acc[qt % 2]
            for g in range(2):
                rhsQ = qc[:, 4 * g:4 * g + 4, t * 128:(t + 1) * 128]
                use_topk = qt >= 8
                cts = [0] if qt < 16 else [0, 1]
                for ci, ct in enumerate(cts):
                    delta = qt - 16 * ct
                    masks = []
                    if delta <= 16:
                        for h in range(4):
                            masks.append((ident_b[:, :], ident_b, Mtab[:, 128 * delta:128 * delta + 128], Mtab,
                                          slice(h * 128, (h + 1) * 128)))
                    p = score_tile(KcT[:, g, ct * 128:(ct + 1) * 128], KcT, rhsQ, qc, masks)

                    def cmp_pv(p=p, ct=ct, ci=ci):
                        pv(p, po[0], Vc[:, ct, g, :], Vc, ci == 0)
                        if use_topk:
                            for h in range(4):
                                S.op(PE, lambda e, h=h: e.matmul(IMP[:, h * 64:(h + 1) * 64], lhsT=p[:, h, :],
                                                                 rhs=Cm[ct][:, :], start=(ci == 0 and h == 0),
                                                                 stop=False, skip_group_check=True), [p, Cm[ct]], [IMP])
                    defer(cmp_pv)
                flush()
                po3 = [po[b][:, 0:260].rearrange("p (h e) -> p h e", h=4) for b in range(3)]
                S.op(DVE, lambda e: e.tensor_scalar(out=rd[0][:, :], in0=po3[0][:, :, 64], scalar1=1e-30, scalar2=None,
                                                    op0=ALU.add), [po[0]], [rd[0]])
                S.op(DVE, lambda e: e.reciprocal(out=rd[0][:, :], in_=rd[0][:, :]), [rd[0]], [rd[0]])
                nT = None
                if use_topk:
                    W = 2 * qt + 2
                    S.op(DVE, lambda e: e.tensor_scalar(out=imp[:, 0:W], in0=IMP[:, 0:W], scalar1=rd[0][:, 0:1], scalar2=None,
                                                        op0=ALU.mult), [IMP, rd[0]], [imp])
                    for h in range(1, 4):
                        S.op(DVE, lambda e, h=h: e.scalar_tensor_tensor(out=imp[:, 0:W], in0=IMP[:, h * 64:h * 64 + W],
                                                                        scalar=rd[0][:, h:h + 1], in1=imp[:, 0:W],
                                                                        op0=ALU.mult, op1=ALU.add), [IMP, rd[0], imp], [imp])
                    S.op(DVE, lambda e: e.memset(imp[:, 0:1], 4e9), [], [imp])
                    S.op(DVE, lambda e: e.tensor_tensor(out=imp[:, 2 * qt - 1:2 * qt + 2], in0=imp[:, 2 * qt - 1:2 * qt + 2],
                                                        in1=BON[:, :], op=ALU.add), [imp, BON], [imp])
                    S.op(DVE, lambda e: e.max(out=m8a[:, :], in_=imp[:, 0:W]), [imp], [m8a])
                    S.op(DVE, lambda e: e.match_replace(out=imp2[:, 0:W], in_to_replace=m8a[:, :], in_values=imp[:, 0:W],
                                                        imm_value=-3e9), [imp, m8a], [imp2])
                    S.op(DVE, lambda e: e.max(out=m8b[:, :], in_=imp2[:, 0:W]), [imp2], [m8b])
                    S.op(DVE, lambda e: e.tensor_scalar(out=negm[:, 0:W], in0=imp[:, 0:W], scalar1=m8b[:, 7:8], scalar2=NEG,
                                                        op0=ALU.is_lt, op1=ALU.mult), [imp, m8b], [negm])
                    S.op(PE, lambda e: e.transpose(out=psT[0:64, 0:128], in_=negm[:, :], identity=ident_f[:, :]),
                         [negm, ident_f], [psT])
                    nT = negT4[(qt * 2 + g) % 2]
                    copy(DVE, nT, nT[:, :, :], psT, psT[0:64, 0:128].unsqueeze(1).to_broadcast([64, 4, 128]))
                k0 = max(0, qt - 4)
                for kt in range(k0, qt + 1):
                    masks = []
                    if kt == qt:
                        masks.append((ident_b[:, :], ident_b, CM4[:, :, :], CM4, None))
                    if kt == qt - 4:
                        masks.append((ident_b[:, :], ident_b, WM4[:, :, :], WM4, None))
                    p = score_tile(KTs[:, 2 + g, kt * 128:(kt + 1) * 128], KTs, rhsQ, qc, masks)
                    defer(lambda p=p, kt=kt: pv(p, po[2], Vaug[:, kt, (2 + g) * 65:(3 + g) * 65], Vaug, kt == k0))
                for kt in range(qt + 1):
                    masks = []
                    if use_topk:
                        masks.append((Ebig[:, kt * 128:(kt + 1) * 128], Ebig, nT[:, :, :], nT, None))
                    if kt == qt:
                        masks.append((ident_b[:, :], ident_b, CM4[:, :, :], CM4, None))
                    p = score_tile(KTs[:, g, kt * 128:(kt + 1) * 128], KTs, rhsQ, qc, masks)
                    defer(lambda p=p, kt=kt: pv(p, po[1], Vaug[:, kt, g * 65:(g + 1) * 65], Vaug, kt == 0))
                flush()
                gts = Gall[:, 0:24].rearrange("p (h b) -> p h b", b=3)
                for b in range(3):
                    if b > 0:
                        S.op(DVE, lambda e, b=b: e.tensor_scalar(out=rd[b][:, :], in0=po3[b][:, :, 64], scalar1=1e-30,
                                                                 scalar2=None, op0=ALU.add), [po[b]], [rd[b]])
                        S.op(DVE, lambda e, b=b: e.reciprocal(out=rd[b][:, :], in_=rd[b][:, :]), [rd[b]], [rd[b]])
                    S.op(DVE, lambda e, b=b: e.tensor_tensor(out=rdg[b][:, :], in0=rd[b][:, :], in1=gts[:, 4 * g:4 * g + 4, b],
                                                             op=ALU.mult), [rd[b], Gall], [rdg[b]])
                    rb = rdg[b][:, :].unsqueeze(2).to_broadcast([128, 4, 64])
                    if b == 0:
                        S.op(DVE, lambda e, rb=rb: e.tensor_tensor(out=ac[:, 4 * g:4 * g + 4, :], in0=po3[0][:, :, 0:64], in1=rb,
                                                                   op=ALU.mult), [po[0], rdg[0]], [ac])
                    else:
                        tp = tmpo[b % 2]
                        S.op(DVE, lambda e, rb=rb, b=b, tp=tp: e.tensor_tensor(out=tp[:, :, :], in0=po3[b][:, :, 0:64], in1=rb,
                                                                               op=ALU.mult), [po[b], rdg[b]], [tp])
                        S.op(POOL, lambda e, tp=tp: e.tensor_tensor(out=ac[:, 4 * g:4 * g + 4, :], in0=ac[:, 4 * g:4 * g + 4, :],
                                                                     in1=tp[:, :, :], op=ALU.add), [ac, tp], [ac])
            yy = ybf[qt % 2]
            copy(ACT, yy, yy[:, :], ac, ac[:, :, :].rearrange("p h d -> p (h d)"))
            S.dma(DT("mixB", qt), scr["mix_d"][qt * 128:(qt + 1) * 128, 256:768], yy, yy[:, :], queue=POOL)


def phase_hgrn(C, L):
    nc, S, sb, PS, copy, DT = C.nc, C.S, C.sb, C.PS, C.copy, C.DT
    scr = C.scr
    ident_f, ones_f, zeros_f = C.ident_f, C.ones_f, C.zeros_f
    with ExitStack() as ph:
        ob = sb(ph, "h_ob", [128, 512], BF16)
        S.op(POOL, lambda e: e.memset(ob[:, :], 1.0), [], [ob])
        HM4 = sb(ph, "HM4", [128, 4, 128], BF16)
        S.op(POOL, lambda e: e.affine_select(out=HM4[:, :, :], in_=ob[:, :].rearrange("p (h q) -> p h q", h=4),
                                             pattern=[[0, 4], [1, 128]], compare_op=ALU.is_ge, fill=0.0, base=0,
                                             channel_multiplier=-1), [ob], [HM4])
        S.op(POOL, lambda e: e.memset(HM4[0:64, :, 64:128], 0.0), [], [HM4])
        Mrev = sb(ph, "Mrev", [128, 128], F32)
        S.op(POOL, lambda e: e.affine_select(out=Mrev[:, :], in_=ones_f[:, 0:128], pattern=[[-1, 128]],
                                             compare_op=ALU.is_gt, fill=0.0, base=0, channel_multiplier=1), [ones_f], [Mrev])
        S.op(POOL, lambda e: e.memset(Mrev[64:128, 0:64], 0.0), [], [Mrev])
        Mmid = sb(ph, "Mmid", [128, 128], F32)
        Bm = sb(ph, "Bm", [128, 128], F32)
        S.op(POOL, lambda e: e.affine_select(out=Mmid[:, :], in_=ones_f[:, 0:128], pattern=[[1, 128]],
                                             compare_op=ALU.is_ge, fill=0.0, base=0, channel_multiplier=-1), [ones_f], [Mmid])
        S.op(POOL, lambda e: e.memset(Mmid[0:64, 64:128], 0.0), [], [Mmid])
        S.op(POOL, lambda e: e.memset(Bm[:, :], 0.0), [], [Bm])
        S.op(POOL, lambda e: e.memset(Bm[0:32, 0:64], 1.0), [], [Bm])
        S.op(POOL, lambda e: e.memset(Bm[64:96, 64:128], 1.0), [], [Bm])
        S.op(POOL, lambda e: e.tensor_tensor(out=Mmid[:, :], in0=Mmid[:, :], in1=Bm[:, :], op=ALU.subtract), [Mmid, Bm], [Mmid])
        Ecol = sb(ph, "Ecol", [128, 4], F32)
        S.op(POOL, lambda e: e.memset(Ecol[:, :], 0.0), [], [Ecol])
        S.op(POOL, lambda e: e.memset(Ecol[0:32, 0:1], 1.0), [], [Ecol])
        S.op(POOL, lambda e: e.memset(Ecol[0:64, 1:2], 1.0), [], [Ecol])
        S.op(POOL, lambda e: e.memset(Ecol[64:96, 2:3], 1.0), [], [Ecol])
        S.op(POOL, lambda e: e.memset(Ecol[64:128, 3:4], 1.0), [], [Ecol])

        def lbcalc(src, shape3, name):
            P_, X = shape3[0], shape3[2]
            ex = sb(ph, name + "_ex", shape3, F32)
            S.op(ACT, lambda e: e.activation(out=ex[:, :, :], in_=src[:, :, :], func=AF.Exp), [src], [ex])
            ssum = sb(ph, name + "_ss", [P_, X], F32)
            S.op(DVE, lambda e: e.tensor_tensor(out=ssum[:, :], in0=ex[:, 0, :], in1=ex[:, 1, :], op=ALU.add), [ex], [ssum])
            S.op(DVE, lambda e: e.tensor_tensor(out=ssum[:, :], in0=ssum[:, :], in1=ex[:, 2, :], op=ALU.add), [ex, ssum], [ssum])
            S.op(DVE, lambda e: e.tensor_tensor(out=ssum[:, :], in0=ssum[:, :], in1=ex[:, 3, :], op=ALU.add), [ex, ssum], [ssum])
            S.op(DVE, lambda e: e.reciprocal(out=ssum[:, :], in_=ssum[:, :]), [ssum], [ssum])
            lb = sb(ph, name + "_lb", [P_, X], F32)
            S.op(DVE, lambda e: e.memset(lb[:, :], 0.0), [], [lb])
            for d in range(1, L + 1):
                S.op(DVE, lambda e, d=d: e.tensor_tensor(out=lb[:, :], in0=lb[:, :], in1=ex[:, d, :], op=ALU.add), [lb, ex], [lb])
            S.op(DVE, lambda e: e.tensor_tensor(out=lb[:, :], in0=lb[:, :], in1=ssum[:, :], op=ALU.mult), [lb, ssum], [lb])
            oml = sb(ph, name + "_oml", [P_, X], F32)
            S.op(DVE, lambda e: e.tensor_scalar(out=oml[:, :], in0=lb[:, :], scalar1=-1.0, scalar2=1.0, op0=ALU.mult,
                                                op1=ALU.add), [lb], [oml])
            return lb, oml
        lsrc_bc = sb(ph, "lsrc_bc", [128, 4, 256], F32)
        S.dma(lsrc_bc, lsrc_bc[:, :, :], DT("lbl"), C.Wd["lb_logits"].partition_broadcast(128))
        lb_bc, oml_bc = lbcalc(lsrc_bc, [128, 4, 256], "lbb")
        lsrc_fm = sb(ph, "lsrc_fm", [64, 4, 4], F32)
        lraw = sb(ph, "lraw", [4, 256], F32)
        S.dma(lraw, lraw[:, :], DT("lbl"), C.Wd["lb_logits"])
        psl = PS()
        for h in range(4):
            S.op(PE, lambda e, h=h: e.transpose(out=psl[0:64, h * 4:(h + 1) * 4], in_=lraw[:, h * 64:(h + 1) * 64],
                                                identity=ident_f[0:4, 0:4]), [lraw, ident_f], [psl])
        copy(DVE, lsrc_fm, lsrc_fm[:, :, :].rearrange("p d h -> p h d"), psl, psl[0:64, 0:16].rearrange("p (h d) -> p h d", h=4))
        lb_fm, oml_fm = lbcalc(lsrc_fm, [64, 4, 4], "lbf")
        ng_bc = bc_load(C, ph, "ng_bc", C.Wd["hgrn_norm_g"][L], 64)

        htm = [sb(ph, "htm%d" % i, [128, 768], F32) for i in range(2)]
        hq = [sb(ph, "hq%d" % i, [64, 4, 128], F32) for i in range(2)]
        hf = [sb(ph, "hf%d" % i, [64, 4, 128], F32) for i in range(2)]
        sig = sb(ph, "hsig", [128, 256], F32)
        ff = sb(ph, "hff", [128, 256], F32)
        logf = [sb(ph, "hlogf%d" % i, [128, 256], F32) for i in range(2)]
        kk = sb(ph, "hkk", [128, 256], F32)
        erev = sb(ph, "herev", [128, 256], F32)
        khat = [sb(ph, "hkhat%d" % i, [128, 256], BF16) for i in range(2)]
        vb = [sb(ph, "hvb%d" % i, [128, 256], BF16) for i in range(2)]
        sigT = sb(ph, "hsigT", [64, 4, 128], F32)
        kkT = sb(ph, "hkkT", [64, 4, 128], F32)
        eq = sb(ph, "heq", [64, 4, 128], F32)
        ek = sb(ph, "hek", [64, 4, 128], F32)
        qz0 = [sb(ph, "hqz0_%d" % i, [64, 4, 128], BF16) for i in range(2)]
        qz1 = [sb(ph, "hqz1_%d" % i, [64, 4, 128], BF16) for i in range(2)]
        for i in range(2):
            S.op(POOL, lambda e, i=i: e.memset(qz0[i][:, :, :], 0.0), [], [qz0[i]])
            S.op(POOL, lambda e, i=i: e.memset(qz1[i][:, :, :], 0.0), [], [qz1[i]])
        ktl = [sb(ph, "hktl%d" % i, [64, 4, 128], BF16) for i in range(2)]
        em = [sb(ph, "hem%d" % i, [64, 4, 4], F32) for i in range(2)]
        sT = [sb(ph, "hsT%d" % i, [128, 4, 128], BF16) for i in range(2)]
        state = [sb(ph, "hstate%d" % i, [64, 4, 64], F32) for i in range(2)]
        S.op(DVE, lambda e: e.memset(state[0][:, :, :], 0.0), [], [state[0]])
        stmp = sb(ph, "hstmp", [64, 4, 64], F32)
        ss0 = [sb(ph, "hss0_%d" % i, [64, 4, 64], BF16) for i in range(2)]
        ss1 = [sb(ph, "hss1_%d" % i, [64, 4, 64], BF16) for i in range(2)]
        o = sb(ph, "ho", [128, 4, 64], F32)
        osq = sb(ph, "hosq", [128, 4, 64], F32)
        rs = sb(ph, "hrs", [128, 4], F32)
        sgl = sb(ph, "hsgl", [128, 256], F32)
        yb = [sb(ph, "hy%d" % i, [128, 256], BF16) for i in range(2)]

        yield
        for tt in range(NT):
            b = tt % 2
            rows = slice(tt * 128, (tt + 1) * 128)
            cols = slice(tt * 128, (tt + 1) * 128)
            S.dma(htm[b], htm[b][:, :], DT("Htm", tt), scr["Htm_d"][rows, 0:768])
            S.dma(hq[b], hq[b][:, :, :], DT("HQ", tt // 4), scr["HQ_d"][:, :, cols].rearrange("j p t -> p j t"))
            S.dma(hf[b], hf[b][:, :, :], DT("HF", tt // 4), scr["HF_d"][:, :, cols].rearrange("j p t -> p j t"))
            H = htm[b]
            if DBGD < 2:
                yield
                continue
            S.op(ACT, lambda e: e.activation(out=sig[:, :], in_=H[:, 0:256], func=AF.Sigmoid), [H], [sig])
            S.op(DVE, lambda e: e.tensor_tensor(out=ff[:, :], in0=sig[:, :], in1=oml_bc[:, :], op=ALU.mult), [sig, oml_bc], [ff])
            S.op(POOL, lambda e: e.tensor_tensor(out=ff[:, :], in0=ff[:, :], in1=lb_bc[:, :], op=ALU.add), [ff, lb_bc], [ff])
            lf = logf[b]
            S.op(ACT, lambda e: e.activation(out=lf[:, :], in_=ff[:, :], func=AF.Ln), [ff], [lf])
            S.op(DVE, lambda e: e.tensor_scalar(out=kk[:, :], in0=ff[:, :], scalar1=-1.0, scalar2=1.0, op0=ALU.mult,
                                                op1=ALU.add), [ff], [kk])
            psA = PS()
            S.op(PE, lambda e: e.matmul(psA[:, 0:256], lhsT=Mrev[:, :], rhs=lf[:, :], start=True, stop=True), [Mrev, lf], [psA])
            S.op(ACT, lambda e: e.activation(out=erev[:, :], in_=psA[:, 0:256], func=AF.Exp), [psA], [erev])
            S.op(DVE, lambda e: e.tensor_tensor(out=khat[b][:, :], in0=kk[:, :], in1=erev[:, :], op=ALU.mult), [kk, erev], [khat[b]])
            copy(POOL, vb[b], vb[b][:, :], H, H[:, 256:512])
            if DBGD < 3:
                yield
                continue
            S.op(ACT, lambda e: e.activation(out=sigT[:, :, :], in_=hf[b][:, :, :], func=AF.Sigmoid), [hf[b]], [sigT])
            S.op(DVE, lambda e: e.tensor_tensor(out=sigT[:, :, :], in0=sigT[:, :, :],
                                                in1=oml_fm[:, :].unsqueeze(2).to_broadcast([64, 4, 128]), op=ALU.mult),
                 [sigT, oml_fm], [sigT])
            S.op(DVE, lambda e: e.tensor_tensor(out=sigT[:, :, :], in0=sigT[:, :, :],
                                                 in1=lb_fm[:, :].unsqueeze(2).to_broadcast([64, 4, 128]), op=ALU.add),
                 [sigT, lb_fm], [sigT])
            S.op(DVE, lambda e: e.tensor_scalar(out=kkT[:, :, :], in0=sigT[:, :, :], scalar1=-1.0, scalar2=1.0, op0=ALU.mult,
                                                op1=ALU.add), [sigT], [kkT])
            psB = PS()
            for h in range(4):
                S.op(PE, lambda e, h=h: e.matmul(psB[0:64, h * 128:(h + 1) * 128], lhsT=lf[:, h * 64:(h + 1) * 64], rhs=Mmid[:, :],
                                                 start=True, stop=True), [lf, Mmid], [psB])
            psB3 = psB[0:64, :].rearrange("p (h t) -> p h t", h=4)
            S.op(ACT, lambda e: e.activation(out=eq[:, :, :], in_=psB3, func=AF.Exp), [psB], [eq])
            S.op(ACT, lambda e: e.activation(out=ek[:, :, :], in_=psB3, func=AF.Exp, scale=-1.0), [psB], [ek])
            S.op(DVE, lambda e: e.tensor_tensor(out=qz0[b][:, :, 0:64], in0=hq[b][:, :, 0:64], in1=eq[:, :, 0:64], op=ALU.mult),
                 [hq[b], eq], [qz0[b]])
            S.op(POOL, lambda e: e.tensor_tensor(out=qz1[b][:, :, 64:128], in0=hq[b][:, :, 64:128], in1=eq[:, :, 64:128],
                                                 op=ALU.mult), [hq[b], eq], [qz1[b]])
            S.op(DVE, lambda e: e.tensor_tensor(out=ktl[b][:, :, :], in0=kkT[:, :, :], in1=ek[:, :, :], op=ALU.mult),
                 [kkT, ek], [ktl[b]])
            psC = PS()
            for h in range(4):
                S.op(PE, lambda e, h=h: e.matmul(psC[0:64, h * 4:(h + 1) * 4], lhsT=lf[:, h * 64:(h + 1) * 64], rhs=Ecol[:, :],
                                                 start=True, stop=True), [lf, Ecol], [psC])
            S.op(ACT, lambda e: e.activation(out=em[b][:, :, :], in_=psC[0:64, 0:16].rearrange("p (h c) -> p h c", h=4),
                                             func=AF.Exp), [psC], [em[b]])
            if DBGD < 4:
                yield
                continue
            psS = PS()
            for h in range(4):
                S.op(PE, lambda e, h=h: e.matmul(psS[:, h * 128:(h + 1) * 128], lhsT=ktl[b][:, h, :], rhs=qz0[b][:, h, :],
                                                 start=True, stop=False), [ktl[b], qz0[b]], [psS])
                S.op(PE, lambda e, h=h: e.matmul(psS[:, h * 128:(h + 1) * 128], lhsT=ktl[b][:, h, :], rhs=qz1[b][:, h, :],
                                                 start=False, stop=True), [ktl[b], qz1[b]], [psS])
            S.op(DVE, lambda e: e.tensor_tensor(out=sT[b][:, :, :], in0=psS[:, :].rearrange("p (h t) -> p h t", h=4),
                                                in1=HM4[:, :, :], op=ALU.mult), [psS, HM4], [sT[b]])
            if DBGD < 5:
                yield
                continue
            psKc = [PS(), PS()]
            for c in range(2):
                for h in range(4):
                    S.op(PE, lambda e, c=c, h=h: e.matmul(psKc[c][0:64, h * 64:(h + 1) * 64],
                                                          lhsT=khat[b][c * 64:(c + 1) * 64, h * 64:(h + 1) * 64],
                                                          rhs=vb[b][c * 64:(c + 1) * 64, h * 64:(h + 1) * 64],
                                                          start=True, stop=True), [khat[b], vb[b]], [psKc[c]])
            psK4 = [psKc[c][0:64, 0:256].rearrange("p (h v) -> p h v", h=4) for c in range(2)]
            st_in = state[tt % 2]
            st_out = state[(tt + 1) % 2]
            E = em[b]

            def ebc(i):
                return E[:, :, i:i + 1].to_broadcast([64, 4, 64])
            S.op(DVE, lambda e: e.tensor_tensor(out=ss0[b][:, :, :], in0=st_in[:, :, :], in1=ebc(0), op=ALU.mult), [st_in, E], [ss0[b]])
            S.op(DVE, lambda e: e.tensor_tensor(out=stmp[:, :, :], in0=st_in[:, :, :], in1=ebc(1), op=ALU.mult), [st_in, E], [stmp])
            S.op(DVE, lambda e: e.tensor_tensor(out=stmp[:, :, :], in0=stmp[:, :, :], in1=psK4[0], op=ALU.add),
                 [stmp, psKc[0]], [stmp])
            S.op(DVE, lambda e: e.tensor_tensor(out=ss1[b][:, :, :], in0=stmp[:, :, :], in1=ebc(2), op=ALU.mult), [stmp, E], [ss1[b]])
            S.op(DVE, lambda e: e.tensor_tensor(out=st_out[:, :, :], in0=stmp[:, :, :], in1=ebc(3), op=ALU.mult), [stmp, E], [st_out])
            S.op(DVE, lambda e: e.tensor_tensor(out=st_out[:, :, :], in0=st_out[:, :, :], in1=psK4[1], op=ALU.add),
                 [st_out, psKc[1]], [st_out])
            if DBGD < 6:
                yield
                continue
            psO = PS()
            for h in range(4):
                osl = psO[:, h * 64:(h + 1) * 64]
                S.op(PE, lambda e, h=h, osl=osl: e.matmul(osl, lhsT=sT[b][:, h, :], rhs=vb[b][:, h * 64:(h + 1) * 64],
                                                          start=True, stop=False), [sT[b], vb[b]], [psO])
                S.op(PE, lambda e, h=h, osl=osl: e.matmul(osl, lhsT=qz0[b][:, h, :], rhs=ss0[b][:, h, :],
                                                          start=False, stop=False), [qz0[b], ss0[b]], [psO])
                S.op(PE, lambda e, h=h, osl=osl: e.matmul(osl, lhsT=qz1[b][:, h, :], rhs=ss1[b][:, h, :],
                                                          start=False, stop=True), [qz1[b], ss1[b]], [psO])
            copy(ACT, o, o[:, :, :], psO, psO[:, 0:256].rearrange("p (h v) -> p h v", h=4))
            S.op(POOL, lambda e: e.tensor_tensor(out=osq[:, :, :], in0=o[:, :, :], in1=o[:, :, :], op=ALU.mult), [o], [osq])
            S.op(DVE, lambda e: e.tensor_reduce(out=rs[:, :], in_=osq[:, :, :], axis=mybir.AxisListType.X, op=ALU.add), [osq], [rs])
            S.op(DVE, lambda e: e.tensor_scalar(out=rs[:, :], in0=rs[:, :], scalar1=1.0 / 64.0, scalar2=LN_EPS, op0=ALU.mult,
                                                op1=ALU.add), [rs], [rs])
            S.op(ACT, lambda e: e.activation(out=rs[:, :], in_=rs[:, :], func=AF.Sqrt), [rs], [rs])
            S.op(DVE, lambda e: e.reciprocal(out=rs[:, :], in_=rs[:, :]), [rs], [rs])
            S.op(DVE, lambda e: e.tensor_tensor(out=o[:, :, :], in0=o[:, :, :], in1=rs[:, :].unsqueeze(2).to_broadcast([128, 4, 64]),
                                                op=ALU.mult), [o, rs], [o])
            S.op(DVE, lambda e: e.tensor_tensor(out=o[:, :, :], in0=o[:, :, :],
                                                 in1=ng_bc[:, :].unsqueeze(1).to_broadcast([128, 4, 64]), op=ALU.mult), [o, ng_bc], [o])
            S.op(ACT, lambda e: e.activation(out=sgl[:, :], in_=H[:, 512:768], func=AF.Silu), [H], [sgl])
            S.op(DVE, lambda e: e.tensor_tensor(out=yb[b][:, :], in0=o[:, :, :].rearrange("p h v -> p (h v)"), in1=sgl[:, :],
                                                op=ALU.mult), [o, sgl], [yb[b]])
            S.dma(DT("mixC", tt), scr["mix_d"][rows, 768:1024], yb[b], yb[b][:, :], queue=POOL)
            yield


def phase_ffn1(C, L, h_src):
    nc, S, sb, PS, copy, DT = C.nc, C.S, C.sb, C.PS, C.copy, C.DT
    scr = C.scr
    with ExitStack() as ph:
        wout = C.load_bf16(ph, "wout", C.Wd["w_out"][L], 8, D, stage_cols=512)
        wup = C.load_bf16(ph, "wup", C.Wd["w_up"][L], 8, 2 * D_FF, stage_cols=512)
        fw = small_T(C, ph, "fw", C.Wd["ffnp"][L], 4, 2 * D_FF)
        g_bc = bc_load(C, ph, "ln1g", C.Wd["ln1_g"][L], D)
        b_bc = bc_load(C, ph, "ln1b", C.Wd["ln1_b"][L], D)
        mixt = [sb(ph, "mixt%d" % i, [128, D], BF16) for i in range(2)]
        mixT = [sb(ph, "mixT%d" % i, [128, 8, 128], BF16) for i in range(2)]
        hin = [sb(ph, "hin%d" % i, [128, D], F32) for i in range(2)]
        z = [sb(ph, "z1_%d" % i, [128, D], F32) for i in range(2)]
        h1 = [sb(ph, "h1_%d" % i, [128, D], F32) for i in range(2)]
        h1b = [sb(ph, "h1b_%d" % i, [128, D], BF16) for i in range(2)]
        hT1 = [sb(ph, "hT1_%d" % i, [128, 8, 512], BF16) for i in range(2)]
        st6 = sb(ph, "st6", [128, 24], F32)
        mv = sb(ph, "mv", [128, 2], F32)
        rstd = sb(ph, "rstd", [128, 1], F32)
        usb = [sb(ph, "usb%d" % i, [128, 514], F32) for i in range(4)]
        yv = [sb(ph, "yv%d" % i, [128, 512], F32) for i in range(3)]
        yg = [sb(ph, "yg%d" % i, [128, 512], F32) for i in range(3)]
        gj = [sb(ph, "gj%d" % i, [128, 512], BF16) for i in range(3)]
        carry = [sb(ph, "carry%d" % i, [128, 44, 2], F32) for i in range(2)]
        S.op(POOL, lambda e: e.memset(carry[0][:, :, :], 0.0), [], [carry[0]])
        def ln_tile(c, t):
            cb = c % 2
            tt = c * 4 + t
            b = tt % 2
            rows = slice(tt * 128, (tt + 1) * 128)
            S.dma(mixt[b], mixt[b][:, :], DT("mix_all"), scr["mix_d"][rows, :])
            S.dma(hin[b], hin[b][:, :], DT("h", tt), h_src[rows, :])
            C.transposes_bf(mixT[b], mixT[b][:, :, :], mixt[b], lambda i, b=b: mixt[b][:, i * 128:(i + 1) * 128], 8)
            pss = []
            for half in range(2):
                ps = PS()
                pss.append(ps)
                for kt in range(8):
                    S.op(PE, lambda e, kt=kt, ps=ps, half=half, b=b: e.matmul(
                        ps[:, :], lhsT=mixT[b][:, kt, :], rhs=wout[:, kt, half * 512:(half + 1) * 512],
                        start=(kt == 0), stop=(kt == 7)), [mixT[b], wout], [ps])
            for half in range(2):
                S.op(DVE, lambda e, half=half, b=b: e.scalar_tensor_tensor(
                    out=z[b][:, half * 512:(half + 1) * 512], in0=hin[b][:, half * 512:(half + 1) * 512], scalar=ALPHA,
                    in1=pss[half][:, :], op0=ALU.mult, op1=ALU.add), [hin[b], pss[half]], [z[b]])
            C.layernorm((st6, mv, rstd), z[b], D, g_bc, b_bc, h1[b], h1[b][:, :])
            S.dma(DT("h1", tt), scr["h1_d"][rows, :], h1[b], h1[b][:, :], queue=POOL)
            copy(ACT, h1b[b], h1b[b][:, :], h1[b], h1[b][:, :])
            C.transposes_bf(hT1[cb], hT1[cb][:, :, t * 128:(t + 1) * 128], h1b[b],
                            lambda i, b=b: h1b[b][:, i * 128:(i + 1) * 128], 8)

        def store_hT1(c):
            cb = c % 2
            sl = slice(c * 512, (c + 1) * 512)
            S.dma(DT("hT1", c), scr["hT1_d"][:, :, sl].rearrange("j p t -> p j t"), hT1[cb], hT1[cb][:, :, :], queue=POOL)

        tails = []

        def up_j(c, j):
            cb = c % 2
            sl = slice(c * 512, (c + 1) * 512)
            cin, cout = carry[c % 2], carry[(c + 1) % 2]
            ys = []
            for vg in range(2):
                col = vg * D_FF + j * 128
                jj = vg * NFF + j
                ps = PS()
                for kt in range(8):
                    S.op(PE, lambda e, kt=kt, ps=ps, col=col: e.matmul(ps[:, :], lhsT=wup[:, kt, col:col + 128],
                                                                        rhs=hT1[cb][:, kt, :], start=(kt == 0), stop=(kt == 7)),
                         [wup, hT1[cb]], [ps])
                u = usb[(j * 2 + vg) % 4]
                copy(ACT, u, u[:, 2:514], ps, ps[:, :])
                copy(ACT, u, u[:, 0:2], cin, cin[:, jj, :])
                copy(ACT, cout, cout[:, jj, :], u, u[:, 512:514])
                y = (yv if vg == 0 else yg)[j % 3]
                S.op(POOL, lambda e, u=u, y=y, jj=jj: e.tensor_scalar(out=y[:, :], in0=u[:, 2:514], scalar1=fw[:, jj, 2:3],
                                                                     scalar2=fw[:, jj, 3:4], op0=ALU.mult, op1=ALU.add),
                     [u, fw], [y])
                S.op(DVE, lambda e, u=u, y=y, jj=jj: e.scalar_tensor_tensor(out=y[:, :], in0=u[:, 1:513], scalar=fw[:, jj, 1:2],
                                                                             in1=y[:, :], op0=ALU.mult, op1=ALU.add),
                     [u, fw, y], [y])
                S.op(DVE, lambda e, u=u, y=y, jj=jj: e.scalar_tensor_tensor(out=y[:, :], in0=u[:, 0:512], scalar=fw[:, jj, 0:1],
                                                                            in1=y[:, :], op0=ALU.mult, op1=ALU.add),
                     [u, fw, y], [y])
                ys.append(y)
            def tail(ys=ys, j=j, c=c, sl=sl):
                S.op(ACT, lambda e, y=ys[1]: e.activation(out=y[:, :], in_=y[:, :], func=AF.Silu), [ys[1]], [ys[1]])
                gjt = gj[j % 3]
                S.op(POOL, lambda e, gjt=gjt, ys=ys: e.tensor_tensor(out=gjt[:, :], in0=ys[0][:, :], in1=ys[1][:, :], op=ALU.mult),
                     [ys[0], ys[1]], [gjt])
                S.dma(DT("g", c), scr["g_d"][j, :, sl], gjt, gjt[:, :], queue=POOL)
            if tails:
                tails.pop(0)()
            tails.append(tail)

        for t in range(4):
            ln_tile(0, t)
        store_hT1(0)
        for c in range(NCH):
            for j in range(NFF if DBGE >= 2 else 0):
                up_j(c, j)
                if c + 1 < NCH and j in (2, 7, 12, 17):
                    ln_tile(c + 1, (j - 2) // 5)
            while tails:
                tails.pop(0)()
            if c + 1 < NCH:
                if DBGE < 2:
                    for t in range(4):
                        ln_tile(c + 1, t)
                store_hT1(c + 1)


def phase_ffn2(C, L, h_dst):
    nc, S, sb, PS, copy, DT = C.nc, C.S, C.sb, C.PS, C.copy, C.DT
    scr = C.scr
    with ExitStack() as ph:
        wdn = C.load_bf16(ph, "wdn", C.Wd["w_down"][L], NFF, D, stage_cols=256)
        wg = C.load_bf16(ph, "wg", C.Wd["w_ple_gate"][L], 8, D, stage_cols=512)
        wp = C.load_bf16(ph, "wp", C.Wd["w_ple_proj"][L], 2, D, stage_cols=512)
        g_bc = bc_load(C, ph, "ln2g", C.Wd["ln2_g"][L], D)
        b_bc = bc_load(C, ph, "ln2b", C.Wd["ln2_b"][L], D)
        gT = [sb(ph, "gTl%d" % i, [128, NFF, 512], BF16) for i in range(2)]
        hT1 = [sb(ph, "hT1l%d" % i, [128, 8, 512], BF16) for i in range(2)]
        h1 = [sb(ph, "h1l%d" % i, [128, D], F32) for i in range(2)]
        pt = [sb(ph, "pt%d" % i, [128, 256], F32) for i in range(2)]
        ptb = [sb(ph, "ptb%d" % i, [128, 256], BF16) for i in range(2)]
        pT = [sb(ph, "ppT%d" % i, [128, 2, 128], BF16) for i in range(2)]
        sgt = [sb(ph, "sgt%d" % i, [128, D], F32) for i in range(2)]
        z = [sb(ph, "z2_%d" % i, [128, D], F32) for i in range(2)]
        h2 = [sb(ph, "h2_%d" % i, [128, D], F32) for i in range(2)]
        st6 = sb(ph, "st6b", [128, 24], F32)
        mv = sb(ph, "mvb", [128, 2], F32)
        rstd = sb(ph, "rstdb", [128, 1], F32)
        def prep(tt):
            b = tt % 2
            rows = slice(tt * 128, (tt + 1) * 128)
            S.dma(h1[b], h1[b][:, :], DT("h1", tt), scr["h1_d"][rows, :])
            S.dma(pt[b], pt[b][:, :], DT("p", tt), C.p_in[L, rows, :])
            copy(DVE, ptb[b], ptb[b][:, :], pt[b], pt[b][:, :])
            C.transposes_bf(pT[b], pT[b][:, :, :], ptb[b], lambda i, b=b: ptb[b][:, i * 128:(i + 1) * 128], 2)

        for c in range(NCH):
            cb = c % 2
            sl = slice(c * 512, (c + 1) * 512)
            S.dma(gT[cb], gT[cb][:, :, :], DT("g", c), scr["g_d"][:, :, sl].rearrange("j p t -> p j t"))
            S.dma(hT1[cb], hT1[cb][:, :, :], DT("hT1", c), scr["hT1_d"][:, :, sl].rearrange("j p t -> p j t"))
            for t in range(4):
                tt = c * 4 + t
                b = tt % 2
                rows = slice(tt * 128, (tt + 1) * 128)
                tok = slice(t * 128, (t + 1) * 128)
                if tt == 0:
                    prep(0)
                if tt + 1 < NT:
                    prep(tt + 1)
                psf, psg, psp = [], [], []
                for half in range(2):
                    hs = slice(half * 512, (half + 1) * 512)
                    ps = PS()
                    psg.append(ps)
                    for kt in range(8):
                        S.op(PE, lambda e, kt=kt, ps=ps, hs=hs: e.matmul(ps[:, :], lhsT=hT1[cb][:, kt, tok], rhs=wg[:, kt, hs],
                                                                          start=(kt == 0), stop=(kt == 7)), [hT1[cb], wg], [ps])
                    S.op(ACT, lambda e, ps=ps, hs=hs, b=b: e.activation(out=sgt[b][:, hs], in_=ps[:, :], func=AF.Sigmoid),
                         [ps], [sgt[b]])
                    ps = PS()
                    psp.append(ps)
                    for kt in range(2):
                        S.op(PE, lambda e, kt=kt, ps=ps, hs=hs, b=b: e.matmul(ps[:, :], lhsT=pT[b][:, kt, :], rhs=wp[:, kt, hs],
                                                                               start=(kt == 0), stop=(kt == 1)), [pT[b], wp], [ps])
                    S.op(DVE, lambda e, ps=ps, hs=hs, b=b: e.tensor_tensor(out=sgt[b][:, hs], in0=sgt[b][:, hs], in1=ps[:, :],
                                                                           op=ALU.mult), [sgt[b], ps], [sgt[b]])
                    ps = PS()
                    psf.append(ps)
                    for j in range(NFF):
                        S.op(PE, lambda e, j=j, ps=ps, hs=hs: e.matmul(ps[:, :], lhsT=gT[cb][:, j, tok], rhs=wdn[:, j, hs],
                                                                        start=(j == 0), stop=(j == NFF - 1)), [gT[cb], wdn], [ps])
                    S.op(DVE, lambda e, ps=ps, hs=hs, b=b: e.scalar_tensor_tensor(out=z[b][:, hs], in0=h1[b][:, hs], scalar=ALPHA,
                                                                                  in1=ps[:, :], op0=ALU.mult, op1=ALU.add),
                         [h1[b], ps], [z[b]])
                    S.op(POOL, lambda e, hs=hs, b=b: e.tensor_tensor(out=z[b][:, hs], in0=z[b][:, hs], in1=sgt[b][:, hs], op=ALU.add),
                         [z[b], sgt[b]], [z[b]])
                C.layernorm((st6, mv, rstd), z[b], D, g_bc, b_bc, h2[b], h2[b][:, :])
                S.dma(DT("h", tt), h_dst[rows, :], h2[b], h2[b][:, :], queue=POOL)


_NC_CACHE = {}


def _prep_inputs(inp, depth=DEPTH):
    f = lambda a: np.ascontiguousarray(np.asarray(a, dtype=np.float32))
    shared = {}
    for k in ("w_in", "conv_ln_g", "conv_ln_b", "cmp_pe_k", "cmp_pe_v", "cmp_w1_k", "cmp_w2_k", "cmp_w1_v", "cmp_w2_v",
              "lb_logits", "hgrn_norm_g", "w_out", "ln1_g", "ln1_b", "w_up", "w_down", "w_ple_gate", "w_ple_proj",
              "ln2_g", "ln2_b"):
        shared[k] = f(inp[k])
    shared["convp"] = f(np.concatenate([np.asarray(inp["conv_w"]), np.asarray(inp["conv_b"])[:, None, :]], axis=1))
    shared["ffnp"] = f(np.concatenate([np.asarray(inp["ffn_conv_w"]), np.asarray(inp["ffn_conv_b"])[:, None, :]], axis=1))
    x = np.asarray(inp["x"], dtype=np.float32)
    p = np.asarray(inp["p"], dtype=np.float32)
    maps = []
    for b in range(8):
        m = dict(shared)
        m["x"] = np.ascontiguousarray(x[b])
        m["p"] = np.ascontiguousarray(p[:, b])
        maps.append(m)
    return maps


def kernel(**inputs):
    if "nc" not in _NC_CACHE:
        _NC_CACHE["nc"] = build()
    nc = _NC_CACHE["nc"]
    maps = _prep_inputs(inputs)
    res = run_bass_kernel_spmd(nc, maps, core_ids=list(range(8)))
    out = np.stack([np.asarray(r["y"], dtype=np.float32) for r in res.results], axis=0)
    return out
```
